# Optimizing a Trainium2 kernel written in Bass

```python
import math
import jax, jax.numpy as jnp
from jax import lax
import numpy as np

D_MODEL = 4096
BATCH = 4
SEQ = 4096
DEPTH = 2

CHUNK = 64
Q_BLOCK = 128
PLE_DIM = 256

SSM_WIDTH = D_MODEL // 2
SSM_HEAD_DIM = 64
SSM_HEADS = SSM_WIDTH // SSM_HEAD_DIM
SSM_GROUPS = 8
SSM_STATE = 128
SSM_CONV = 4
SSM_XBC = SSM_WIDTH + 2 * SSM_GROUPS * SSM_STATE

DIFF_WIDTH = D_MODEL // 4
DIFF_HEAD_DIM = 64
DIFF_HEADS = DIFF_WIDTH // (2 * DIFF_HEAD_DIM)

MLSTM_WIDTH = D_MODEL // 4
MLSTM_HEAD_DIM = 128
MLSTM_HEADS = MLSTM_WIDTH // MLSTM_HEAD_DIM

MIX_WIDTH = SSM_WIDTH + DIFF_WIDTH + MLSTM_WIDTH
N_BRANCH = 3

SPLIT_SIZES = (
    SSM_XBC, SSM_WIDTH, SSM_HEADS,
    DIFF_WIDTH, DIFF_WIDTH, DIFF_WIDTH, DIFF_WIDTH,
    MLSTM_WIDTH, MLSTM_WIDTH, MLSTM_WIDTH, MLSTM_WIDTH, MLSTM_WIDTH,
    MLSTM_HEADS, MLSTM_HEADS,
    N_BRANCH * D_MODEL,
)
IN_WIDTH = sum(SPLIT_SIZES)

DEEPNORM_ALPHA = (2.0 * DEPTH) ** 0.25
DEEPNORM_BETA = (8.0 * DEPTH) ** -0.25

kernel_name = "hybrid_ssd_diffattn_mlstm_deepnorm"


def _split_cols(t, sizes):
    idx, acc = [], 0
    for s in sizes[:-1]:
        acc += s
        idx.append(acc)
    return jnp.split(t, idx, axis=-1)


def _rmsnorm(x, g, eps=1e-5):
    xf = x.astype(jnp.float32)
    y = xf * lax.rsqrt(jnp.mean(xf * xf, axis=-1, keepdims=True) + eps)
    return y * g.astype(jnp.float32)


def _layernorm(x, g, b, eps=1e-5):
    xf = x.astype(jnp.float32)
    mu = jnp.mean(xf, axis=-1, keepdims=True)
    var = jnp.mean(jnp.square(xf - mu), axis=-1, keepdims=True)
    return (xf - mu) * lax.rsqrt(var + eps) * g.astype(jnp.float32) + b.astype(jnp.float32)


def _causal_dwconv(u, w, b):
    k = w.shape[0]
    out = lax.conv_general_dilated(u, w[:, None, :], window_strides=(1,), padding=[(k - 1, 0)],
                                   dimension_numbers=('NWC', 'WIO', 'NWC'),
                                   feature_group_count=u.shape[-1])
    return out + b


def _ssd(xh, dt, a, bmat, cmat):
    bsz, s, h, pdim = xh.shape
    g, n = bmat.shape[2], bmat.shape[3]
    r = h // g
    nc = s // CHUNK
    f32 = jnp.float32
    xdt = (xh.astype(f32) * dt[..., None]).reshape(bsz, nc, CHUNK, g, r, pdim)
    la = (dt * a).reshape(bsz, nc, CHUNK, g, r)
    bc = bmat.astype(f32).reshape(bsz, nc, CHUNK, g, n)
    cc = cmat.astype(f32).reshape(bsz, nc, CHUNK, g, n)
    la_cum = jnp.cumsum(la, axis=2)
    causal = jnp.tril(jnp.ones((CHUNK, CHUNK), bool))[:, :, None, None]
    seg = la_cum[:, :, :, None] - la_cum[:, :, None, :]
    decay = jnp.where(causal, jnp.exp(jnp.where(causal, seg, 0.0)), 0.0)
    cb = jnp.einsum('bclgn,bcsgn->bclsg', cc, bc)
    y_diag = jnp.einsum('bclsgr,bcsgrp->bclgrp', cb[..., None] * decay, xdt)
    decay_to_end = jnp.exp(la_cum[:, :, -1:] - la_cum)
    states = jnp.einsum('bclgn,bclgrp->bcgrpn', bc, xdt * decay_to_end[..., None])
    chunk_decay = jnp.exp(la_cum[:, :, -1])

    def step(carry, inp):
        st, dec = inp
        return carry * dec[..., None, None] + st, carry

    init = jnp.zeros((bsz, g, r, pdim, n), f32)
    _, prev = lax.scan(step, init, (jnp.moveaxis(states, 1, 0), jnp.moveaxis(chunk_decay, 1, 0)))
    prev = jnp.moveaxis(prev, 0, 1)
    y_off = jnp.einsum('bclgn,bcgrpn->bclgrp', cc, prev) * jnp.exp(la_cum)[..., None]
    return (y_diag + y_off).reshape(bsz, s, h, pdim)


def _diff_attention(q, k, v, lam, slopes):
    bsz, s, h = q.shape[:3]
    nb = s // Q_BLOCK
    scale = DIFF_HEAD_DIM ** -0.5
    kpos = jnp.arange(s)
    qb = jnp.moveaxis(q.reshape(bsz, nb, Q_BLOCK, h, 2, DIFF_HEAD_DIM), 1, 0)

    def block(args):
        qblk, bi = args
        qpos = bi * Q_BLOCK + jnp.arange(Q_BLOCK)
        visible = (kpos[None, :] // CHUNK) <= (qpos[:, None] // CHUNK)
        dist = jnp.abs(qpos[:, None] - kpos[None, :]).astype(jnp.float32)
        bias = -slopes[:, None, None] * dist
        sc = jnp.einsum('bqhcd,bkhcd->bhcqk', qblk, k,
                        preferred_element_type=jnp.float32) * scale + bias[None, :, None]
        sc = jnp.where(visible, sc, -jnp.inf)
        pr = jax.nn.softmax(sc, axis=-1)
        attn = pr[:, :, 0] - lam * pr[:, :, 1]
        return jnp.einsum('bhqk,bkhe->bqhe', attn.astype(v.dtype), v)

    out = lax.map(block, (qb, jnp.arange(nb)))
    return jnp.moveaxis(out, 0, 1).reshape(bsz, s, h, 2 * DIFF_HEAD_DIM)


def _mlstm(q, k, v, i_raw, f_raw):
    f32 = jnp.float32
    bsz, s, h, d = q.shape
    nc = s // CHUNK
    qc = q.astype(f32).reshape(bsz, nc, CHUNK, h, d)
    kc = (k.astype(f32) * d ** -0.5).reshape(bsz, nc, CHUNK, h, d)
    vc = v.astype(f32).reshape(bsz, nc, CHUNK, h, d)
    logf = jax.nn.log_sigmoid(f_raw.astype(f32)).reshape(bsz, nc, CHUNK, h)
    ig = i_raw.astype(f32).reshape(bsz, nc, CHUNK, h)
    bcum = jnp.cumsum(logf, axis=2)
    btot = bcum[:, :, -1]
    w_end = btot[:, :, None] - bcum + ig
    m_loc = jnp.max(w_end, axis=2)
    e_end = jnp.exp(w_end - m_loc[:, :, None])
    c_loc = jnp.einsum('bcjhv,bcjhk->bchvk', vc * e_end[..., None], kc)
    n_loc = jnp.einsum('bcjh,bcjhk->bchk', e_end, kc)

    def step(carry, inp):
        cm, nm, mm = carry
        cl, nl, ml, bt = inp
        m_new = jnp.maximum(bt + mm, ml)
        a_old = jnp.exp(bt + mm - m_new)
        a_loc = jnp.exp(ml - m_new)
        c_new = a_old[..., None, None] * cm + a_loc[..., None, None] * cl
        n_new = a_old[..., None] * nm + a_loc[..., None] * nl
        return (c_new, n_new, m_new), (cm, nm, mm)

    init = (jnp.zeros((bsz, h, d, d), f32), jnp.zeros((bsz, h, d), f32), jnp.zeros((bsz, h), f32))
    xs = tuple(jnp.moveaxis(t, 1, 0) for t in (c_loc, n_loc, m_loc, btot))
    _, (c_prev, n_prev, m_prev) = lax.scan(step, init, xs)
    c_prev = jnp.moveaxis(c_prev, 0, 1)
    n_prev = jnp.moveaxis(n_prev, 0, 1)
    m_prev = jnp.moveaxis(m_prev, 0, 1)
    causal = jnp.tril(jnp.ones((CHUNK, CHUNK), bool))[:, :, None]
    dmat = bcum[:, :, :, None] - bcum[:, :, None, :] + ig[:, :, None, :]
    dmat = jnp.where(causal, dmat, -jnp.inf)
    m_inter = bcum + m_prev[:, :, None]
    m_row = jnp.maximum(m_inter, jnp.max(dmat, axis=3))
    wts = jnp.exp(dmat - m_row[:, :, :, None])
    sw = wts * jnp.einsum('bcihd,bcjhd->bcijh', qc, kc)
    num = jnp.einsum('bcijh,bcjhd->bcihd', sw, vc)
    den = jnp.sum(sw, axis=3)
    a_inter = jnp.exp(m_inter - m_row)
    num = num + jnp.einsum('bcihk,bchvk->bcihv', qc, c_prev) * a_inter[..., None]
    den = den + jnp.einsum('bcihk,bchk->bcih', qc, n_prev) * a_inter
    hout = num / jnp.maximum(jnp.abs(den), jnp.exp(-m_row))[..., None]
    return hout.reshape(bsz, s, h, d)


def _layer(x, p_i, li, w_in, conv_w, conv_b, dt_bias, a_log, d_skip, ssm_norm_g,
           diff_lambda, diff_norm_g, mlstm_gate_b, w_branch, w_out, ln_g, ln_b,
           w_ple, ple_norm_g, w_ple_gate):
    bsz, s, _ = x.shape
    f32 = jnp.float32
    proj = x @ w_in
    (xbc, z_a, dt_raw, q_b, k_b, v_b, z_b, q_c, k_c, v_c, o_c, z_c, i_c, f_c,
     gates) = _split_cols(proj, SPLIT_SIZES)

    xbc = jax.nn.silu(_causal_dwconv(xbc, conv_w, conv_b))
    x_a, b_a, c_a = _split_cols(xbc, (SSM_WIDTH, SSM_GROUPS * SSM_STATE, SSM_GROUPS * SSM_STATE))
    dt = jax.nn.softplus(dt_raw.astype(f32) + dt_bias.astype(f32))
    a = -jnp.exp(a_log.astype(f32))
    xh = x_a.reshape(bsz, s, SSM_HEADS, SSM_HEAD_DIM)
    y = _ssd(xh, dt, a, b_a.reshape(bsz, s, SSM_GROUPS, SSM_STATE),
             c_a.reshape(bsz, s, SSM_GROUPS, SSM_STATE))
    y = (y + d_skip.astype(f32)[:, None] * xh.astype(f32)).reshape(bsz, s, SSM_WIDTH)
    out_a = _rmsnorm(y * jax.nn.silu(z_a.astype(f32)), ssm_norm_g).astype(x.dtype)

    lam_init = 0.8 - 0.6 * math.exp(-0.3 * li)
    dl = diff_lambda.astype(f32)
    lam = jnp.exp(jnp.sum(dl[0] * dl[1])) - jnp.exp(jnp.sum(dl[2] * dl[3])) + lam_init
    slopes = 2.0 ** (-8.0 * jnp.arange(1, DIFF_HEADS + 1, dtype=f32) / DIFF_HEADS)
    qd = q_b.reshape(bsz, s, DIFF_HEADS, 2, DIFF_HEAD_DIM)
    kd = k_b.reshape(bsz, s, DIFF_HEADS, 2, DIFF_HEAD_DIM)
    vd = v_b.reshape(bsz, s, DIFF_HEADS, 2 * DIFF_HEAD_DIM)
    od = _rmsnorm(_diff_attention(qd, kd, vd, lam, slopes), diff_norm_g) * (1.0 - lam_init)
    out_b = (od.reshape(bsz, s, DIFF_WIDTH) * jax.nn.silu(z_b.astype(f32))).astype(x.dtype)

    i_raw = i_c.astype(f32) + mlstm_gate_b[0].astype(f32)
    f_raw = f_c.astype(f32) + mlstm_gate_b[1].astype(f32)
    hm = _mlstm(q_c.reshape(bsz, s, MLSTM_HEADS, MLSTM_HEAD_DIM),
                k_c.reshape(bsz, s, MLSTM_HEADS, MLSTM_HEAD_DIM),
                v_c.reshape(bsz, s, MLSTM_HEADS, MLSTM_HEAD_DIM), i_raw, f_raw)
    out_c = (jax.nn.sigmoid(o_c.astype(f32)) * hm.reshape(bsz, s, MLSTM_WIDTH)
             * jax.nn.silu(z_c.astype(f32))).astype(x.dtype)

    y_a = out_a @ w_branch[:SSM_WIDTH]
    y_b = out_b @ w_branch[SSM_WIDTH:SSM_WIDTH + DIFF_WIDTH]
    y_c = out_c @ w_branch[SSM_WIDTH + DIFF_WIDTH:]
    g = jax.nn.sigmoid(gates.astype(f32)).reshape(bsz, s, N_BRANCH, D_MODEL)
    merged = (g[:, :, 0] * y_a + g[:, :, 1] * y_b + g[:, :, 2] * y_c).astype(x.dtype)
    sub = merged @ w_out

    h = _layernorm(DEEPNORM_ALPHA * x + sub, ln_g, ln_b).astype(x.dtype)

    e = _rmsnorm(p_i @ w_ple, ple_norm_g)
    gate = jax.nn.sigmoid((h @ w_ple_gate).astype(f32))
    return (h.astype(f32) + gate * e).astype(x.dtype)


def setup_inputs(seed: int = 0) -> dict:
    key = jax.random.key(seed)
    ks = jax.random.split(key, 24)
    f32 = jnp.float32
    nrm = lambda k, shape: jax.random.normal(k, shape, f32)
    x = nrm(ks[0], (BATCH, SEQ, D_MODEL))
    p = nrm(ks[1], (DEPTH, BATCH, SEQ, PLE_DIM))
    w_in = nrm(ks[2], (DEPTH, D_MODEL, IN_WIDTH)) * D_MODEL ** -0.5
    conv_w = nrm(ks[3], (DEPTH, SSM_CONV, SSM_XBC)) * SSM_CONV ** -0.5
    conv_b = 0.01 * nrm(ks[4], (DEPTH, SSM_XBC))
    log_dt = jax.random.uniform(ks[5], (DEPTH, SSM_HEADS), f32, math.log(1e-3), math.log(1e-1))
    dt0 = jnp.exp(log_dt)
    dt_bias = dt0 + jnp.log(-jnp.expm1(-dt0))
    a_log = jnp.log(jax.random.uniform(ks[6], (DEPTH, SSM_HEADS), f32, 1.0, 16.0))
    d_skip = 1.0 + 0.1 * nrm(ks[7], (DEPTH, SSM_HEADS))
    ssm_norm_g = 1.0 + 0.02 * nrm(ks[8], (DEPTH, SSM_WIDTH))
    diff_lambda = 0.1 * nrm(ks[9], (DEPTH, 4, DIFF_HEAD_DIM))
    diff_norm_g = 1.0 + 0.02 * nrm(ks[10], (DEPTH, 2 * DIFF_HEAD_DIM))
    i_bias = -3.0 + 0.1 * nrm(ks[11], (DEPTH, MLSTM_HEADS))
    f_bias = jnp.linspace(3.0, 6.0, MLSTM_HEADS, dtype=f32)[None] + 0.1 * nrm(ks[12], (DEPTH, MLSTM_HEADS))
    mlstm_gate_b = jnp.stack([i_bias, f_bias], axis=1)
    row_scale = jnp.concatenate([jnp.full((SSM_WIDTH,), SSM_WIDTH ** -0.5, f32),
                                 jnp.full((DIFF_WIDTH,), DIFF_WIDTH ** -0.5, f32),
                                 jnp.full((MLSTM_WIDTH,), MLSTM_WIDTH ** -0.5, f32)])
    w_branch = nrm(ks[13], (DEPTH, MIX_WIDTH, D_MODEL)) * row_scale[None, :, None] * DEEPNORM_BETA
    w_out = nrm(ks[14], (DEPTH, D_MODEL, D_MODEL)) * (D_MODEL ** -0.5 * DEEPNORM_BETA)
    ln_g = 1.0 + 0.02 * nrm(ks[15], (DEPTH, D_MODEL))
    ln_b = 0.02 * nrm(ks[16], (DEPTH, D_MODEL))
    w_ple = nrm(ks[17], (DEPTH, PLE_DIM, D_MODEL)) * PLE_DIM ** -0.5
    ple_norm_g = 1.0 + 0.02 * nrm(ks[18], (DEPTH, D_MODEL))
    w_ple_gate = nrm(ks[19], (DEPTH, D_MODEL, D_MODEL)) * D_MODEL ** -0.5
    return {"x": x, "p": p, "w_in": w_in, "conv_w": conv_w, "conv_b": conv_b,
            "dt_bias": dt_bias, "a_log": a_log, "d_skip": d_skip, "ssm_norm_g": ssm_norm_g,
            "diff_lambda": diff_lambda, "diff_norm_g": diff_norm_g, "mlstm_gate_b": mlstm_gate_b,
            "w_branch": w_branch, "w_out": w_out, "ln_g": ln_g, "ln_b": ln_b,
            "w_ple": w_ple, "ple_norm_g": ple_norm_g, "w_ple_gate": w_ple_gate}


def reference(x, p, w_in, conv_w, conv_b, dt_bias, a_log, d_skip, ssm_norm_g,
              diff_lambda, diff_norm_g, mlstm_gate_b, w_branch, w_out, ln_g, ln_b,
              w_ple, ple_norm_g, w_ple_gate):
    h = x
    for li in range(DEPTH):
        h = _layer(h, p[li], li, w_in[li], conv_w[li], conv_b[li], dt_bias[li], a_log[li],
                   d_skip[li], ssm_norm_g[li], diff_lambda[li], diff_norm_g[li],
                   mlstm_gate_b[li], w_branch[li], w_out[li], ln_g[li], ln_b[li],
                   w_ple[li], ple_norm_g[li], w_ple_gate[li])
    return h
```

```python
import math
import numpy as np
import concourse.bass as bass
import concourse.mybir as mybir
from concourse.bass_utils import run_bass_kernel_spmd

F32 = mybir.dt.float32
BF16 = mybir.dt.bfloat16
AF = mybir.ActivationFunctionType
ALU = mybir.AluOpType

D = 4096
SEQ = 4096
T = 256
NSL = T // 128
KC = 32
WIN = 27696
C_XBC, C_ZA, C_DT, C_QB, C_KB, C_VB, C_ZB = 0, 4096, 6144, 6176, 7200, 8224, 9248
C_QC, C_KCC, C_VC, C_OC, C_ZC, C_I, C_F, C_G = 10272, 11296, 12320, 13344, 14368, 15392, 15400, 15408
ALPHA = (2.0 * 2) ** 0.25
EPS = 1e-5


class Res:
    __slots__ = ("w", "r")

    def __init__(self):
        self.w = None
        self.r = {}


class Sched:
    def __init__(self, nc, ndma=20):
        self.nc = nc
        self.eng = dict(pe=nc.tensor, act=nc.scalar, dve=nc.vector, pool=nc.gpsimd, sp=nc.sync)
        self.sems = {}
        self.cnt = {}
        for k in ("pe", "act", "dve", "pool"):
            self.sems[k] = nc.semaphore("s_" + k).__enter__()
            self.cnt[k] = 0
        self.dsem = [nc.semaphore("d%d" % i).__enter__() for i in range(ndma)]
        self.dcnt = [0] * ndma
        self.dnext = 0
        self.seen = {e: {} for e in self.eng}
        self.nops = 0

    def _sem(self, k):
        return self.sems[k] if isinstance(k, str) else self.dsem[k]

    def _wait(self, e, need):
        sn = self.seen[e]
        for k, v in need.items():
            if sn.get(k, 0) >= v:
                continue
            self.eng[e].wait_ge(self._sem(k), v)
            sn[k] = v

    def _collect(self, reads, writes):
        need = {}

        def add(t):
            if t is None:
                return
            k, v = t
            if need.get(k, 0) < v:
                need[k] = v

        for r in reads:
            add(r.w)
        for w in writes:
            add(w.w)
            for k, v in w.r.items():
                add((k, v))
        return need

    def _commit(self, tok, reads, writes):
        k, v = tok
        for r in reads:
            if r.r.get(k, 0) < v:
                r.r[k] = v
        for w in writes:
            w.w = tok
            w.r = {}

    def op(self, e, fn, reads=(), writes=()):
        need = self._collect(reads, writes)
        if e == "pe":
            need.pop("pe", None)
        self._wait(e, need)
        ins = fn(self.eng[e])
        self.cnt[e] += 1
        ins.then_inc(self.sems[e], 1)
        tok = (e, self.cnt[e])
        self._commit(tok, reads, writes)
        self.nops += 1
        return tok

    def dma(self, e, out, in_, reads=(), writes=(), **kw):
        i = self.dnext
        self.dnext = (self.dnext + 1) % len(self.dsem)
        need = self._collect(reads, writes)
        if self.dcnt[i]:
            if need.get(i, 0) < self.dcnt[i]:
                need[i] = self.dcnt[i]
        self._wait(e, need)
        self.eng[e].dma_start(out=out, in_=in_, **kw).then_inc(self.dsem[i], 16)
        self.dcnt[i] += 16
        tok = (i, self.dcnt[i])
        self._commit(tok, reads, writes)
        return tok

    def barrier(self, engines=("pe", "act", "dve", "pool", "sp")):
        need = {k: v for k, v in self.cnt.items() if v}
        for i, v in enumerate(self.dcnt):
            if v:
                need[i] = v
        for e in engines:
            self._wait(e, dict(need))


def _wblocks():
    b = []
    for i in range(16):
        b.append(("w_in", C_XBC + 256 * i, 256, 32))
    b.append(("w_in", C_DT, 32, 32))
    for i in range(8):
        b.append(("w_in", C_ZA + 256 * i, 256, 32))
    for c0 in (C_QB, C_KB, C_VB, C_ZB):
        for i in range(4):
            b.append(("w_in", c0 + 256 * i, 256, 32))
    for c0 in (C_QC, C_KCC, C_VC, C_OC, C_ZC):
        for i in range(4):
            b.append(("w_in", c0 + 256 * i, 256, 32))
    b.append(("w_in", C_I, 16, 32))
    for j in range(16):
        b.append(("w_branch", 256 * j, 256, 32))
        for br in range(3):
            b.append(("w_in", C_G + br * 4096 + 256 * j, 256, 32))
    for j in range(16):
        b.append(("w_out", 256 * j, 256, 32))
    b.append(("w_ple", 0, 4096, 2))
    for j in range(16):
        b.append(("w_ple_gate", 256 * j, 256, 32))
    return b


WBLOCKS = _wblocks()
WB_ELEMS = [128 * kc * n for (_, _, n, kc) in WBLOCKS]
WB_OFF = np.concatenate([[0], np.cumsum(WB_ELEMS)]).astype(np.int64)
WL_ELEMS = int(WB_OFF[-1])
PAGE_ELEMS = 48 * 1024 * 1024
WB_PAGE = []
_pg, _po = 0, 0
for _n in WB_ELEMS:
    if _po + _n > PAGE_ELEMS:
        _pg += 1
        _po = 0
    WB_PAGE.append((_pg, _po))
    _po += _n
NPAGES = _pg + 1


def build(NL=2, NTOK=SEQ, dbg=None, stop=None):
    NT = NTOK // T
    nc = bass.Bass("TRN2", target_bir_lowering=False)
    sc = Sched(nc)
    E = sc.eng
    dbg_out = {}

    def din(name, shape, dt=F32):
        return nc.dram_tensor(name, list(shape), dt, kind="ExternalInput").ap()

    x_d = din("x", [NTOK, D])
    p_d = din("p", [NL, NTOK, 256])
    wmat = dict(w_in=din("w_in", [NL, D, WIN]), w_branch=din("w_branch", [NL, D, D]), w_out=din("w_out", [NL, D, D]),
                w_ple=din("w_ple", [NL, 256, D]), w_ple_gate=din("w_ple_gate", [NL, D, D]))
    convw_d = din("convw", [NL, 128, 32, 4])
    convb_d = din("convb", [NL, 128, 32])
    hp_d = din("hp", [NL, 32, 4])
    dskip_d = din("dskip", [NL, 1, 32])
    ssmg_d = din("ssmg", [NL, 1, 2048])
    dlam_d = din("dlam", [NL, 1, 256])
    dng_d = din("dng", [NL, 1, 128])
    mgb_d = din("mgb", [NL, 8, 2])
    lng_d = din("lng", [NL, 1, D])
    lnb_d = din("lnb", [NL, 1, D])
    pleg_d = din("pleg", [NL, 1, D])
    cst_d = din("cst", [128, 128 * 3 + 8 * 32 + 8 * 128])
    out_d = nc.dram_tensor("out", [NTOK, D], F32, kind="ExternalOutput").ap()
    wsc_pages = [[nc.dram_tensor("wsc%d_%d" % (l, g), [PAGE_ELEMS], BF16, kind="Internal").ap() for g in range(NPAGES)] for l in range(NL)]

    def wsc_view(l, b, ne):
        pg, po = WB_PAGE[b]
        return wsc_pages[l][pg][po: po + 128 * ne].rearrange("(p e) -> p e", p=128)
    kc_d = nc.dram_tensor("kcache", [NL, 8, 128, NTOK], BF16, kind="Internal").ap()
    vc_d = nc.dram_tensor("vcache", [NL, 8, NTOK, 129], BF16, kind="Internal").ap()
    hb_d = nc.dram_tensor("hbuf", [NTOK, D], F32, kind="Internal").ap()
    r_wsc, r_kc, r_vc, r_hb = Res(), Res(), Res(), Res()

    def dbg_dump(name, ap, shape, dt=F32, reads=()):
        if dbg is None or name not in dbg:
            return
        o = nc.dram_tensor("dbg_" + name, list(shape), dt, kind="ExternalOutput").ap()
        dbg_out[name] = sc.dma("pool", o, ap, reads=reads)

    def sb(name, shape, dt=F32):
        return nc.sbuf_tensor(name, list(shape), dt).__enter__()

    st_cm = [nc.sbuf_tensor("pst%d" % i, [128, 8192], F32) for i in range(2)]
    sb_cm = [nc.sbuf_tensor("psb%d" % i, [128, 8192], BF16) for i in range(2)]
    stg = [c.__enter__() for c in st_cm]
    sbg = [c.__enter__() for c in sb_cm]
    r_stg = [Res(), Res()]
    r_sbg = [Res(), Res()]
    bi = 0
    for l in range(NL):
        for bidx, (mat, c0, n, kc) in enumerate(WBLOCKS):
            s = bi % 2
            W = wmat[mat][l]
            src = W[:, c0:c0 + n].rearrange("(k p) c -> p k c", p=128)
            stv = stg[s][:, 0:kc * n].rearrange("p (k c) -> p k c", k=kc)
            step = 8 if kc >= 8 else kc
            if n > 1024:
                step = 1
            for k0 in range(0, kc, step):
                if n > 1024:
                    for c1 in range(0, n, 1024):
                        sc.dma("sp", stv[:, k0:k0 + step, c1:c1 + 1024], src[:, k0:k0 + step, c1:c1 + 1024], writes=[r_stg[s]])
                else:
                    sc.dma("sp", stv[:, k0:k0 + step, :], src[:, k0:k0 + step, :], writes=[r_stg[s]])
            ne = kc * n
            if bi % 2 == 0:
                sc.op("dve", lambda e: e.tensor_copy(out=sbg[s][:, 0:ne], in_=stg[s][:, 0:ne]), reads=[r_stg[s]], writes=[r_sbg[s]])
            else:
                sc.op("act", lambda e: e.copy(out=sbg[s][:, 0:ne], in_=stg[s][:, 0:ne]), reads=[r_stg[s]], writes=[r_sbg[s]])
            dst = wsc_view(l, bidx, ne)
            sc.dma("pool", dst, sbg[s][:, 0:ne], reads=[r_sbg[s]], writes=[r_wsc])
            bi += 1
    sc.barrier()
    for c in reversed(sb_cm):
        c.__exit__(None, None, None)
    for c in reversed(st_cm):
        c.__exit__(None, None, None)

    cst = sb("cst_sb", [128, 128 * 3 + 8 * 32 + 8 * 128])
    ident = cst[:, 0:128]
    trimask = cst[:, 128:256]
    ones = cst[:, 256:384]
    tab = cst[:, 384:384 + 256].rearrange("p (h r) -> p h r", h=8)
    Dh = cst[:, 640:640 + 1024].rearrange("p (h q) -> p h q", h=8)
    identb = sb("identb", [128, 128], BF16)
    epsc = sb("epsc", [128, 1])
    zero32 = sb("zero32", [32, 1])
    xT = sb("xT", [128, KC, T], BF16)
    wbuf = [sb("wbuf%d" % i, [128, 8192], BF16) for i in range(2)]
    r_wbuf = [Res(), Res()]
    oaT = sb("oaT", [128, 16, T], BF16)
    obT = sb("obT", [128, 8, T], BF16)
    ocT = sb("ocT", [128, 8, T], BF16)
    mT = sb("mT", [128, KC, T], BF16)
    r_xT, r_oaT, r_obT, r_ocT, r_mT = Res(), Res(), Res(), Res(), Res()
    Sst = [sb("Sst%d" % l, [128, 2048]) for l in range(NL)]
    Sbf = [sb("Sbf%d" % l, [128, 2048], BF16) for l in range(NL)]
    tails = [sb("tails%d" % l, [128, 32, 3]) for l in range(NL)]
    Cst = [sb("Cst%d" % l, [128, 8, 129]) for l in range(NL)]
    Cbf = [sb("Cbf%d" % l, [128, 8, 129], BF16) for l in range(NL)]
    FM8 = [sb("fm8_%d" % l, [8, 2]) for l in range(NL)]
    r_S = [Res() for _ in range(NL)]
    r_Sbf = [Res() for _ in range(NL)]
    r_tails = [Res() for _ in range(NL)]
    r_C = [Res() for _ in range(NL)]
    r_Cbf = [Res() for _ in range(NL)]
    r_FM8 = [Res() for _ in range(NL)]
    convw = [sb("convw%d" % l, [128, 32, 4]) for l in range(NL)]
    convb = [sb("convb%d" % l, [128, 32]) for l in range(NL)]
    hp = [sb("hp%d" % l, [32, 4]) for l in range(NL)]
    nega = [sb("nega%d" % l, [32, 1]) for l in range(NL)]
    dskB = [sb("dskB%d" % l, [128, 32]) for l in range(NL)]
    dngB = [sb("dngB%d" % l, [128, 128]) for l in range(NL)]
    lamt = [sb("lam%d" % l, [128, 4]) for l in range(NL)]
    dlt = sb("dlt", [128, 256])
    mgb = [sb("mgb%d" % l, [8, 2]) for l in range(NL)]
    nfb = [sb("nfb%d" % l, [8, 1]) for l in range(NL)]
    ARENA = 72 * 1024
    arena = sb("arena", [128, ARENA // 2], BF16)
    psums = [nc.psum_tensor("ps%d" % i, [128, 512], F32).__enter__() for i in range(8)]
    r_ps = [Res() for _ in range(8)]
    psn = [0]

    psrot = [8]

    def psum():
        i = psn[0] % psrot[0]
        psn[0] += 1
        return psums[i], r_ps[i]

    class Carve:
        def __init__(self):
            self.off = 0

        def get(self, shape, dt=F32):
            n = int(np.prod(shape[1:]))
            nb = n * (4 if dt == F32 else 2)
            nb_al = (nb + 63) // 64 * 64
            assert self.off + nb_al <= ARENA, ("arena overflow", self.off, nb_al)
            v = arena[0:shape[0], self.off // 2: self.off // 2 + nb // 2]
            self.off += nb_al
            if dt == F32:
                v = v.bitcast(F32)
            if len(shape) == 3:
                v = v.rearrange("p (a b) -> p a b", a=shape[1])
            return v

    r_c = Res()
    sc.dma("pool", cst[:], cst_d, writes=[r_c])
    sc.op("dve", lambda e: e.tensor_copy(out=identb[:], in_=ident), reads=[r_c], writes=[r_c])
    sc.op("dve", lambda e: e.memset(epsc[:], EPS), writes=[r_c])
    sc.op("dve", lambda e: e.memset(zero32[:], 0.0), writes=[r_c])
    for l in range(NL):
        sc.dma("pool", convw[l][:], convw_d[l], writes=[r_c])
        sc.dma("pool", convb[l][:], convb_d[l], writes=[r_c])
        sc.dma("pool", hp[l][:], hp_d[l], writes=[r_c])
        sc.dma("pool", dskB[l][:], dskip_d[l].partition_broadcast(128), writes=[r_c])
        sc.dma("pool", dngB[l][:], dng_d[l].partition_broadcast(128), writes=[r_c])
        sc.dma("pool", mgb[l][:], mgb_d[l], writes=[r_c])
        sc.dma("pool", dlt[:], dlam_d[l].partition_broadcast(128), writes=[r_c])
        sc.op("act", lambda e: e.activation(out=nega[l][:], in_=hp[l][:, 1:2], func=AF.Exp), reads=[r_c], writes=[r_c])
        sc.op("dve", lambda e: e.tensor_scalar(out=nega[l][:], in0=nega[l][:], scalar1=-1.0, scalar2=None, op0=ALU.mult), reads=[r_c], writes=[r_c])
        sc.op("dve", lambda e: e.tensor_scalar(out=nfb[l][:], in0=mgb[l][:, 1:2], scalar1=-1.0, scalar2=None, op0=ALU.mult), reads=[r_c], writes=[r_c])
        lam_init = 0.8 - 0.6 * math.exp(-0.3 * l)
        sc.op("dve", lambda e: e.tensor_tensor(out=dlt[:, 0:64], in0=dlt[:, 0:64], in1=dlt[:, 64:128], op=ALU.mult), reads=[r_c], writes=[r_c])
        sc.op("dve", lambda e: e.tensor_tensor(out=dlt[:, 128:192], in0=dlt[:, 128:192], in1=dlt[:, 192:256], op=ALU.mult), reads=[r_c], writes=[r_c])
        sc.op("dve", lambda e: e.reduce_sum(out=lamt[l][:, 2:3], in_=dlt[:, 0:64], axis=mybir.AxisListType.X), reads=[r_c], writes=[r_c])
        sc.op("dve", lambda e: e.reduce_sum(out=lamt[l][:, 3:4], in_=dlt[:, 128:192], axis=mybir.AxisListType.X), reads=[r_c], writes=[r_c])
        sc.op("act", lambda e: e.activation(out=lamt[l][:, 2:4], in_=lamt[l][:, 2:4], func=AF.Exp), reads=[r_c], writes=[r_c])
        sc.op("dve", lambda e: e.tensor_tensor(out=lamt[l][:, 0:1], in0=lamt[l][:, 2:3], in1=lamt[l][:, 3:4], op=ALU.subtract), reads=[r_c], writes=[r_c])
        sc.op("dve", lambda e: e.tensor_scalar(out=lamt[l][:, 0:1], in0=lamt[l][:, 0:1], scalar1=lam_init, scalar2=None, op0=ALU.add), reads=[r_c], writes=[r_c])
        sc.op("dve", lambda e: e.tensor_scalar(out=lamt[l][:, 1:2], in0=lamt[l][:, 0:1], scalar1=-1.0, scalar2=None, op0=ALU.mult), reads=[r_c], writes=[r_c])
        sc.op("dve", lambda e: e.memset(Sst[l][:], 0.0), writes=[r_S[l]])
        sc.op("dve", lambda e: e.memset(Sbf[l][:], 0.0), writes=[r_Sbf[l]])
        sc.op("dve", lambda e: e.memset(tails[l][:], 0.0), writes=[r_tails[l]])
        sc.op("dve", lambda e: e.memset(Cst[l][:], 0.0), writes=[r_C[l]])
        sc.op("dve", lambda e: e.memset(Cbf[l][:], 0.0), writes=[r_Cbf[l]])
        sc.op("dve", lambda e: e.memset(FM8[l][:], 0.0), writes=[r_FM8[l]])
    sc.barrier()

    WPLE_B = WBLOCKS.index(("w_ple", 0, 4096, 2))
    wplan = [(l, b) for _t in range(NT) for l in range(NL) for b in range(len(WBLOCKS)) if b != WPLE_B]
    wstate = dict(issued=0, used=0)

    def w_issue():
        i = wstate["issued"]
        if i >= len(wplan):
            return
        l, b = wplan[i]
        _, _, n, kc = WBLOCKS[b]
        ne = kc * n
        s = i % 2
        src = wsc_view(l, b, ne)
        sc.dma("sp", wbuf[s][:, 0:ne], src, writes=[r_wbuf[s]])
        wstate["issued"] = i + 1

    def w_next(expect):
        i = wstate["used"]
        l, b = wplan[i]
        mat, c0, n, kc = WBLOCKS[b]
        assert (mat, c0) == expect, (WBLOCKS[b], expect)
        while wstate["issued"] <= min(i + 1, len(wplan) - 1):
            w_issue()
        wstate["used"] = i + 1
        s = i % 2
        return wbuf[s][:, 0:kc * n].rearrange("p (k c) -> p k c", k=kc), r_wbuf[s]

    def mm_group(ps_ap, r_p, pairs, reads):
        n = len(pairs)

        def fn(e):
            ins = None
            for i, (a, b) in enumerate(pairs):
                ins = e.matmul(ps_ap, lhsT=a, rhs=b, start=(i == 0), stop=(i == n - 1))
            return ins
        return sc.op("pe", fn, reads=reads, writes=[r_p])

    def transpose_to(dst_ap, r_dst, src_ap, r_src, kpart, dt, evac="dve"):
        ps, r_p = psum()
        nfree = src_ap.shape[-1]
        if dt == BF16:
            pv = ps[:].bitcast(BF16)[0:nfree, 0:kpart]
            idn = identb[0:kpart, 0:kpart]
        else:
            pv = ps[0:nfree, 0:kpart]
            idn = ident[0:kpart, 0:kpart]
        sc.op("pe", lambda e: e.transpose(out=pv, in_=src_ap, identity=idn), reads=[r_src], writes=[r_p])
        if evac == "dve":
            sc.op("dve", lambda e: e.tensor_copy(out=dst_ap, in_=pv), reads=[r_p], writes=[r_dst])
        else:
            sc.op("act", lambda e: e.copy(out=dst_ap, in_=pv), reads=[r_p], writes=[r_dst])

    def dla_chunk(cv, nh, hpg, vw, bT, cT_, r_bc, b_tm, val_tm, r_val, rFM, cFM, r_fm, scT, r_scT, state, state_bf,
                  r_state, r_state_bf, decB, r_dec, out_cb):
        ng = nh // hpg
        gw = hpg * vw
        GTm = cv.get([128, 128])
        r_GTm = Res()
        xin = cv.get([128, hpg, vw], BF16)
        xcs = cv.get([128, hpg, vw], BF16)
        r_xin, r_xcs = Res(), Res()
        rh = [cv.get([nh, 128]) for _ in range(2)]
        argb = [cv.get([128, 128]) for _ in range(2)]
        PTb = [cv.get([128, 128], BF16) for _ in range(2)]
        r_rh = [Res(), Res()]
        r_arg = [Res(), Res()]
        r_PT = [Res(), Res()]
        t4 = cv.get([128, hpg, vw])
        r_t4 = Res()
        hi = 0
        for g in range(ng):
            hs = slice(g * hpg, (g + 1) * hpg)
            ps1, r_p1 = psum()
            mm_group(ps1[:, 0:128], r_p1, [(bT(g), cT_(g))], [r_bc])
            sc.op("dve", lambda e: e.tensor_tensor(out=GTm, in0=ps1[:, 0:128], in1=trimask, op=ALU.mult), reads=[r_p1], writes=[r_GTm])
            sc.op("dve", lambda e: e.tensor_tensor(out=xin, in0=val_tm[:, hs, :], in1=scT[:, 1, hs].unsqueeze(2).to_broadcast([128, hpg, vw]), op=ALU.mult),
                  reads=[r_val, r_scT], writes=[r_xin])
            sc.op("dve", lambda e: e.tensor_tensor(out=xcs, in0=val_tm[:, hs, :], in1=scT[:, 3, hs].unsqueeze(2).to_broadcast([128, hpg, vw]), op=ALU.mult),
                  reads=[r_val, r_scT], writes=[r_xcs])
            psy, r_py = psum()
            psyv = psy[:, 0:2 * gw]
            for hh in range(hpg):
                h = g * hpg + hh
                k = hi % 2
                hi += 1
                sc.op("dve", lambda e: e.tensor_scalar(out=rh[k], in0=rFM, scalar1=ident[0:nh, h:h + 1], scalar2=None, op0=ALU.mult),
                      reads=[r_fm], writes=[r_rh[k]])
                ps2, r_p2 = psum()
                mm_group(ps2[:, 0:128], r_p2, [(ones[0:nh, :], rh[k])], [r_rh[k]])
                sc.op("dve", lambda e: e.tensor_scalar(out=argb[k], in0=ps2[:, 0:128], scalar1=scT[:, 0, h:h + 1], scalar2=0.0, op0=ALU.subtract, op1=ALU.min),
                      reads=[r_p2, r_scT], writes=[r_arg[k]])
                sc.op("act", lambda e: e.activation(out=argb[k], in_=argb[k], func=AF.Exp), reads=[r_arg[k]], writes=[r_arg[k]])
                sc.op("dve", lambda e: e.tensor_tensor(out=PTb[k], in0=argb[k], in1=GTm, op=ALU.mult), reads=[r_arg[k], r_GTm], writes=[r_PT[k]])
                mm_group(psy[:, hh * vw:(hh + 1) * vw], r_py, [(PTb[k], xin[:, hh, :])], [r_PT[k], r_xin])
            mm_group(psy[:, gw:2 * gw], r_py, [(cT_(g), state_bf[:, hs, :])], [r_bc, r_state_bf])
            out_cb(g, psy, r_py)
            ps3, r_p3 = psum()
            mm_group(ps3[:, 0:gw], r_p3, [(b_tm(g), xcs[:])], [r_bc, r_xcs])
            sc.op("dve", lambda e: e.tensor_tensor(out=t4, in0=state[:, hs, :], in1=decB[:, hs].unsqueeze(2).to_broadcast([128, hpg, vw]), op=ALU.mult),
                  reads=[r_state, r_dec], writes=[r_t4])
            sc.op("dve", lambda e: e.tensor_tensor(out=state[:, hs, :], in0=t4, in1=ps3[:, 0:gw].rearrange("p (a b) -> p a b", a=hpg), op=ALU.add),
                  reads=[r_t4, r_p3], writes=[r_state])
            sc.op("act", lambda e: e.copy(out=state_bf[:, hs, :], in_=state[:, hs, :]), reads=[r_state], writes=[r_state_bf])

    def fm_terms(cv, nh, rFM_t, cFM_t, extra_t, r_fm, r_prev, rprev_ap, nchunks):
        outs = []
        nrp = cv.get([nh, 1])
        rs = cv.get([nh, 128])
        cs = cv.get([nh, 128])
        dec = cv.get([nh, 1])
        dg = cv.get([nh, nh])
        r_t = Res()
        for c in range(nchunks):
            ch = slice(c * 128, (c + 1) * 128)
            prev = rprev_ap if c == 0 else rFM_t[:, c * 128 - 1:c * 128]
            rend = rFM_t[:, (c + 1) * 128 - 1:(c + 1) * 128]
            scT = cv.get([128, 4, nh])
            decB = cv.get([128, nh])
            r_scT, r_dec = Res(), Res()
            sc.op("dve", lambda e: e.tensor_scalar(out=nrp, in0=prev, scalar1=-1.0, scalar2=None, op0=ALU.mult), reads=[r_fm, r_prev], writes=[r_t])
            sc.op("act", lambda e: e.activation(out=rs, in_=rFM_t[:, ch], func=AF.Exp, bias=nrp, scale=1.0), reads=[r_fm, r_t], writes=[r_t])
            sc.op("act", lambda e: e.activation(out=cs, in_=cFM_t[:, ch], func=AF.Exp, bias=rend, scale=-1.0), reads=[r_fm, r_t], writes=[r_t])
            if extra_t is not None:
                sc.op("dve", lambda e: e.tensor_tensor(out=cs, in0=cs, in1=extra_t[:, ch], op=ALU.mult), reads=[r_fm, r_t], writes=[r_t])
            sc.op("act", lambda e: e.activation(out=dec, in_=rend, func=AF.Exp, bias=nrp, scale=1.0), reads=[r_fm, r_t], writes=[r_t])
            sc.op("dve", lambda e: e.tensor_scalar(out=dg, in0=ident[0:nh, 0:nh], scalar1=dec, scalar2=None, op0=ALU.mult), reads=[r_t], writes=[r_t])
            ps, r_p = psum()
            mm_group(ps[:, 0:nh], r_p, [(ones[0:nh, :], dg)], [r_t])
            sc.op("dve", lambda e: e.tensor_copy(out=decB, in_=ps[:, 0:nh]), reads=[r_p], writes=[r_dec])
            ps, r_p = psum()
            srcs = [cFM_t[:, ch], (extra_t[:, ch] if extra_t is not None else None), rs, cs]

            def fn(e):
                ins = None
                for i, s_ in enumerate(srcs):
                    if s_ is None:
                        continue
                    ins = e.transpose(out=ps[:, i * nh:(i + 1) * nh], in_=s_, identity=ident[0:nh, 0:nh])
                return ins
            sc.op("pe", fn, reads=[r_fm, r_t], writes=[r_p])
            if extra_t is None:
                sc.op("dve", lambda e: e.memset(scT[:, 1, :], 1.0), writes=[r_scT])
                sc.op("dve", lambda e: e.tensor_copy(out=scT[:, 0, :], in_=ps[:, 0:nh]), reads=[r_p], writes=[r_scT])
                sc.op("dve", lambda e: e.tensor_copy(out=scT[:, 2:4, :], in_=ps[:, 2 * nh:4 * nh].rearrange("p (a b) -> p a b", a=2)), reads=[r_p], writes=[r_scT])
            else:
                sc.op("dve", lambda e: e.tensor_copy(out=scT, in_=ps[:, 0:4 * nh].rearrange("p (a b) -> p a b", a=4)), reads=[r_p], writes=[r_scT])
            outs.append((scT, r_scT, decB, r_dec))
        return outs

    for t in range(NT):
        t0 = t * T
        for l in range(NL):
            last = (l == NL - 1)
            lam_init = 0.8 - 0.6 * math.exp(-0.3 * l)
            if l == 0:
                cv = Carve()
                xs = [cv.get([128, D]) for _ in range(2)]
                r_xs = [Res(), Res()]
                for s in range(NSL):
                    k = s % 2
                    sc.dma("pool", xs[k], x_d[t0 + s * 128: t0 + (s + 1) * 128, :], writes=[r_xs[k]])
                    for kc in range(KC):
                        transpose_to(xT[:, kc, s * 128:(s + 1) * 128], r_xT, xs[k][:, kc * 128:(kc + 1) * 128], r_xs[k], 128, F32,
                                     evac=("dve" if kc % 2 == 0 else "act"))
                sc.barrier()
            cv = Carve()
            xh = cv.get([128, NSL, 2048], BF16)
            BF = cv.get([128, 8, T], BF16)
            CF = cv.get([128, 8, T], BF16)
            BT = cv.get([128, NSL, 1024], BF16)
            za = cv.get([128, NSL, 2048], BF16)
            ub = [cv.get([128, T + 3]) for _ in range(2)]
            acc = [cv.get([128, T]) for _ in range(2)]
            xc = [cv.get([128, T], BF16) for _ in range(2)]
            r_xh, r_BC, r_za = Res(), Res(), Res()
            r_ub = [Res(), Res()]
            r_acc = [Res(), Res()]
            r_xc = [Res(), Res()]
            blk = 0
            for wi in range(16):
                wv, r_w = w_next(("w_in", C_XBC + 256 * wi))
                for half in range(2):
                    k = blk % 2
                    ps, r_p = psum()
                    mm_group(ps[:, 0:T], r_p, [(wv[:, kc, half * 128:(half + 1) * 128], xT[:, kc, :]) for kc in range(KC)], [r_w, r_xT])
                    sc.op("act", lambda e: e.copy(out=ub[k][:, 3:3 + T], in_=ps[:, 0:T]), reads=[r_p], writes=[r_ub[k]])
                    sc.op("dve", lambda e: e.tensor_copy(out=ub[k][:, 0:3], in_=tails[l][:, blk, :]), reads=[r_tails[l]], writes=[r_ub[k]])
                    sc.op("dve", lambda e: e.tensor_scalar(out=acc[k], in0=ub[k][:, 0:T], scalar1=convw[l][:, blk, 0:1], scalar2=None, op0=ALU.mult),
                          reads=[r_ub[k]], writes=[r_acc[k]])
                    for j in range(1, 4):
                        sc.op("dve", lambda e: e.scalar_tensor_tensor(out=acc[k], in0=ub[k][:, j:j + T], scalar=convw[l][:, blk, j:j + 1], in1=acc[k],
                                                                      op0=ALU.mult, op1=ALU.add), reads=[r_ub[k], r_acc[k]], writes=[r_acc[k]])
                    sc.op("dve", lambda e: e.tensor_copy(out=tails[l][:, blk, :], in_=ub[k][:, T:T + 3]), reads=[r_ub[k]], writes=[r_tails[l]])
                    if blk < 16:
                        dst, r_dst = xc[k], r_xc[k]
                    elif blk < 24:
                        dst, r_dst = BF[:, blk - 16, :], r_BC
                    else:
                        dst, r_dst = CF[:, blk - 24, :], r_BC
                    sc.op("act", lambda e: e.activation(out=dst, in_=acc[k], func=AF.Silu, bias=convb[l][:, blk:blk + 1], scale=1.0),
                          reads=[r_acc[k]], writes=[r_dst])
                    if blk < 16:
                        for s in range(NSL):
                            transpose_to(xh[:, s, blk * 128:(blk + 1) * 128], r_xh, xc[k][:, s * 128:(s + 1) * 128], r_xc[k], 128, BF16,
                                         evac=("dve" if s % 2 == 0 else "act"))
                    elif blk < 24:
                        for s in range(NSL):
                            transpose_to(BT[:, s, (blk - 16) * 128:(blk - 15) * 128], r_BC, BF[:, blk - 16, s * 128:(s + 1) * 128], r_BC, 128, BF16,
                                         evac=("dve" if s % 2 == 0 else "act"))
                    blk += 1
            dtF = cv.get([32, T])
            laF = cv.get([32, T])
            AF_ = cv.get([32, T])
            ones32T = cv.get([32, T])
            r_dtf = Res()
            wv, r_w = w_next(("w_in", C_DT))
            ps, r_p = psum()
            mm_group(ps[0:32, 0:T], r_p, [(wv[:, kc, 0:32], xT[:, kc, :]) for kc in range(KC)], [r_w, r_xT])
            sc.op("act", lambda e: e.activation(out=dtF, in_=ps[0:32, 0:T], func=AF.Exp, bias=hp[l][:, 0:1], scale=1.0), reads=[r_p], writes=[r_dtf])
            sc.op("act", lambda e: e.activation(out=dtF, in_=dtF, func=AF.Ln, bias=ones[0:32, 0:1], scale=1.0), reads=[r_dtf], writes=[r_dtf])
            sc.op("dve", lambda e: e.tensor_scalar(out=laF, in0=dtF, scalar1=nega[l][:, 0:1], scalar2=None, op0=ALU.mult), reads=[r_dtf], writes=[r_dtf])
            sc.op("dve", lambda e: e.memset(ones32T, 1.0), writes=[r_dtf])
            sc.op("dve", lambda e: e.tensor_tensor_scan(out=AF_, data0=ones32T, data1=laF, initial=0.0, op0=ALU.mult, op1=ALU.add), reads=[r_dtf], writes=[r_dtf])
            r_z32 = Res()
            terms = fm_terms(cv, 32, AF_, AF_, dtF, r_dtf, r_z32, zero32[:], NSL)
            for wi in range(8):
                wv, r_w = w_next(("w_in", C_ZA + 256 * wi))
                for s in range(NSL):
                    ps, r_p = psum()
                    mm_group(ps[:, 0:256], r_p, [(xT[:, kc, s * 128:(s + 1) * 128], wv[:, kc, :]) for kc in range(KC)], [r_w, r_xT])
                    sc.op("act", lambda e: e.activation(out=za[:, s, wi * 256:(wi + 1) * 256], in_=ps[:, 0:256], func=AF.Silu), reads=[r_p], writes=[r_za])
            ysb = cv.get([128, 2048])
            r_y = Res()
            t1 = cv.get([128, 4, 64])
            t3 = cv.get([128, 4, 64])
            r_t1, r_t3 = Res(), Res()
            gA = cv.get([128, 2048])
            r_gA = Res()
            oab = cv.get([128, 2048], BF16)
            r_oab = Res()
            st2 = cv.get([128, 4])
            r_st2 = Res()
            sc.dma("pool", gA, ssmg_d[l].partition_broadcast(128), writes=[r_gA])
            cvmark = cv.off
            for c in range(NSL):
                cv.off = cvmark
                ch = slice(c * 128, (c + 1) * 128)
                scT, r_scT, decB, r_dec = terms[c]
                xh_c = xh[:, c, :].rearrange("p (h v) -> p h v", h=32)

                def out_cb(g, psy, r_py, c=c, xh_c=xh_c, scT=scT, r_scT=r_scT):
                    hs = slice(4 * g, 4 * g + 4)
                    sc.op("dve", lambda e: e.tensor_tensor(out=t1, in0=psy[:, 256:512].rearrange("p (a b) -> p a b", a=4),
                                                          in1=scT[:, 2, hs].unsqueeze(2).to_broadcast([128, 4, 64]), op=ALU.mult),
                          reads=[r_py, r_scT], writes=[r_t1])
                    sc.op("dve", lambda e: e.tensor_tensor(out=t1, in0=t1, in1=psy[:, 0:256].rearrange("p (a b) -> p a b", a=4), op=ALU.add),
                          reads=[r_py, r_t1], writes=[r_t1])
                    sc.op("dve", lambda e: e.tensor_tensor(out=t3, in0=xh_c[:, hs, :], in1=dskB[l][:, hs].unsqueeze(2).to_broadcast([128, 4, 64]), op=ALU.mult),
                          reads=[r_xh], writes=[r_t3])
                    sc.op("dve", lambda e: e.tensor_tensor(out=ysb[:, g * 256:(g + 1) * 256].rearrange("p (a b) -> p a b", a=4), in0=t1, in1=t3, op=ALU.add),
                          reads=[r_t1, r_t3], writes=[r_y])

                dla_chunk(cv, 32, 4, 64,
                          lambda g: BF[:, g, ch], lambda g: CF[:, g, ch], r_BC,
                          lambda g: BT[:, c, g * 128:(g + 1) * 128], xh_c, r_xh,
                          AF_[:, ch], AF_[:, ch], r_dtf, scT, r_scT,
                          Sst[l][:].rearrange("p (h v) -> p h v", h=32), Sbf[l][:].rearrange("p (h v) -> p h v", h=32),
                          r_S[l], r_Sbf[l], decB, r_dec, out_cb)
                sc.op("dve", lambda e: e.tensor_tensor(out=ysb, in0=ysb, in1=za[:, c, :], op=ALU.mult), reads=[r_y, r_za], writes=[r_y])
                sc.op("dve", lambda e: e.memset(st2[:], 0.0), writes=[r_st2])
                sc.op("act", lambda e: e.activation(out=oab, in_=ysb, func=AF.Square, accum_out=st2[:, 0:1]), reads=[r_y], writes=[r_oab, r_st2])
                sc.op("act", lambda e: e.activation(out=st2[:, 1:2], in_=st2[:, 0:1], func=AF.Sqrt, bias=epsc[:, 0:1], scale=1.0 / 2048), reads=[r_st2], writes=[r_st2])
                sc.op("dve", lambda e: e.reciprocal(out=st2[:, 2:3], in_=st2[:, 1:2]), reads=[r_st2], writes=[r_st2])
                sc.op("dve", lambda e: e.scalar_tensor_tensor(out=oab, in0=ysb, scalar=st2[:, 2:3], in1=gA, op0=ALU.mult, op1=ALU.mult),
                      reads=[r_y, r_st2, r_gA], writes=[r_oab])
                for kc in range(16):
                    transpose_to(oaT[:, kc, ch], r_oaT, oab[:, kc * 128:(kc + 1) * 128], r_oab, 128, BF16, evac=("dve" if kc % 2 == 0 else "act"))
            dbg_dump("oaT", oaT[:], [128, 16, T], BF16, reads=[r_oaT])
            sc.barrier()
            if stop == "A":
                break

            cv = Carve()
            qT = cv.get([128, 8, T], BF16)
            kTn = cv.get([128, 8, T], BF16)
            vn = cv.get([128, NSL, 8 * 129], BF16)
            zb = cv.get([128, NSL, 1024], BF16)
            obb = cv.get([128, NSL, 1024], BF16)
            r_q, r_kn, r_vn, r_zb, r_obb = Res(), Res(), Res(), Res(), Res()
            sc.op("dve", lambda e: e.memset(vn[:], 1.0), writes=[r_vn])
            for (c0, dstT, r_d) in ((C_QB, qT, r_q), (C_KB, kTn, r_kn)):
                for wi in range(4):
                    wv, r_w = w_next(("w_in", c0 + 256 * wi))
                    for half in range(2):
                        h = 2 * wi + half
                        ps, r_p = psum()
                        mm_group(ps[:, 0:T], r_p, [(wv[:, kc, half * 128:(half + 1) * 128], xT[:, kc, :]) for kc in range(KC)], [r_w, r_xT])
                        if half == 0:
                            sc.op("act", lambda e: e.copy(out=dstT[:, h, :], in_=ps[:, 0:T]), reads=[r_p], writes=[r_d])
                        else:
                            sc.op("dve", lambda e: e.tensor_copy(out=dstT[:, h, :], in_=ps[:, 0:T]), reads=[r_p], writes=[r_d])
            sc.dma("pool", kc_d[l][:, :, t0:t0 + T].rearrange("h p t -> p h t"), kTn[:], reads=[r_kn], writes=[r_kc])
            for wi in range(4):
                wv, r_w = w_next(("w_in", C_VB + 256 * wi))
                for s in range(NSL):
                    ps, r_p = psum()
                    mm_group(ps[:, 0:256], r_p, [(xT[:, kc, s * 128:(s + 1) * 128], wv[:, kc, :]) for kc in range(KC)], [r_w, r_xT])
                    sc.op("dve", lambda e: e.tensor_copy(out=vn[:, s, :].rearrange("p (h e) -> p h e", h=8)[:, 2 * wi:2 * wi + 2, 0:128],
                                                        in_=ps[:, 0:256].rearrange("p (h e) -> p h e", h=2)), reads=[r_p], writes=[r_vn])
            for s in range(NSL):
                sc.dma("pool", vc_d[l][:, t0 + s * 128: t0 + (s + 1) * 128, :].rearrange("h p e -> p h e"),
                       vn[:, s, :].rearrange("p (h e) -> p h e", h=8), reads=[r_vn], writes=[r_vc])
            for wi in range(4):
                wv, r_w = w_next(("w_in", C_ZB + 256 * wi))
                for s in range(NSL):
                    ps, r_p = psum()
                    mm_group(ps[:, 0:256], r_p, [(xT[:, kc, s * 128:(s + 1) * 128], wv[:, kc, :]) for kc in range(KC)], [r_w, r_xT])
                    sc.op("act", lambda e: e.activation(out=zb[:, s, wi * 256:(wi + 1) * 256], in_=ps[:, 0:256], func=AF.Silu), reads=[r_p], writes=[r_zb])
            nkb = (t0 + T) // 128
            kbuf = [cv.get([128, NTOK], BF16) for _ in range(2)]
            vbuf = [cv.get([128, NTOK // 128, 129], BF16) for _ in range(2)]
            r_kb = [Res(), Res()]
            r_vb = [Res(), Res()]
            ptb = [cv.get([128, 128], BF16) for _ in range(2)]
            r_pt = [Res(), Res()]
            dtmp = cv.get([128, 128])
            r_dtmp = Res()
            a0 = cv.get([128, 128])
            att = cv.get([128, 128])
            r_a0, r_att = Res(), Res()
            rec = cv.get([128, 8])
            r_rec = Res()
            psrot[0] = 6
            pti = 0
            oi = 0
            for h in range(8):
                k = h % 2
                sc.dma("pool", kbuf[k][:, 0:t0 + T], kc_d[l, h, :, 0:t0 + T], reads=[r_kc], writes=[r_kb[k]])
                sc.dma("pool", vbuf[k][:, 0:nkb, :], vc_d[l, h, 0:t0 + T, :].rearrange("(b p) e -> p b e", p=128), reads=[r_vc], writes=[r_vb[k]])
                for qb in range(NSL):
                    qabs = t0 // 128 + qb
                    psO, r_pO = psums[6 + oi % 2], r_ps[6 + oi % 2]
                    oi += 1
                    for c in range(2):
                        Oc = psO[:, c * 256: c * 256 + 129]
                        for kb in range(qabs + 1):
                            ps, r_p = psum()
                            mm_group(ps[:, 0:128], r_p, [(kbuf[k][c * 64:(c + 1) * 64, kb * 128:(kb + 1) * 128],
                                                          qT[c * 64:(c + 1) * 64, h, qb * 128:(qb + 1) * 128])], [r_kb[k], r_q])
                            pk = pti % 2
                            pti += 1
                            if kb < qabs:
                                r_ = qabs - kb
                                sc.op("act", lambda e: e.activation(out=ptb[pk], in_=ps[:, 0:128], func=AF.Exp, bias=tab[:, h, r_:r_ + 1], scale=0.125),
                                      reads=[r_p], writes=[r_pt[pk]])
                            else:
                                sc.op("dve", lambda e: e.scalar_tensor_tensor(out=dtmp, in0=ps[:, 0:128], scalar=0.125, in1=Dh[:, h, :], op0=ALU.mult, op1=ALU.add),
                                      reads=[r_p], writes=[r_dtmp])
                                sc.op("act", lambda e: e.activation(out=ptb[pk], in_=dtmp, func=AF.Exp), reads=[r_dtmp], writes=[r_pt[pk]])
                            sc.op("pe", lambda e: e.matmul(Oc, lhsT=ptb[pk], rhs=vbuf[k][:, kb, :], start=(kb == 0), stop=(kb == qabs)),
                                  reads=[r_pt[pk], r_vb[k]], writes=[r_pO])
                    sc.op("dve", lambda e: e.reciprocal(out=rec[:, 0:1], in_=psO[:, 128:129]), reads=[r_pO], writes=[r_rec])
                    sc.op("dve", lambda e: e.reciprocal(out=rec[:, 1:2], in_=psO[:, 384:385]), reads=[r_pO], writes=[r_rec])
                    sc.op("dve", lambda e: e.tensor_tensor(out=rec[:, 2:3], in0=rec[:, 1:2], in1=lamt[l][:, 1:2], op=ALU.mult), reads=[r_rec], writes=[r_rec])
                    sc.op("dve", lambda e: e.tensor_scalar(out=a0, in0=psO[:, 0:128], scalar1=rec[:, 0:1], scalar2=None, op0=ALU.mult), reads=[r_pO, r_rec], writes=[r_a0])
                    sc.op("dve", lambda e: e.scalar_tensor_tensor(out=att, in0=psO[:, 256:384], scalar=rec[:, 2:3], in1=a0, op0=ALU.mult, op1=ALU.add),
                          reads=[r_pO, r_rec, r_a0], writes=[r_att])
                    sc.op("dve", lambda e: e.memset(rec[:, 3:4], 0.0), writes=[r_rec])
                    sc.op("act", lambda e: e.activation(out=a0, in_=att, func=AF.Square, accum_out=rec[:, 3:4]), reads=[r_att], writes=[r_a0, r_rec])
                    sc.op("act", lambda e: e.activation(out=rec[:, 4:5], in_=rec[:, 3:4], func=AF.Sqrt, bias=epsc[:, 0:1], scale=1.0 / 128), reads=[r_rec], writes=[r_rec])
                    sc.op("dve", lambda e: e.reciprocal(out=rec[:, 5:6], in_=rec[:, 4:5]), reads=[r_rec], writes=[r_rec])
                    sc.op("dve", lambda e: e.scalar_tensor_tensor(out=att, in0=att, scalar=rec[:, 5:6], in1=dngB[l][:], op0=ALU.mult, op1=ALU.mult),
                          reads=[r_att, r_rec], writes=[r_att])
                    sc.op("dve", lambda e: e.scalar_tensor_tensor(out=obb[:, qb, h * 128:(h + 1) * 128], in0=att, scalar=(1.0 - lam_init), in1=zb[:, qb, h * 128:(h + 1) * 128],
                                                                  op0=ALU.mult, op1=ALU.mult), reads=[r_att, r_zb], writes=[r_obb])
            psrot[0] = 8
            for s in range(NSL):
                for kc in range(8):
                    transpose_to(obT[:, kc, s * 128:(s + 1) * 128], r_obT, obb[:, s, kc * 128:(kc + 1) * 128], r_obb, 128, BF16, evac=("dve" if kc % 2 == 0 else "act"))
            dbg_dump("obT", obT[:], [128, 8, T], BF16, reads=[r_obT])
            sc.barrier()
            if stop == "B":
                break
            cv = Carve()
            qcT = cv.get([128, 8, T], BF16)
            kcT = cv.get([128, 8, T], BF16)
            kcM = cv.get([128, NSL, 1024], BF16)
            vcm = cv.get([128, NSL, 8 * 129], BF16)
            ocm = cv.get([128, NSL, 1024], BF16)
            zcm = cv.get([128, NSL, 1024], BF16)
            ocb = cv.get([128, 1024], BF16)
            r_qk, r_kcM, r_vcm, r_ocm, r_zcm, r_ocb = Res(), Res(), Res(), Res(), Res(), Res()
            sc.op("dve", lambda e: e.memset(vcm[:], 1.0), writes=[r_vcm])
            for (c0, dstT, scl) in ((C_QC, qcT, 1.0), (C_KCC, kcT, 128.0 ** -0.5)):
                for wi in range(4):
                    wv, r_w = w_next(("w_in", c0 + 256 * wi))
                    for half in range(2):
                        h = 2 * wi + half
                        ps, r_p = psum()
                        mm_group(ps[:, 0:T], r_p, [(wv[:, kc, half * 128:(half + 1) * 128], xT[:, kc, :]) for kc in range(KC)], [r_w, r_xT])
                        sc.op("dve", lambda e: e.tensor_scalar(out=dstT[:, h, :], in0=ps[:, 0:T], scalar1=scl, scalar2=None, op0=ALU.mult), reads=[r_p], writes=[r_qk])
                        if c0 == C_KCC:
                            for s in range(NSL):
                                transpose_to(kcM[:, s, h * 128:(h + 1) * 128], r_kcM, kcT[:, h, s * 128:(s + 1) * 128], r_qk, 128, BF16, evac="act")
            for (c0, kind) in ((C_VC, "v"), (C_OC, "o"), (C_ZC, "z")):
                for wi in range(4):
                    wv, r_w = w_next(("w_in", c0 + 256 * wi))
                    for s in range(NSL):
                        ps, r_p = psum()
                        mm_group(ps[:, 0:256], r_p, [(xT[:, kc, s * 128:(s + 1) * 128], wv[:, kc, :]) for kc in range(KC)], [r_w, r_xT])
                        if kind == "v":
                            sc.op("dve", lambda e: e.tensor_copy(out=vcm[:, s, :].rearrange("p (h e) -> p h e", h=8)[:, 2 * wi:2 * wi + 2, 0:128],
                                                                in_=ps[:, 0:256].rearrange("p (h e) -> p h e", h=2)), reads=[r_p], writes=[r_vcm])
                        elif kind == "o":
                            sc.op("act", lambda e: e.activation(out=ocm[:, s, wi * 256:(wi + 1) * 256], in_=ps[:, 0:256], func=AF.Sigmoid), reads=[r_p], writes=[r_ocm])
                        else:
                            sc.op("act", lambda e: e.activation(out=zcm[:, s, wi * 256:(wi + 1) * 256], in_=ps[:, 0:256], func=AF.Silu), reads=[r_p], writes=[r_zcm])
            iF = cv.get([8, T])
            lf = cv.get([8, T])
            FF = cv.get([8, T])
            gg = cv.get([8, T])
            MM = cv.get([8, T])
            rF = cv.get([8, T])
            cF = cv.get([8, T])
            flF = cv.get([8, T])
            ones8T = cv.get([8, T])
            nMp = cv.get([8, 1])
            flT = cv.get([128, NSL, 8])
            r_g = Res()
            r_nMp = Res()
            r_flT = Res()
            wv, r_w = w_next(("w_in", C_I))
            ps, r_p = psum()
            mm_group(ps[0:8, 0:T], r_p, [(wv[:, kc, 0:8], xT[:, kc, :]) for kc in range(KC)], [r_w, r_xT])
            sc.op("act", lambda e: e.activation(out=iF, in_=ps[0:8, 0:T], func=AF.Identity, bias=mgb[l][:, 0:1], scale=1.0), reads=[r_p], writes=[r_g])
            ps, r_p = psum()
            mm_group(ps[0:8, 0:T], r_p, [(wv[:, kc, 8:16], xT[:, kc, :]) for kc in range(KC)], [r_w, r_xT])
            sc.op("act", lambda e: e.activation(out=lf, in_=ps[0:8, 0:T], func=AF.Exp, bias=nfb[l][:, 0:1], scale=-1.0), reads=[r_p], writes=[r_g])
            sc.op("act", lambda e: e.activation(out=lf, in_=lf, func=AF.Ln, bias=ones[0:8, 0:1], scale=1.0), reads=[r_g], writes=[r_g])
            sc.op("dve", lambda e: e.tensor_scalar(out=lf, in0=lf, scalar1=-1.0, scalar2=None, op0=ALU.mult), reads=[r_g], writes=[r_g])
            sc.op("dve", lambda e: e.memset(ones8T, 1.0), writes=[r_g])
            sc.op("dve", lambda e: e.tensor_tensor_scan(out=FF, data0=ones8T, data1=lf, initial=FM8[l][:, 0:1], op0=ALU.mult, op1=ALU.add), reads=[r_g, r_FM8[l]], writes=[r_g])
            sc.op("dve", lambda e: e.tensor_tensor(out=gg, in0=iF, in1=FF, op=ALU.subtract), reads=[r_g], writes=[r_g])
            sc.op("dve", lambda e: e.tensor_tensor_scan(out=MM, data0=gg, data1=gg, initial=FM8[l][:, 1:2], op0=ALU.max, op1=ALU.max), reads=[r_g, r_FM8[l]], writes=[r_g])
            sc.op("dve", lambda e: e.tensor_scalar(out=nMp, in0=FM8[l][:, 1:2], scalar1=-1.0, scalar2=None, op0=ALU.mult), reads=[r_FM8[l]], writes=[r_nMp])
            sc.op("dve", lambda e: e.tensor_copy(out=FM8[l][:, 0:1], in_=FF[:, T - 1:T]), reads=[r_g, r_nMp], writes=[r_FM8[l]])
            sc.op("dve", lambda e: e.tensor_copy(out=FM8[l][:, 1:2], in_=MM[:, T - 1:T]), reads=[r_g, r_nMp], writes=[r_FM8[l]])
            sc.op("dve", lambda e: e.tensor_scalar(out=rF, in0=MM, scalar1=-1.0, scalar2=None, op0=ALU.mult), reads=[r_g], writes=[r_g])
            sc.op("dve", lambda e: e.tensor_scalar(out=cF, in0=gg, scalar1=-1.0, scalar2=None, op0=ALU.mult), reads=[r_g], writes=[r_g])
            sc.op("dve", lambda e: e.tensor_tensor(out=flF, in0=FF, in1=MM, op=ALU.add), reads=[r_g], writes=[r_g])
            sc.op("act", lambda e: e.activation(out=flF, in_=flF, func=AF.Exp, scale=-1.0), reads=[r_g], writes=[r_g])
            for s in range(NSL):
                transpose_to(flT[:, s, :], r_flT, flF[:, s * 128:(s + 1) * 128], r_g, 8, F32)
            termsC = fm_terms(cv, 8, rF, cF, None, r_g, r_nMp, nMp, NSL)
            t1c = cv.get([128, 129])
            r_t1c = Res()
            dn = cv.get([128, 4])
            r_dn = Res()
            hc = cv.get([128, 128])
            r_hc = Res()
            cvmark = cv.off
            for c in range(NSL):
                cv.off = cvmark
                ch = slice(c * 128, (c + 1) * 128)
                scT, r_scT, decB, r_dec = termsC[c]

                def out_cbc(g, psy, r_py, c=c, scT=scT, r_scT=r_scT):
                    h = g
                    sc.op("dve", lambda e: e.tensor_scalar(out=t1c, in0=psy[:, 129:258], scalar1=scT[:, 2, h:h + 1], scalar2=None, op0=ALU.mult),
                          reads=[r_py, r_scT], writes=[r_t1c])
                    sc.op("dve", lambda e: e.tensor_tensor(out=t1c, in0=t1c, in1=psy[:, 0:129], op=ALU.add), reads=[r_py, r_t1c], writes=[r_t1c])
                    sc.op("dve", lambda e: e.tensor_scalar(out=dn[:, 3:4], in0=t1c[:, 128:129], scalar1=-1.0, scalar2=None, op0=ALU.mult), reads=[r_t1c], writes=[r_dn])
                    sc.op("dve", lambda e: e.tensor_tensor(out=dn[:, 0:1], in0=dn[:, 3:4], in1=t1c[:, 128:129], op=ALU.max), reads=[r_t1c, r_dn], writes=[r_dn])
                    sc.op("dve", lambda e: e.tensor_tensor(out=dn[:, 1:2], in0=dn[:, 0:1], in1=flT[:, c, h:h + 1], op=ALU.max), reads=[r_dn, r_flT], writes=[r_dn])
                    sc.op("dve", lambda e: e.reciprocal(out=dn[:, 2:3], in_=dn[:, 1:2]), reads=[r_dn], writes=[r_dn])
                    sc.op("dve", lambda e: e.scalar_tensor_tensor(out=hc, in0=t1c[:, 0:128], scalar=dn[:, 2:3], in1=ocm[:, c, h * 128:(h + 1) * 128], op0=ALU.mult, op1=ALU.mult),
                          reads=[r_t1c, r_dn, r_ocm], writes=[r_hc])
                    sc.op("dve", lambda e: e.tensor_tensor(out=ocb[:, h * 128:(h + 1) * 128], in0=hc, in1=zcm[:, c, h * 128:(h + 1) * 128], op=ALU.mult),
                          reads=[r_hc, r_zcm], writes=[r_ocb])

                dla_chunk(cv, 8, 1, 129,
                          lambda g: kcT[:, g, ch], lambda g: qcT[:, g, ch], r_qk,
                          lambda g: kcM[:, c, g * 128:(g + 1) * 128], vcm[:, c, :].rearrange("p (h e) -> p h e", h=8), r_vcm,
                          rF[:, ch], cF[:, ch], r_g, scT, r_scT,
                          Cst[l][:], Cbf[l][:], r_C[l], r_Cbf[l], decB, r_dec, out_cbc)
                for kc in range(8):
                    transpose_to(ocT[:, kc, ch], r_ocT, ocb[:, kc * 128:(kc + 1) * 128], r_ocb, 128, BF16, evac=("dve" if kc % 2 == 0 else "act"))
            dbg_dump("ocT", ocT[:], [128, 8, T], BF16, reads=[r_ocT])
            sc.barrier()
            if stop == "C":
                break
            cv = Carve()
            sg = [cv.get([128, 2 * T]) for _ in range(2)]
            macc = cv.get([128, 2 * T])
            mtmp = cv.get([128, 2 * T])
            r_sg = [Res(), Res()]
            r_macc, r_mtmp = Res(), Res()
            gi = 0
            for j in range(16):
                wb, r_wb = w_next(("w_branch", 256 * j))
                ybanks = []
                for (k0, k1, actT, r_a) in ((0, 16, oaT, r_oaT), (16, 24, obT, r_obT), (24, 32, ocT, r_ocT)):
                    py, r_py = psum()
                    for half in range(2):
                        mm_group(py[:, half * T:(half + 1) * T], r_py, [(wb[:, kc, half * 128:(half + 1) * 128], actT[:, kc - k0, :]) for kc in range(k0, k1)], [r_wb, r_a])
                    ybanks.append((py, r_py))
                for br in range(3):
                    wg, r_wg = w_next(("w_in", C_G + br * 4096 + 256 * j))
                    pg, r_pg = psum()
                    for half in range(2):
                        mm_group(pg[:, half * T:(half + 1) * T], r_pg, [(wg[:, kc, half * 128:(half + 1) * 128], xT[:, kc, :]) for kc in range(KC)], [r_wg, r_xT])
                    k = gi % 2
                    gi += 1
                    sc.op("act", lambda e: e.activation(out=sg[k], in_=pg[:, 0:2 * T], func=AF.Sigmoid), reads=[r_pg], writes=[r_sg[k]])
                    py, r_py = ybanks[br]
                    if br == 0:
                        sc.op("dve", lambda e: e.tensor_tensor(out=macc, in0=sg[k], in1=py[:, 0:2 * T], op=ALU.mult), reads=[r_sg[k], r_py], writes=[r_macc])
                    else:
                        sc.op("dve", lambda e: e.tensor_tensor(out=mtmp, in0=sg[k], in1=py[:, 0:2 * T], op=ALU.mult), reads=[r_sg[k], r_py], writes=[r_mtmp])
                        if br == 1:
                            sc.op("dve", lambda e: e.tensor_tensor(out=macc, in0=macc, in1=mtmp, op=ALU.add), reads=[r_macc, r_mtmp], writes=[r_macc])
                        else:
                            sc.op("dve", lambda e: e.tensor_tensor(out=mT[:, 2 * j:2 * j + 2, :], in0=macc.rearrange("p (a b) -> p a b", a=2),
                                                                  in1=mtmp.rearrange("p (a b) -> p a b", a=2), op=ALU.add), reads=[r_macc, r_mtmp], writes=[r_mT])
            dbg_dump("mT", mT[:], [128, KC, T], BF16, reads=[r_mT])
            sc.barrier()
            if stop == "M":
                break
            cv = Carve()
            xr = cv.get([128, NSL, D])
            r_xr = [Res() for _ in range(NSL)]
            gsl = [cv.get([128, 512]) for _ in range(2)]
            bsl = [cv.get([128, 512]) for _ in range(2)]
            r_gsl = [Res(), Res()]
            r_bsl = [Res(), Res()]
            wple = cv.get([128, 2, D], BF16)
            r_wple = Res()
            pT = cv.get([128, 2, T], BF16)
            pl = cv.get([128, 256])
            r_pT, r_pl = Res(), Res()
            stats = cv.get([128, 8, 6])
            mv = cv.get([128, 8])
            r_stats, r_mv = Res(), Res()
            ssqp = cv.get([128, NSL, 8])
            rse = cv.get([128, NSL, 4])
            r_ssqp, r_rse = Res(), Res()
            sqj = cv.get([128, 512], BF16)
            r_sqj = Res()
            gt = [cv.get([128, 256]) for _ in range(2)]
            et = [cv.get([128, 256]) for _ in range(2)]
            pgs = [cv.get([128, 256]) for _ in range(2)]
            r_gt = [Res(), Res()]
            r_et = [Res(), Res()]
            r_pgs = [Res(), Res()]
            xsrc = x_d if l == 0 else hb_d
            for s in range(NSL):
                sc.dma("pool", xr[:, s, :], xsrc[t0 + s * 128: t0 + (s + 1) * 128, :], reads=([r_hb] if l > 0 else []), writes=[r_xr[s]])
            pg_, po_ = WB_PAGE[WPLE_B]
            sc.dma("pool", wple[:].rearrange("p a b -> p (a b)"), wsc_view(l, WPLE_B, 2 * D), writes=[r_wple])
            for j in range(16):
                wo, r_wo = w_next(("w_out", 256 * j))
                for s in range(NSL):
                    ps, r_p = psum()
                    mm_group(ps[:, 0:256], r_p, [(mT[:, kc, s * 128:(s + 1) * 128], wo[:, kc, :]) for kc in range(KC)], [r_wo, r_mT])
                    sc.op("dve", lambda e: e.scalar_tensor_tensor(out=xr[:, s, j * 256:(j + 1) * 256], in0=xr[:, s, j * 256:(j + 1) * 256], scalar=ALPHA, in1=ps[:, 0:256],
                                                                  op0=ALU.mult, op1=ALU.add), reads=[r_p, r_xr[s]], writes=[r_xr[s]])
            gi = 0
            for s in range(NSL):
                for cb in range(8):
                    sc.op("dve", lambda e: e.bn_stats(out=stats[:, cb, :], in_=xr[:, s, cb * 512:(cb + 1) * 512]), reads=[r_xr[s]], writes=[r_stats])
                sc.op("dve", lambda e: e.bn_aggr(out=mv[:, 0:2], in_=stats[:].rearrange("p a b -> p (a b)")), reads=[r_stats], writes=[r_mv])
                sc.op("act", lambda e: e.activation(out=mv[:, 2:3], in_=mv[:, 1:2], func=AF.Sqrt, bias=epsc[:, 0:1], scale=1.0), reads=[r_mv], writes=[r_mv])
                sc.op("dve", lambda e: e.reciprocal(out=mv[:, 3:4], in_=mv[:, 2:3]), reads=[r_mv], writes=[r_mv])
                sc.op("dve", lambda e: e.tensor_scalar(out=xr[:, s, :], in0=xr[:, s, :], scalar1=mv[:, 0:1], scalar2=mv[:, 3:4], op0=ALU.subtract, op1=ALU.mult),
                      reads=[r_mv, r_xr[s]], writes=[r_xr[s]])
                for cb in range(8):
                    k = gi % 2
                    gi += 1
                    cols = slice(cb * 512, (cb + 1) * 512)
                    sc.dma("pool", gsl[k], lng_d[l][:, cols].partition_broadcast(128), writes=[r_gsl[k]])
                    sc.dma("pool", bsl[k], lnb_d[l][:, cols].partition_broadcast(128), writes=[r_bsl[k]])
                    sc.op("dve", lambda e: e.tensor_tensor(out=xr[:, s, cols], in0=xr[:, s, cols], in1=gsl[k], op=ALU.mult), reads=[r_gsl[k], r_xr[s]], writes=[r_xr[s]])
                    sc.op("dve", lambda e: e.tensor_tensor(out=xr[:, s, cols], in0=xr[:, s, cols], in1=bsl[k], op=ALU.add), reads=[r_bsl[k], r_xr[s]], writes=[r_xr[s]])
                for kc in range(KC):
                    transpose_to(xT[:, kc, s * 128:(s + 1) * 128], r_xT, xr[:, s, kc * 128:(kc + 1) * 128], r_xr[s], 128, F32, evac=("dve" if kc % 2 == 0 else "act"))
                sc.dma("pool", pl, p_d[l, t0 + s * 128: t0 + (s + 1) * 128, :], writes=[r_pl])
                for k2 in range(2):
                    transpose_to(pT[:, k2, s * 128:(s + 1) * 128], r_pT, pl[:, k2 * 128:(k2 + 1) * 128], r_pl, 128, F32)
                sc.op("dve", lambda e: e.memset(ssqp[:, s, :], 0.0), writes=[r_ssqp])
                for cb in range(8):
                    ps, r_p = psum()
                    mm_group(ps[:, 0:512], r_p, [(pT[:, k2, s * 128:(s + 1) * 128], wple[:, k2, cb * 512:(cb + 1) * 512]) for k2 in range(2)], [r_pT, r_wple])
                    sc.op("act", lambda e: e.activation(out=sqj, in_=ps[:, 0:512], func=AF.Square, accum_out=ssqp[:, s, cb:cb + 1]), reads=[r_p], writes=[r_sqj, r_ssqp])
                sc.op("dve", lambda e: e.reduce_sum(out=rse[:, s, 0:1], in_=ssqp[:, s, :], axis=mybir.AxisListType.X), reads=[r_ssqp], writes=[r_rse])
                sc.op("act", lambda e: e.activation(out=rse[:, s, 1:2], in_=rse[:, s, 0:1], func=AF.Sqrt, bias=epsc[:, 0:1], scale=1.0 / D), reads=[r_rse], writes=[r_rse])
                sc.op("dve", lambda e: e.reciprocal(out=rse[:, s, 2:3], in_=rse[:, s, 1:2]), reads=[r_rse], writes=[r_rse])
            gi = 0
            for j in range(16):
                wg, r_wg = w_next(("w_ple_gate", 256 * j))
                cols = slice(j * 256, (j + 1) * 256)
                kk = j % 2
                sc.dma("pool", pgs[kk], pleg_d[l][:, cols].partition_broadcast(128), writes=[r_pgs[kk]])
                for s in range(NSL):
                    k = gi % 2
                    gi += 1
                    ps1, r_p1 = psum()
                    mm_group(ps1[:, 0:256], r_p1, [(xT[:, kc, s * 128:(s + 1) * 128], wg[:, kc, :]) for kc in range(KC)], [r_wg, r_xT])
                    ps2, r_p2 = psum()
                    mm_group(ps2[:, 0:256], r_p2, [(pT[:, k2, s * 128:(s + 1) * 128], wple[:, k2, cols]) for k2 in range(2)], [r_pT, r_wple])
                    sc.op("act", lambda e: e.activation(out=gt[k], in_=ps1[:, 0:256], func=AF.Sigmoid), reads=[r_p1], writes=[r_gt[k]])
                    sc.op("dve", lambda e: e.scalar_tensor_tensor(out=et[k], in0=ps2[:, 0:256], scalar=rse[:, s, 2:3], in1=pgs[kk], op0=ALU.mult, op1=ALU.mult),
                          reads=[r_p2, r_rse, r_pgs[kk]], writes=[r_et[k]])
                    sc.op("dve", lambda e: e.tensor_tensor(out=et[k], in0=et[k], in1=gt[k], op=ALU.mult), reads=[r_et[k], r_gt[k]], writes=[r_et[k]])
                    sc.op("dve", lambda e: e.tensor_tensor(out=xr[:, s, cols], in0=xr[:, s, cols], in1=et[k], op=ALU.add), reads=[r_et[k], r_xr[s]], writes=[r_xr[s]])
            for s in range(NSL):
                rows = slice(t0 + s * 128, t0 + (s + 1) * 128)
                if last:
                    sc.dma("pool", out_d[rows, :], xr[:, s, :], reads=[r_xr[s]])
                else:
                    sc.dma("pool", hb_d[rows, :], xr[:, s, :], reads=[r_xr[s]], writes=[r_hb])
                    for kc in range(KC):
                        transpose_to(xT[:, kc, s * 128:(s + 1) * 128], r_xT, xr[:, s, kc * 128:(kc + 1) * 128], r_xr[s], 128, F32, evac=("dve" if kc % 2 == 0 else "act"))
            sc.barrier()
        if stop is not None:
            break
    sc.barrier()
    return nc, dbg_out


def make_consts():
    c = np.zeros((128, 128 * 3 + 8 * 32 + 8 * 128), np.float32)
    c[:, 0:128] = np.eye(128, dtype=np.float32)
    j = np.arange(128)[:, None]
    i = np.arange(128)[None, :]
    c[:, 128:256] = (j <= i).astype(np.float32)
    c[:, 256:384] = 1.0
    slopes = 2.0 ** (-(np.arange(8) + 1.0))
    pp = np.arange(128, dtype=np.float64)[:, None, None]
    rr = np.arange(32, dtype=np.float64)[None, None, :]
    tab = slopes[None, :, None] * (pp - 128.0 * rr - 64.0)
    c[:, 384:640] = tab.reshape(128, 256).astype(np.float32)
    ki = np.arange(128, dtype=np.float64)[:, None, None]
    qi = np.arange(128, dtype=np.float64)[None, None, :]
    dh = slopes[None, :, None] * (qi - 64.0 - np.abs(qi - ki))
    dh = np.where((ki >= 64) & (qi < 64), -30000.0, dh)
    c[:, 640:1664] = dh.reshape(128, 1024).astype(np.float32)
    return c


def prep_inputs(inp, b, NL, NTOK):
    f = np.float32
    m = {}
    m["x"] = np.ascontiguousarray(inp["x"][b, :NTOK]).astype(f, copy=False)
    m["p"] = np.ascontiguousarray(inp["p"][:NL, b, :NTOK]).astype(f, copy=False)
    for k in ("w_in", "w_branch", "w_out", "w_ple", "w_ple_gate"):
        m[k] = np.ascontiguousarray(inp[k][:NL])
    m["convw"] = np.ascontiguousarray(inp["conv_w"][:NL].reshape(NL, 4, 32, 128).transpose(0, 3, 2, 1))
    m["convb"] = np.ascontiguousarray(inp["conv_b"][:NL].reshape(NL, 32, 128).transpose(0, 2, 1))
    hp = np.zeros((NL, 32, 4), f)
    hp[:, :, 0] = inp["dt_bias"][:NL]
    hp[:, :, 1] = inp["a_log"][:NL]
    m["hp"] = hp
    m["dskip"] = np.ascontiguousarray(inp["d_skip"][:NL].reshape(NL, 1, 32))
    m["ssmg"] = np.ascontiguousarray(inp["ssm_norm_g"][:NL].reshape(NL, 1, 2048))
    m["dlam"] = np.ascontiguousarray(inp["diff_lambda"][:NL].reshape(NL, 1, 256))
    m["dng"] = np.ascontiguousarray(inp["diff_norm_g"][:NL].reshape(NL, 1, 128))
    m["mgb"] = np.ascontiguousarray(inp["mlstm_gate_b"][:NL].transpose(0, 2, 1))
    m["lng"] = np.ascontiguousarray(inp["ln_g"][:NL].reshape(NL, 1, D))
    m["lnb"] = np.ascontiguousarray(inp["ln_b"][:NL].reshape(NL, 1, D))
    m["pleg"] = np.ascontiguousarray(inp["ple_norm_g"][:NL].reshape(NL, 1, D))
    m["cst"] = make_consts()
    return m


def kernel(**inputs):
    NL = 2
    nb = inputs["x"].shape[0]
    nc, _ = build(NL=NL, NTOK=SEQ)
    in_maps = [prep_inputs(inputs, b, NL, SEQ) for b in range(nb)]
    res = run_bass_kernel_spmd(nc, in_maps, core_ids=list(range(nb)))
    out = np.stack([np.asarray(r["out"], dtype=np.float32) for r in res.results], axis=0)
    return out
```

```python
import math
import numpy as np
import concourse.bass as bass
import concourse.mybir as mybir
from concourse.bass_utils import run_bass_kernel_spmd

F32 = mybir.dt.float32
BF16 = mybir.dt.bfloat16
AF = mybir.ActivationFunctionType
ALU = mybir.AluOpType

D = 4096
SEQ = 4096
T = 256
NSL = T // 128
KC = 32
WIN = 27696
C_XBC, C_ZA, C_DT, C_QB, C_KB, C_VB, C_ZB = 0, 4096, 6144, 6176, 7200, 8224, 9248
C_QC, C_KCC, C_VC, C_OC, C_ZC, C_I, C_F, C_G = 10272, 11296, 12320, 13344, 14368, 15392, 15400, 15408
ALPHA = (2.0 * 2) ** 0.25
ARENA_MAX = [0]
PEI = [0]
MARKS = []
EPS = 1e-5


class Res:
    __slots__ = ("w", "r")

    def __init__(self):
        self.w = None
        self.r = {}


class Sched:
    def __init__(self, nc, ndma=20):
        self.nc = nc
        self.eng = dict(pe=nc.tensor, act=nc.scalar, dve=nc.vector, pool=nc.gpsimd, sp=nc.sync)
        self.sems = {}
        self.cnt = {}
        for k in ("pe", "act", "dve", "pool"):
            self.sems[k] = nc.semaphore("s_" + k).__enter__()
            self.cnt[k] = 0
        self.dsem = [nc.semaphore("d%d" % i).__enter__() for i in range(ndma)]
        self.dcnt = [0] * ndma
        self.dnext = 0
        self.seen = {e: {} for e in self.eng}
        self.nops = 0

    def _sem(self, k):
        return self.sems[k] if isinstance(k, str) else self.dsem[k]

    def _wait(self, e, need, keep_one=False):
        sn = self.seen[e]
        todo = [(k, v) for k, v in need.items() if sn.get(k, 0) < v]
        fused = None
        if keep_one and todo:
            k, v = todo.pop()
            fused = (self._sem(k), v)
            sn[k] = v
        for k, v in todo:
            self.eng[e].wait_ge(self._sem(k), v)
            sn[k] = v
        return fused

    def _collect(self, reads, writes):
        need = {}

        def add(t):
            if t is None:
                return
            k, v = t
            if need.get(k, 0) < v:
                need[k] = v

        for r in reads:
            add(r.w)
        for w in writes:
            add(w.w)
            for k, v in w.r.items():
                add((k, v))
        return need

    def _commit(self, tok, reads, writes):
        k, v = tok
        for r in reads:
            if r.r.get(k, 0) < v:
                r.r[k] = v
        for w in writes:
            w.w = tok
            w.r = {}

    def op(self, e, fn, reads=(), writes=()):
        need = self._collect(reads, writes)
        if e == "pe":
            need.pop("pe", None)
        fused = self._wait(e, need, keep_one=True)
        ins = fn(self.eng[e])
        first = ins
        if isinstance(ins, tuple):
            first, ins = ins
        if fused is not None:
            first._wait_ge(fused[0], fused[1])
        self.cnt[e] += 1
        ins.then_inc(self.sems[e], 1)
        tok = (e, self.cnt[e])
        self._commit(tok, reads, writes)
        self.nops += 1
        return tok

    def dma(self, e, out, in_, reads=(), writes=(), **kw):
        i = self.dnext
        self.dnext = (self.dnext + 1) % len(self.dsem)
        need = self._collect(reads, writes)
        if self.dcnt[i]:
            if need.get(i, 0) < self.dcnt[i]:
                need[i] = self.dcnt[i]
        fused = self._wait(e, need, keep_one=True)
        ins = self.eng[e].dma_start(out=out, in_=in_, **kw)
        if fused is not None:
            ins._wait_ge(fused[0], fused[1])
        ins.then_inc(self.dsem[i], 16)
        self.dcnt[i] += 16
        tok = (i, self.dcnt[i])
        self._commit(tok, reads, writes)
        return tok

    def barrier(self, engines=("pe", "act", "dve", "pool", "sp"), skip_sems=()):
        need = {k: v for k, v in self.cnt.items() if v}
        for i, v in enumerate(self.dcnt):
            if v:
                need[i] = v
        for e in engines:
            self._wait(e, dict(need))


def _wblocks():
    b = []
    for i in range(16):
        b.append(("w_in", C_XBC + 256 * i, 256, 32))
    b.append(("w_in", C_DT, 32, 32))
    for i in range(8):
        b.append(("w_in", C_ZA + 256 * i, 256, 32))
    for c0 in (C_QB, C_KB, C_VB, C_ZB):
        for i in range(4):
            b.append(("w_in", c0 + 256 * i, 256, 32))
    for c0 in (C_QC, C_KCC, C_VC, C_OC, C_ZC):
        for i in range(4):
            b.append(("w_in", c0 + 256 * i, 256, 32))
    b.append(("w_in", C_I, 16, 32))
    for j in range(16):
        b.append(("w_branch", 256 * j, 256, 32))
        for br in range(3):
            b.append(("w_in", C_G + br * 4096 + 256 * j, 256, 32))
    for j in range(16):
        b.append(("w_out", 256 * j, 256, 32))
    b.append(("w_ple", 0, 4096, 2))
    for j in range(16):
        b.append(("w_ple_gate", 256 * j, 256, 32))
    return b


WBLOCKS = _wblocks()
WB_ELEMS = [128 * kc * n for (_, _, n, kc) in WBLOCKS]
WB_OFF = np.concatenate([[0], np.cumsum(WB_ELEMS)]).astype(np.int64)
WL_ELEMS = int(WB_OFF[-1])
PAGE_ELEMS = 48 * 1024 * 1024
WB_PAGE = []
_pg, _po = 0, 0
for _n in WB_ELEMS:
    if _po + _n > PAGE_ELEMS:
        _pg += 1
        _po = 0
    WB_PAGE.append((_pg, _po))
    _po += _n
NPAGES = _pg + 1


def build(NL=2, NTOK=SEQ, dbg=None, stop=None):
    NT = NTOK // T
    nc = bass.Bass("TRN2", target_bir_lowering=False)
    sc = Sched(nc)
    E = sc.eng
    dbg_out = {}

    def din(name, shape, dt=F32):
        return nc.dram_tensor(name, list(shape), dt, kind="ExternalInput").ap()

    x_d = din("x", [NTOK, D])
    p_d = din("p", [NL, NTOK, 256])
    wmat = dict(w_in=din("w_in", [NL, D, WIN]), w_branch=din("w_branch", [NL, D, D]), w_out=din("w_out", [NL, D, D]),
                w_ple=din("w_ple", [NL, 256, D]), w_ple_gate=din("w_ple_gate", [NL, D, D]))
    convw_d = din("convw", [NL, 128, 32, 4])
    convb_d = din("convb", [NL, 128, 32])
    hp_d = din("hp", [NL, 32, 4])
    dskip_d = din("dskip", [NL, 1, 32])
    ssmg_d = din("ssmg", [NL, 1, 2048])
    dlam_d = din("dlam", [NL, 1, 256])
    dng_d = din("dng", [NL, 1, 128])
    mgb_d = din("mgb", [NL, 8, 2])
    lng_d = din("lng", [NL, 1, D])
    lnb_d = din("lnb", [NL, 1, D])
    pleg_d = din("pleg", [NL, 1, D])
    cst_d = din("cst", [128, 128 * 3 + 8 * 32 + 8 * 128])
    out_d = nc.dram_tensor("out", [NTOK, D], F32, kind="ExternalOutput").ap()
    wsc_pages = [[nc.dram_tensor("wsc%d_%d" % (l, g), [PAGE_ELEMS], BF16, kind="Internal").ap() for g in range(NPAGES)] for l in range(NL)]

    def wsc_view(l, b, ne):
        pg, po = WB_PAGE[b]
        return wsc_pages[l][pg][po: po + 128 * ne].rearrange("(p e) -> p e", p=128)
    kc_d = nc.dram_tensor("kcache", [NL, 8, 128, NTOK], BF16, kind="Internal").ap()
    vc_d = nc.dram_tensor("vcache", [NL, 8, NTOK, 129], BF16, kind="Internal").ap()
    hb_d = nc.dram_tensor("hbuf", [NTOK, D], F32, kind="Internal").ap()
    r_wsc, r_kc, r_vc, r_hb = Res(), Res(), Res(), Res()

    def dbg_dump(name, ap, shape, dt=F32, reads=()):
        if dbg is None or name not in dbg:
            return
        o = nc.dram_tensor("dbg_" + name, list(shape), dt, kind="ExternalOutput").ap()
        dbg_out[name] = sc.dma("pool", o, ap, reads=reads)

    def sb(name, shape, dt=F32):
        return nc.sbuf_tensor(name, list(shape), dt).__enter__()

    st_cm = [nc.sbuf_tensor("pst%d" % i, [128, 8192], F32) for i in range(2)]
    sb_cm = [nc.sbuf_tensor("psb%d" % i, [128, 8192], BF16) for i in range(2)]
    stg = [c.__enter__() for c in st_cm]
    sbg = [c.__enter__() for c in sb_cm]
    r_stg = [Res(), Res()]
    r_sbg = [Res(), Res()]
    bi = 0
    for l in range(NL):
        for bidx, (mat, c0, n, kc) in enumerate(WBLOCKS):
            s = bi % 2
            W = wmat[mat][l]
            src = W[:, c0:c0 + n].rearrange("(k p) c -> p k c", p=128)
            stv = stg[s][:, 0:kc * n].rearrange("p (k c) -> p k c", k=kc)
            step = 8 if kc >= 8 else kc
            if n > 1024:
                step = 1
            for k0 in range(0, kc, step):
                if n > 1024:
                    for c1 in range(0, n, 1024):
                        sc.dma("sp", stv[:, k0:k0 + step, c1:c1 + 1024], src[:, k0:k0 + step, c1:c1 + 1024], writes=[r_stg[s]])
                else:
                    sc.dma("sp", stv[:, k0:k0 + step, :], src[:, k0:k0 + step, :], writes=[r_stg[s]])
            ne = kc * n
            if bi % 2 == 0:
                sc.op("dve", lambda e: e.tensor_copy(out=sbg[s][:, 0:ne], in_=stg[s][:, 0:ne]), reads=[r_stg[s]], writes=[r_sbg[s]])
            else:
                sc.op("act", lambda e: e.copy(out=sbg[s][:, 0:ne], in_=stg[s][:, 0:ne]), reads=[r_stg[s]], writes=[r_sbg[s]])
            dst = wsc_view(l, bidx, ne)
            sc.dma("pool", dst, sbg[s][:, 0:ne], reads=[r_sbg[s]], writes=[r_wsc])
            bi += 1
    sc.barrier()
    for c in reversed(sb_cm):
        c.__exit__(None, None, None)
    for c in reversed(st_cm):
        c.__exit__(None, None, None)

    cst = sb("cst_sb", [128, 128 * 3 + 8 * 32 + 8 * 128])
    ident = cst[:, 0:128]
    trimask = cst[:, 128:256]
    ones = cst[:, 256:384]
    tab = cst[:, 384:384 + 256].rearrange("p (h r) -> p h r", h=8)
    Dh = cst[:, 640:640 + 1024].rearrange("p (h q) -> p h q", h=8)
    identb = sb("identb", [128, 128], BF16)
    epsc = sb("epsc", [128, 1])
    zero32 = sb("zero32", [32, 1])
    xT = sb("xT", [128, KC, T], BF16)
    wbuf = [sb("wbuf%d" % i, [128, 8192], BF16) for i in range(2)]
    r_wbuf = [Res(), Res()]
    oaT = sb("oaT", [128, 16, T], BF16)
    obT = sb("obT", [128, 8, T], BF16)
    ocT = sb("ocT", [128, 8, T], BF16)
    mT = sb("mT", [128, KC, T], BF16)
    r_xT, r_oaT, r_obT, r_ocT, r_mT = Res(), Res(), Res(), Res(), Res()
    Sst = [sb("Sst%d" % l, [128, 2048]) for l in range(NL)]
    Sbf = [sb("Sbf%d" % l, [128, 2048], BF16) for l in range(NL)]
    tails = [sb("tails%d" % l, [128, 32, 3]) for l in range(NL)]
    Cst = [sb("Cst%d" % l, [128, 8, 129]) for l in range(NL)]
    Cbf = [sb("Cbf%d" % l, [128, 8, 129], BF16) for l in range(NL)]
    FM8 = [sb("fm8_%d" % l, [8, 2]) for l in range(NL)]
    r_S = [Res() for _ in range(NL)]
    r_Sbf = [Res() for _ in range(NL)]
    r_tails = [Res() for _ in range(NL)]
    r_C = [Res() for _ in range(NL)]
    r_Cbf = [Res() for _ in range(NL)]
    r_FM8 = [Res() for _ in range(NL)]
    convw = [sb("convw%d" % l, [128, 32, 4]) for l in range(NL)]
    convb = [sb("convb%d" % l, [128, 32]) for l in range(NL)]
    hp = [sb("hp%d" % l, [32, 4]) for l in range(NL)]
    nega = [sb("nega%d" % l, [32, 1]) for l in range(NL)]
    dskB = [sb("dskB%d" % l, [128, 32]) for l in range(NL)]
    dngB = [sb("dngB%d" % l, [128, 128]) for l in range(NL)]
    lamt = [sb("lam%d" % l, [128, 4]) for l in range(NL)]
    dlt = sb("dlt", [128, 256])
    mgb = [sb("mgb%d" % l, [8, 2]) for l in range(NL)]
    nfb = [sb("nfb%d" % l, [8, 1]) for l in range(NL)]
    ARENA = 72 * 1024
    arena = sb("arena", [128, ARENA // 2], BF16)
    psums = [nc.psum_tensor("ps%d" % i, [128, 512], F32).__enter__() for i in range(8)]
    r_ps = [Res() for _ in range(8)]
    psn = [0]

    psrot = [8]

    def psum():
        i = psn[0] % psrot[0]
        psn[0] += 1
        return psums[i], r_ps[i]

    class Carve:
        def __init__(self):
            self.off = 0

        def get(self, shape, dt=F32):
            n = int(np.prod(shape[1:]))
            nb = n * (4 if dt == F32 else 2)
            nb_al = (nb + 63) // 64 * 64
            assert self.off + nb_al <= ARENA, ("arena overflow", self.off, nb_al)
            ARENA_MAX[0] = max(ARENA_MAX[0], self.off + nb_al)
            v = arena[0:shape[0], self.off // 2: self.off // 2 + nb // 2]
            self.off += nb_al
            if dt == F32:
                v = v.bitcast(F32)
            if len(shape) == 3:
                v = v.rearrange("p (a b) -> p a b", a=shape[1])
            return v

    r_c = Res()
    sc.dma("pool", cst[:], cst_d, writes=[r_c])
    sc.op("dve", lambda e: e.tensor_copy(out=identb[:], in_=ident), reads=[r_c], writes=[r_c])
    sc.op("dve", lambda e: e.memset(epsc[:], EPS), writes=[r_c])
    sc.op("dve", lambda e: e.memset(zero32[:], 0.0), writes=[r_c])
    for l in range(NL):
        sc.dma("pool", convw[l][:], convw_d[l], writes=[r_c])
        sc.dma("pool", convb[l][:], convb_d[l], writes=[r_c])
        sc.dma("pool", hp[l][:], hp_d[l], writes=[r_c])
        sc.dma("pool", dskB[l][:], dskip_d[l].partition_broadcast(128), writes=[r_c])
        sc.dma("pool", dngB[l][:], dng_d[l].partition_broadcast(128), writes=[r_c])
        sc.dma("pool", mgb[l][:], mgb_d[l], writes=[r_c])
        sc.dma("pool", dlt[:], dlam_d[l].partition_broadcast(128), writes=[r_c])
        sc.op("act", lambda e: e.activation(out=nega[l][:], in_=hp[l][:, 1:2], func=AF.Exp), reads=[r_c], writes=[r_c])
        sc.op("dve", lambda e: e.tensor_scalar(out=nega[l][:], in0=nega[l][:], scalar1=-1.0, scalar2=None, op0=ALU.mult), reads=[r_c], writes=[r_c])
        sc.op("dve", lambda e: e.tensor_scalar(out=nfb[l][:], in0=mgb[l][:, 1:2], scalar1=-1.0, scalar2=None, op0=ALU.mult), reads=[r_c], writes=[r_c])
        lam_init = 0.8 - 0.6 * math.exp(-0.3 * l)
        sc.op("dve", lambda e: e.tensor_tensor(out=dlt[:, 0:64], in0=dlt[:, 0:64], in1=dlt[:, 64:128], op=ALU.mult), reads=[r_c], writes=[r_c])
        sc.op("dve", lambda e: e.tensor_tensor(out=dlt[:, 128:192], in0=dlt[:, 128:192], in1=dlt[:, 192:256], op=ALU.mult), reads=[r_c], writes=[r_c])
        sc.op("dve", lambda e: e.reduce_sum(out=lamt[l][:, 2:3], in_=dlt[:, 0:64], axis=mybir.AxisListType.X), reads=[r_c], writes=[r_c])
        sc.op("dve", lambda e: e.reduce_sum(out=lamt[l][:, 3:4], in_=dlt[:, 128:192], axis=mybir.AxisListType.X), reads=[r_c], writes=[r_c])
        sc.op("act", lambda e: e.activation(out=lamt[l][:, 2:4], in_=lamt[l][:, 2:4], func=AF.Exp), reads=[r_c], writes=[r_c])
        sc.op("dve", lambda e: e.tensor_tensor(out=lamt[l][:, 0:1], in0=lamt[l][:, 2:3], in1=lamt[l][:, 3:4], op=ALU.subtract), reads=[r_c], writes=[r_c])
        sc.op("dve", lambda e: e.tensor_scalar(out=lamt[l][:, 0:1], in0=lamt[l][:, 0:1], scalar1=lam_init, scalar2=None, op0=ALU.add), reads=[r_c], writes=[r_c])
        sc.op("dve", lambda e: e.tensor_scalar(out=lamt[l][:, 1:2], in0=lamt[l][:, 0:1], scalar1=-1.0, scalar2=None, op0=ALU.mult), reads=[r_c], writes=[r_c])
        sc.op("dve", lambda e: e.memset(Sst[l][:], 0.0), writes=[r_S[l]])
        sc.op("dve", lambda e: e.memset(Sbf[l][:], 0.0), writes=[r_Sbf[l]])
        sc.op("dve", lambda e: e.memset(tails[l][:], 0.0), writes=[r_tails[l]])
        sc.op("dve", lambda e: e.memset(Cst[l][:], 0.0), writes=[r_C[l]])
        sc.op("dve", lambda e: e.memset(Cbf[l][:], 0.0), writes=[r_Cbf[l]])
        sc.op("dve", lambda e: e.memset(FM8[l][:], 0.0), writes=[r_FM8[l]])
    sc.barrier()

    WPLE_B = WBLOCKS.index(("w_ple", 0, 4096, 2))
    wplan = [(l, b) for _t in range(NT) for l in range(NL) for b in range(len(WBLOCKS)) if b != WPLE_B]
    wstate = dict(issued=0, used=0)

    def w_issue():
        i = wstate["issued"]
        if i >= len(wplan):
            return
        l, b = wplan[i]
        _, _, n, kc = WBLOCKS[b]
        ne = kc * n
        s = i % 2
        src = wsc_view(l, b, ne)
        sc.dma("sp", wbuf[s][:, 0:ne], src, writes=[r_wbuf[s]])
        wstate["issued"] = i + 1

    def w_next(expect):
        i = wstate["used"]
        l, b = wplan[i]
        mat, c0, n, kc = WBLOCKS[b]
        assert (mat, c0) == expect, (WBLOCKS[b], expect)
        while wstate["issued"] <= min(i + 1, len(wplan) - 1):
            w_issue()
        wstate["used"] = i + 1
        s = i % 2
        return wbuf[s][:, 0:kc * n].rearrange("p (k c) -> p k c", k=kc), r_wbuf[s]

    def mm_group(ps_ap, r_p, pairs, reads):
        n = len(pairs)
        PEI[0] += n

        def fn(e):
            ins = None
            first = None
            for i, (a, b) in enumerate(pairs):
                ins = e.matmul(ps_ap, lhsT=a, rhs=b, start=(i == 0), stop=(i == n - 1))
                if first is None:
                    first = ins
            return (first, ins)
        return sc.op("pe", fn, reads=reads, writes=[r_p])

    def transpose_to(dst_ap, r_dst, src_ap, r_src, kpart, dt, evac="dve"):
        ps, r_p = psum()
        PEI[0] += 1
        nfree = src_ap.shape[-1]
        if dt == BF16:
            pv = ps[:].bitcast(BF16)[0:nfree, 0:kpart]
            idn = identb[0:kpart, 0:kpart]
        else:
            pv = ps[0:nfree, 0:kpart]
            idn = ident[0:kpart, 0:kpart]
        sc.op("pe", lambda e: e.transpose(out=pv, in_=src_ap, identity=idn), reads=[r_src], writes=[r_p])
        if evac == "dve":
            sc.op("dve", lambda e: e.tensor_copy(out=dst_ap, in_=pv), reads=[r_p], writes=[r_dst])
        else:
            sc.op("act", lambda e: e.copy(out=dst_ap, in_=pv), reads=[r_p], writes=[r_dst])

    def dla_chunk(cv, nh, hpg, vw, bT, cT_, r_bc, b_tm, val_tm, r_val, rFM, cFM, r_fm, scT, r_scT, state, state_bf,
                  r_state, r_state_bf, decB, r_dec, out_cb):
        ng = nh // hpg
        gw = hpg * vw
        GTm = cv.get([128, 128])
        r_GTm = Res()
        xin = cv.get([128, hpg, vw], BF16)
        xcs = cv.get([128, hpg, vw], BF16)
        r_xin, r_xcs = Res(), Res()
        rh = [cv.get([nh, 128]) for _ in range(2)]
        argb = [cv.get([128, 128]) for _ in range(2)]
        PTb = [cv.get([128, 128], BF16) for _ in range(2)]
        r_rh = [Res(), Res()]
        r_arg = [Res(), Res()]
        r_PT = [Res(), Res()]
        t4 = cv.get([128, hpg, vw])
        r_t4 = Res()
        hi = 0
        for g in range(ng):
            hs = slice(g * hpg, (g + 1) * hpg)
            ps1, r_p1 = psum()
            mm_group(ps1[:, 0:128], r_p1, [(bT(g), cT_(g))], [r_bc])
            sc.op("dve", lambda e: e.tensor_tensor(out=GTm, in0=ps1[:, 0:128], in1=trimask, op=ALU.mult), reads=[r_p1], writes=[r_GTm])
            sc.op("dve", lambda e: e.tensor_tensor(out=xin, in0=val_tm[:, hs, :], in1=scT[:, 1, hs].unsqueeze(2).to_broadcast([128, hpg, vw]), op=ALU.mult),
                  reads=[r_val, r_scT], writes=[r_xin])
            sc.op("dve", lambda e: e.tensor_tensor(out=xcs, in0=val_tm[:, hs, :], in1=scT[:, 3, hs].unsqueeze(2).to_broadcast([128, hpg, vw]), op=ALU.mult),
                  reads=[r_val, r_scT], writes=[r_xcs])
            psy, r_py = psum()
            psyv = psy[:, 0:2 * gw]
            for hh in range(hpg):
                h = g * hpg + hh
                k = hi % 2
                hi += 1
                sc.op("dve", lambda e: e.tensor_scalar(out=rh[k], in0=rFM, scalar1=ident[0:nh, h:h + 1], scalar2=None, op0=ALU.mult),
                      reads=[r_fm], writes=[r_rh[k]])
                ps2, r_p2 = psum()
                mm_group(ps2[:, 0:128], r_p2, [(ones[0:nh, :], rh[k])], [r_rh[k]])
                sc.op("dve", lambda e: e.tensor_scalar(out=argb[k], in0=ps2[:, 0:128], scalar1=scT[:, 0, h:h + 1], scalar2=0.0, op0=ALU.subtract, op1=ALU.min),
                      reads=[r_p2, r_scT], writes=[r_arg[k]])
                sc.op("act", lambda e: e.activation(out=argb[k], in_=argb[k], func=AF.Exp), reads=[r_arg[k]], writes=[r_arg[k]])
                sc.op("dve", lambda e: e.tensor_tensor(out=PTb[k], in0=argb[k], in1=GTm, op=ALU.mult), reads=[r_arg[k], r_GTm], writes=[r_PT[k]])
                mm_group(psy[:, hh * vw:(hh + 1) * vw], r_py, [(PTb[k], xin[:, hh, :])], [r_PT[k], r_xin])
            mm_group(psy[:, gw:2 * gw], r_py, [(cT_(g), state_bf[:, hs, :])], [r_bc, r_state_bf])
            out_cb(g, psy, r_py)
            ps3, r_p3 = psum()
            mm_group(ps3[:, 0:gw], r_p3, [(b_tm(g), xcs[:])], [r_bc, r_xcs])
            sc.op("dve", lambda e: e.tensor_tensor(out=t4, in0=state[:, hs, :], in1=decB[:, hs].unsqueeze(2).to_broadcast([128, hpg, vw]), op=ALU.mult),
                  reads=[r_state, r_dec], writes=[r_t4])
            sc.op("dve", lambda e: e.tensor_tensor(out=state[:, hs, :], in0=t4, in1=ps3[:, 0:gw].rearrange("p (a b) -> p a b", a=hpg), op=ALU.add),
                  reads=[r_t4, r_p3], writes=[r_state])
            sc.op("act", lambda e: e.copy(out=state_bf[:, hs, :], in_=state[:, hs, :]), reads=[r_state], writes=[r_state_bf])

    def fm_terms(cv, nh, rFM_t, cFM_t, extra_t, r_fm, r_prev, rprev_ap, nchunks):
        outs = []
        nrp = cv.get([nh, 1])
        rs = cv.get([nh, 128])
        cs = cv.get([nh, 128])
        dec = cv.get([nh, 1])
        dg = cv.get([nh, nh])
        r_t = Res()
        for c in range(nchunks):
            ch = slice(c * 128, (c + 1) * 128)
            prev = rprev_ap if c == 0 else rFM_t[:, c * 128 - 1:c * 128]
            rend = rFM_t[:, (c + 1) * 128 - 1:(c + 1) * 128]
            scT = cv.get([128, 4, nh])
            decB = cv.get([128, nh])
            r_scT, r_dec = Res(), Res()
            sc.op("dve", lambda e: e.tensor_scalar(out=nrp, in0=prev, scalar1=-1.0, scalar2=None, op0=ALU.mult), reads=[r_fm, r_prev], writes=[r_t])
            sc.op("act", lambda e: e.activation(out=rs, in_=rFM_t[:, ch], func=AF.Exp, bias=nrp, scale=1.0), reads=[r_fm, r_t], writes=[r_t])
            sc.op("act", lambda e: e.activation(out=cs, in_=cFM_t[:, ch], func=AF.Exp, bias=rend, scale=-1.0), reads=[r_fm, r_t], writes=[r_t])
            if extra_t is not None:
                sc.op("dve", lambda e: e.tensor_tensor(out=cs, in0=cs, in1=extra_t[:, ch], op=ALU.mult), reads=[r_fm, r_t], writes=[r_t])
            sc.op("act", lambda e: e.activation(out=dec, in_=rend, func=AF.Exp, bias=nrp, scale=1.0), reads=[r_fm, r_t], writes=[r_t])
            sc.op("dve", lambda e: e.tensor_scalar(out=dg, in0=ident[0:nh, 0:nh], scalar1=dec, scalar2=None, op0=ALU.mult), reads=[r_t], writes=[r_t])
            ps, r_p = psum()
            mm_group(ps[:, 0:nh], r_p, [(ones[0:nh, :], dg)], [r_t])
            sc.op("dve", lambda e: e.tensor_copy(out=decB, in_=ps[:, 0:nh]), reads=[r_p], writes=[r_dec])
            ps, r_p = psum()
            srcs = [cFM_t[:, ch], (extra_t[:, ch] if extra_t is not None else None), rs, cs]
            PEI[0] += sum(1 for s_ in srcs if s_ is not None)

            def fn(e):
                ins = None
                first = None
                for i, s_ in enumerate(srcs):
                    if s_ is None:
                        continue
                    ins = e.transpose(out=ps[:, i * nh:(i + 1) * nh], in_=s_, identity=ident[0:nh, 0:nh])
                    if first is None:
                        first = ins
                return (first, ins)
            sc.op("pe", fn, reads=[r_fm, r_t], writes=[r_p])
            if extra_t is None:
                sc.op("dve", lambda e: e.memset(scT[:, 1, :], 1.0), writes=[r_scT])
                sc.op("dve", lambda e: e.tensor_copy(out=scT[:, 0, :], in_=ps[:, 0:nh]), reads=[r_p], writes=[r_scT])
                sc.op("dve", lambda e: e.tensor_copy(out=scT[:, 2:4, :], in_=ps[:, 2 * nh:4 * nh].rearrange("p (a b) -> p a b", a=2)), reads=[r_p], writes=[r_scT])
            else:
                sc.op("dve", lambda e: e.tensor_copy(out=scT, in_=ps[:, 0:4 * nh].rearrange("p (a b) -> p a b", a=4)), reads=[r_p], writes=[r_scT])
            outs.append((scT, r_scT, decB, r_dec))
        return outs

    for t in range(NT):
        t0 = t * T
        for l in range(NL):
            last = (l == NL - 1)
            lam_init = 0.8 - 0.6 * math.exp(-0.3 * l)
            if l == 0:
                cv = Carve()
                xs = [cv.get([128, D]) for _ in range(2)]
                r_xs = [Res(), Res()]
                for s in range(NSL):
                    k = s % 2
                    sc.dma("pool", xs[k], x_d[t0 + s * 128: t0 + (s + 1) * 128, :], writes=[r_xs[k]])
                    for kc in range(KC):
                        transpose_to(xT[:, kc, s * 128:(s + 1) * 128], r_xT, xs[k][:, kc * 128:(kc + 1) * 128], r_xs[k], 128, F32,
                                     evac=("dve" if kc % 2 == 0 else "act"))
                sc.barrier(engines=("pe", "act", "dve", "pool"))
            MARKS.append(("A", t, l, PEI[0]))
            cv = Carve()
            xh = cv.get([128, NSL, 2048], BF16)
            BF = cv.get([128, 8, T], BF16)
            CF = cv.get([128, 8, T], BF16)
            BT = cv.get([128, NSL, 1024], BF16)
            za = cv.get([128, NSL, 2048], BF16)
            ub = [cv.get([128, T + 3]) for _ in range(2)]
            acc = [cv.get([128, T]) for _ in range(2)]
            xc = [cv.get([128, T], BF16) for _ in range(2)]
            r_xh, r_BC, r_za = Res(), Res(), Res()
            r_ub = [Res(), Res()]
            r_acc = [Res(), Res()]
            r_xc = [Res(), Res()]
            blk = 0
            pendA = []
            for wi in range(16):
                wv, r_w = w_next(("w_in", C_XBC + 256 * wi))
                for half in range(2):
                    k = blk % 2
                    ps, r_p = psum()
                    mm_group(ps[:, 0:T], r_p, [(wv[:, kc, half * 128:(half + 1) * 128], xT[:, kc, :]) for kc in range(KC)], [r_w, r_xT])
                    for f_ in pendA:
                        f_()
                    pendA = []
                    sc.op("act", lambda e: e.copy(out=ub[k][:, 3:3 + T], in_=ps[:, 0:T]), reads=[r_p], writes=[r_ub[k]])
                    sc.op("dve", lambda e: e.tensor_copy(out=ub[k][:, 0:3], in_=tails[l][:, blk, :]), reads=[r_tails[l]], writes=[r_ub[k]])
                    sc.op("dve", lambda e: e.tensor_scalar(out=acc[k], in0=ub[k][:, 0:T], scalar1=convw[l][:, blk, 0:1], scalar2=None, op0=ALU.mult),
                          reads=[r_ub[k]], writes=[r_acc[k]])
                    for j in range(1, 4):
                        sc.op("dve", lambda e: e.scalar_tensor_tensor(out=acc[k], in0=ub[k][:, j:j + T], scalar=convw[l][:, blk, j:j + 1], in1=acc[k],
                                                                      op0=ALU.mult, op1=ALU.add), reads=[r_ub[k], r_acc[k]], writes=[r_acc[k]])
                    sc.op("dve", lambda e: e.tensor_copy(out=tails[l][:, blk, :], in_=ub[k][:, T:T + 3]), reads=[r_ub[k]], writes=[r_tails[l]])
                    if blk < 16:
                        dst, r_dst = xc[k], r_xc[k]
                    elif blk < 24:
                        dst, r_dst = BF[:, blk - 16, :], r_BC
                    else:
                        dst, r_dst = CF[:, blk - 24, :], r_BC
                    sc.op("act", lambda e: e.activation(out=dst, in_=acc[k], func=AF.Silu, bias=convb[l][:, blk:blk + 1], scale=1.0),
                          reads=[r_acc[k]], writes=[r_dst])
                    if blk < 16:
                        for s in range(NSL):
                            pendA.append(lambda s=s, blk=blk, k=k: transpose_to(xh[:, s, blk * 128:(blk + 1) * 128], r_xh, xc[k][:, s * 128:(s + 1) * 128], r_xc[k], 128, BF16,
                                                                                evac=("dve" if s % 2 == 0 else "act")))
                    elif blk < 24:
                        for s in range(NSL):
                            pendA.append(lambda s=s, blk=blk: transpose_to(BT[:, s, (blk - 16) * 128:(blk - 15) * 128], r_BC, BF[:, blk - 16, s * 128:(s + 1) * 128], r_BC, 128, BF16,
                                                                           evac=("dve" if s % 2 == 0 else "act")))
                    blk += 1
            for f_ in pendA:
                f_()
            pendA = []
            MARKS.append(("A_dt", t, l, PEI[0]))
            dtF = cv.get([32, T])
            laF = cv.get([32, T])
            AF_ = cv.get([32, T])
            ones32T = cv.get([32, T])
            r_dtf = Res()
            wv, r_w = w_next(("w_in", C_DT))
            ps, r_p = psum()
            mm_group(ps[0:32, 0:T], r_p, [(wv[:, kc, 0:32], xT[:, kc, :]) for kc in range(KC)], [r_w, r_xT])
            sc.op("act", lambda e: e.activation(out=dtF, in_=ps[0:32, 0:T], func=AF.Exp, bias=hp[l][:, 0:1], scale=1.0), reads=[r_p], writes=[r_dtf])
            sc.op("act", lambda e: e.activation(out=dtF, in_=dtF, func=AF.Ln, bias=ones[0:32, 0:1], scale=1.0), reads=[r_dtf], writes=[r_dtf])
            sc.op("dve", lambda e: e.tensor_scalar(out=laF, in0=dtF, scalar1=nega[l][:, 0:1], scalar2=None, op0=ALU.mult), reads=[r_dtf], writes=[r_dtf])
            sc.op("dve", lambda e: e.memset(ones32T, 1.0), writes=[r_dtf])
            sc.op("dve", lambda e: e.tensor_tensor_scan(out=AF_, data0=ones32T, data1=laF, initial=0.0, op0=ALU.mult, op1=ALU.add), reads=[r_dtf], writes=[r_dtf])
            r_z32 = Res()
            terms = fm_terms(cv, 32, AF_, AF_, dtF, r_dtf, r_z32, zero32[:], NSL)
            for wi in range(8):
                wv, r_w = w_next(("w_in", C_ZA + 256 * wi))
                for s in range(NSL):
                    ps, r_p = psum()
                    mm_group(ps[:, 0:256], r_p, [(xT[:, kc, s * 128:(s + 1) * 128], wv[:, kc, :]) for kc in range(KC)], [r_w, r_xT])
                    sc.op("act", lambda e: e.activation(out=za[:, s, wi * 256:(wi + 1) * 256], in_=ps[:, 0:256], func=AF.Silu), reads=[r_p], writes=[r_za])
            MARKS.append(("A_ssd", t, l, PEI[0]))
            ysb = cv.get([128, 2048])
            r_y = Res()
            t1 = cv.get([128, 4, 64])
            t3 = cv.get([128, 4, 64])
            r_t1, r_t3 = Res(), Res()
            gA = cv.get([128, 2048])
            r_gA = Res()
            oab = cv.get([128, 2048], BF16)
            r_oab = Res()
            st2 = cv.get([128, 4])
            r_st2 = Res()
            sc.dma("pool", gA, ssmg_d[l].partition_broadcast(128), writes=[r_gA])
            cvmark = cv.off
            for c in range(NSL):
                cv.off = cvmark
                ch = slice(c * 128, (c + 1) * 128)
                scT, r_scT, decB, r_dec = terms[c]
                xh_c = xh[:, c, :].rearrange("p (h v) -> p h v", h=32)

                def out_cb(g, psy, r_py, c=c, xh_c=xh_c, scT=scT, r_scT=r_scT):
                    hs = slice(4 * g, 4 * g + 4)
                    sc.op("dve", lambda e: e.tensor_tensor(out=t1, in0=psy[:, 256:512].rearrange("p (a b) -> p a b", a=4),
                                                          in1=scT[:, 2, hs].unsqueeze(2).to_broadcast([128, 4, 64]), op=ALU.mult),
                          reads=[r_py, r_scT], writes=[r_t1])
                    sc.op("dve", lambda e: e.tensor_tensor(out=t1, in0=t1, in1=psy[:, 0:256].rearrange("p (a b) -> p a b", a=4), op=ALU.add),
                          reads=[r_py, r_t1], writes=[r_t1])
                    sc.op("dve", lambda e: e.tensor_tensor(out=t3, in0=xh_c[:, hs, :], in1=dskB[l][:, hs].unsqueeze(2).to_broadcast([128, 4, 64]), op=ALU.mult),
                          reads=[r_xh], writes=[r_t3])
                    sc.op("dve", lambda e: e.tensor_tensor(out=ysb[:, g * 256:(g + 1) * 256].rearrange("p (a b) -> p a b", a=4), in0=t1, in1=t3, op=ALU.add),
                          reads=[r_t1, r_t3], writes=[r_y])

                dla_chunk(cv, 32, 4, 64,
                          lambda g: BF[:, g, ch], lambda g: CF[:, g, ch], r_BC,
                          lambda g: BT[:, c, g * 128:(g + 1) * 128], xh_c, r_xh,
                          AF_[:, ch], AF_[:, ch], r_dtf, scT, r_scT,
                          Sst[l][:].rearrange("p (h v) -> p h v", h=32), Sbf[l][:].rearrange("p (h v) -> p h v", h=32),
                          r_S[l], r_Sbf[l], decB, r_dec, out_cb)
                sc.op("dve", lambda e: e.tensor_tensor(out=ysb, in0=ysb, in1=za[:, c, :], op=ALU.mult), reads=[r_y, r_za], writes=[r_y])
                sc.op("dve", lambda e: e.memset(st2[:], 0.0), writes=[r_st2])
                sc.op("act", lambda e: e.activation(out=oab, in_=ysb, func=AF.Square, accum_out=st2[:, 0:1]), reads=[r_y], writes=[r_oab, r_st2])
                sc.op("act", lambda e: e.activation(out=st2[:, 1:2], in_=st2[:, 0:1], func=AF.Sqrt, bias=epsc[:, 0:1], scale=1.0 / 2048), reads=[r_st2], writes=[r_st2])
                sc.op("dve", lambda e: e.reciprocal(out=st2[:, 2:3], in_=st2[:, 1:2]), reads=[r_st2], writes=[r_st2])
                sc.op("dve", lambda e: e.scalar_tensor_tensor(out=oab, in0=ysb, scalar=st2[:, 2:3], in1=gA, op0=ALU.mult, op1=ALU.mult),
                      reads=[r_y, r_st2, r_gA], writes=[r_oab])
                for kc in range(16):
                    transpose_to(oaT[:, kc, ch], r_oaT, oab[:, kc * 128:(kc + 1) * 128], r_oab, 128, BF16, evac=("dve" if kc % 2 == 0 else "act"))
            dbg_dump("oaT", oaT[:], [128, 16, T], BF16, reads=[r_oaT])
            sc.barrier(engines=("pe", "act", "dve", "pool"))
            if stop == "A":
                break

            MARKS.append(("B", t, l, PEI[0]))
            cv = Carve()
            qT = cv.get([128, 8, T], BF16)
            kTn = cv.get([128, 8, T], BF16)
            vn = cv.get([128, NSL, 8 * 129], BF16)
            zb = cv.get([128, NSL, 1024], BF16)
            obb = cv.get([128, NSL, 1024], BF16)
            r_q, r_kn, r_vn, r_zb, r_obb = Res(), Res(), Res(), Res(), Res()
            sc.op("dve", lambda e: e.memset(vn[:], 1.0), writes=[r_vn])
            for (c0, dstT, r_d) in ((C_QB, qT, r_q), (C_KB, kTn, r_kn)):
                for wi in range(4):
                    wv, r_w = w_next(("w_in", c0 + 256 * wi))
                    for half in range(2):
                        h = 2 * wi + half
                        ps, r_p = psum()
                        mm_group(ps[:, 0:T], r_p, [(wv[:, kc, half * 128:(half + 1) * 128], xT[:, kc, :]) for kc in range(KC)], [r_w, r_xT])
                        if half == 0:
                            sc.op("act", lambda e: e.copy(out=dstT[:, h, :], in_=ps[:, 0:T]), reads=[r_p], writes=[r_d])
                        else:
                            sc.op("dve", lambda e: e.tensor_copy(out=dstT[:, h, :], in_=ps[:, 0:T]), reads=[r_p], writes=[r_d])
            sc.dma("pool", kc_d[l][:, :, t0:t0 + T].rearrange("h p t -> p h t"), kTn[:], reads=[r_kn], writes=[r_kc])
            for wi in range(4):
                wv, r_w = w_next(("w_in", C_VB + 256 * wi))
                for s in range(NSL):
                    ps, r_p = psum()
                    mm_group(ps[:, 0:256], r_p, [(xT[:, kc, s * 128:(s + 1) * 128], wv[:, kc, :]) for kc in range(KC)], [r_w, r_xT])
                    sc.op("dve", lambda e: e.tensor_copy(out=vn[:, s, :].rearrange("p (h e) -> p h e", h=8)[:, 2 * wi:2 * wi + 2, 0:128],
                                                        in_=ps[:, 0:256].rearrange("p (h e) -> p h e", h=2)), reads=[r_p], writes=[r_vn])
            for s in range(NSL):
                sc.dma("pool", vc_d[l][:, t0 + s * 128: t0 + (s + 1) * 128, :].rearrange("h p e -> p h e"),
                       vn[:, s, :].rearrange("p (h e) -> p h e", h=8), reads=[r_vn], writes=[r_vc])
            for wi in range(4):
                wv, r_w = w_next(("w_in", C_ZB + 256 * wi))
                for s in range(NSL):
                    ps, r_p = psum()
                    mm_group(ps[:, 0:256], r_p, [(xT[:, kc, s * 128:(s + 1) * 128], wv[:, kc, :]) for kc in range(KC)], [r_w, r_xT])
                    sc.op("act", lambda e: e.activation(out=zb[:, s, wi * 256:(wi + 1) * 256], in_=ps[:, 0:256], func=AF.Silu), reads=[r_p], writes=[r_zb])
            MARKS.append(("B_att", t, l, PEI[0]))
            nkb = (t0 + T) // 128
            kbuf = [cv.get([128, NTOK], BF16) for _ in range(2)]
            vbuf = [cv.get([128, NTOK // 128, 129], BF16) for _ in range(2)]
            r_kb = [Res(), Res()]
            r_vb = [Res(), Res()]
            ptb = [cv.get([128, 128], BF16) for _ in range(2)]
            r_pt = [Res(), Res()]
            dtmp = cv.get([128, 128])
            r_dtmp = Res()
            a0 = cv.get([128, 128])
            att = cv.get([128, 128])
            r_a0, r_att = Res(), Res()
            rec = cv.get([128, 8])
            r_rec = Res()
            psrot[0] = 6
            pti = 0
            oi = 0
            for h in range(8):
                k = h % 2
                sc.dma("pool", kbuf[k][:, 0:t0 + T], kc_d[l, h, :, 0:t0 + T], reads=[r_kc], writes=[r_kb[k]])
                sc.dma("pool", vbuf[k][:, 0:nkb, :], vc_d[l, h, 0:t0 + T, :].rearrange("(b p) e -> p b e", p=128), reads=[r_vc], writes=[r_vb[k]])
                for qb in range(NSL):
                    qabs = t0 // 128 + qb
                    psO, r_pO = psums[6 + oi % 2], r_ps[6 + oi % 2]
                    oi += 1
                    for c in range(2):
                        Oc = psO[:, c * 256: c * 256 + 129]
                        for kb in range(qabs + 1):
                            ps, r_p = psum()
                            mm_group(ps[:, 0:128], r_p, [(kbuf[k][c * 64:(c + 1) * 64, kb * 128:(kb + 1) * 128],
                                                          qT[c * 64:(c + 1) * 64, h, qb * 128:(qb + 1) * 128])], [r_kb[k], r_q])
                            pk = pti % 2
                            pti += 1
                            if kb < qabs:
                                r_ = qabs - kb
                                sc.op("act", lambda e: e.activation(out=ptb[pk], in_=ps[:, 0:128], func=AF.Exp, bias=tab[:, h, r_:r_ + 1], scale=0.125),
                                      reads=[r_p], writes=[r_pt[pk]])
                            else:
                                sc.op("dve", lambda e: e.scalar_tensor_tensor(out=dtmp, in0=ps[:, 0:128], scalar=0.125, in1=Dh[:, h, :], op0=ALU.mult, op1=ALU.add),
                                      reads=[r_p], writes=[r_dtmp])
                                sc.op("act", lambda e: e.activation(out=ptb[pk], in_=dtmp, func=AF.Exp), reads=[r_dtmp], writes=[r_pt[pk]])
                            PEI[0] += 1
                            sc.op("pe", lambda e: e.matmul(Oc, lhsT=ptb[pk], rhs=vbuf[k][:, kb, :], start=(kb == 0), stop=(kb == qabs)),
                                  reads=[r_pt[pk], r_vb[k]], writes=[r_pO])
                    sc.op("dve", lambda e: e.reciprocal(out=rec[:, 0:1], in_=psO[:, 128:129]), reads=[r_pO], writes=[r_rec])
                    sc.op("dve", lambda e: e.reciprocal(out=rec[:, 1:2], in_=psO[:, 384:385]), reads=[r_pO], writes=[r_rec])
                    sc.op("dve", lambda e: e.tensor_tensor(out=rec[:, 2:3], in0=rec[:, 1:2], in1=lamt[l][:, 1:2], op=ALU.mult), reads=[r_rec], writes=[r_rec])
                    sc.op("dve", lambda e: e.tensor_scalar(out=a0, in0=psO[:, 0:128], scalar1=rec[:, 0:1], scalar2=None, op0=ALU.mult), reads=[r_pO, r_rec], writes=[r_a0])
                    sc.op("dve", lambda e: e.scalar_tensor_tensor(out=att, in0=psO[:, 256:384], scalar=rec[:, 2:3], in1=a0, op0=ALU.mult, op1=ALU.add),
                          reads=[r_pO, r_rec, r_a0], writes=[r_att])
                    sc.op("dve", lambda e: e.memset(rec[:, 3:4], 0.0), writes=[r_rec])
                    sc.op("act", lambda e: e.activation(out=a0, in_=att, func=AF.Square, accum_out=rec[:, 3:4]), reads=[r_att], writes=[r_a0, r_rec])
                    sc.op("act", lambda e: e.activation(out=rec[:, 4:5], in_=rec[:, 3:4], func=AF.Sqrt, bias=epsc[:, 0:1], scale=1.0 / 128), reads=[r_rec], writes=[r_rec])
                    sc.op("dve", lambda e: e.reciprocal(out=rec[:, 5:6], in_=rec[:, 4:5]), reads=[r_rec], writes=[r_rec])
                    sc.op("dve", lambda e: e.scalar_tensor_tensor(out=att, in0=att, scalar=rec[:, 5:6], in1=dngB[l][:], op0=ALU.mult, op1=ALU.mult),
                          reads=[r_att, r_rec], writes=[r_att])
                    sc.op("dve", lambda e: e.scalar_tensor_tensor(out=obb[:, qb, h * 128:(h + 1) * 128], in0=att, scalar=(1.0 - lam_init), in1=zb[:, qb, h * 128:(h + 1) * 128],
                                                                  op0=ALU.mult, op1=ALU.mult), reads=[r_att, r_zb], writes=[r_obb])
            psrot[0] = 8
            for s in range(NSL):
                for kc in range(8):
                    transpose_to(obT[:, kc, s * 128:(s + 1) * 128], r_obT, obb[:, s, kc * 128:(kc + 1) * 128], r_obb, 128, BF16, evac=("dve" if kc % 2 == 0 else "act"))
            dbg_dump("obT", obT[:], [128, 8, T], BF16, reads=[r_obT])
            sc.barrier(engines=("pe", "act", "dve", "pool"))
            if stop == "B":
                break
            MARKS.append(("C", t, l, PEI[0]))
            cv = Carve()
            qcT = cv.get([128, 8, T], BF16)
            kcT = cv.get([128, 8, T], BF16)
            kcM = cv.get([128, NSL, 1024], BF16)
            vcm = cv.get([128, NSL, 8 * 129], BF16)
            ocm = cv.get([128, NSL, 1024], BF16)
            zcm = cv.get([128, NSL, 1024], BF16)
            ocb = cv.get([128, 1024], BF16)
            r_qk, r_kcM, r_vcm, r_ocm, r_zcm, r_ocb = Res(), Res(), Res(), Res(), Res(), Res()
            sc.op("dve", lambda e: e.memset(vcm[:], 1.0), writes=[r_vcm])
            pendC = []
            for (c0, dstT, scl) in ((C_QC, qcT, 1.0), (C_KCC, kcT, 128.0 ** -0.5)):
                for wi in range(4):
                    wv, r_w = w_next(("w_in", c0 + 256 * wi))
                    for half in range(2):
                        h = 2 * wi + half
                        ps, r_p = psum()
                        mm_group(ps[:, 0:T], r_p, [(wv[:, kc, half * 128:(half + 1) * 128], xT[:, kc, :]) for kc in range(KC)], [r_w, r_xT])
                        for f_ in pendC:
                            f_()
                        pendC = []
                        sc.op("dve", lambda e: e.tensor_scalar(out=dstT[:, h, :], in0=ps[:, 0:T], scalar1=scl, scalar2=None, op0=ALU.mult), reads=[r_p], writes=[r_qk])
                        if c0 == C_KCC:
                            for s in range(NSL):
                                pendC.append(lambda s=s, h=h: transpose_to(kcM[:, s, h * 128:(h + 1) * 128], r_kcM, kcT[:, h, s * 128:(s + 1) * 128], r_qk, 128, BF16, evac="act"))
            for (c0, kind) in ((C_VC, "v"), (C_OC, "o"), (C_ZC, "z")):
                for wi in range(4):
                    wv, r_w = w_next(("w_in", c0 + 256 * wi))
                    for s in range(NSL):
                        ps, r_p = psum()
                        mm_group(ps[:, 0:256], r_p, [(xT[:, kc, s * 128:(s + 1) * 128], wv[:, kc, :]) for kc in range(KC)], [r_w, r_xT])
                        for f_ in pendC:
                            f_()
                        pendC = []
                        if kind == "v":
                            sc.op("dve", lambda e: e.tensor_copy(out=vcm[:, s, :].rearrange("p (h e) -> p h e", h=8)[:, 2 * wi:2 * wi + 2, 0:128],
                                                                in_=ps[:, 0:256].rearrange("p (h e) -> p h e", h=2)), reads=[r_p], writes=[r_vcm])
                        elif kind == "o":
                            sc.op("act", lambda e: e.activation(out=ocm[:, s, wi * 256:(wi + 1) * 256], in_=ps[:, 0:256], func=AF.Sigmoid), reads=[r_p], writes=[r_ocm])
                        else:
                            sc.op("act", lambda e: e.activation(out=zcm[:, s, wi * 256:(wi + 1) * 256], in_=ps[:, 0:256], func=AF.Silu), reads=[r_p], writes=[r_zcm])
            MARKS.append(("C_chunks", t, l, PEI[0]))
            iF = cv.get([8, T])
            lf = cv.get([8, T])
            FF = cv.get([8, T])
            gg = cv.get([8, T])
            MM = cv.get([8, T])
            rF = cv.get([8, T])
            cF = cv.get([8, T])
            flF = cv.get([8, T])
            ones8T = cv.get([8, T])
            nMp = cv.get([8, 1])
            flT = cv.get([128, NSL, 8])
            r_g = Res()
            r_nMp = Res()
            r_flT = Res()
            wv, r_w = w_next(("w_in", C_I))
            ps, r_p = psum()
            mm_group(ps[0:8, 0:T], r_p, [(wv[:, kc, 0:8], xT[:, kc, :]) for kc in range(KC)], [r_w, r_xT])
            sc.op("act", lambda e: e.activation(out=iF, in_=ps[0:8, 0:T], func=AF.Identity, bias=mgb[l][:, 0:1], scale=1.0), reads=[r_p], writes=[r_g])
            ps, r_p = psum()
            mm_group(ps[0:8, 0:T], r_p, [(wv[:, kc, 8:16], xT[:, kc, :]) for kc in range(KC)], [r_w, r_xT])
            sc.op("act", lambda e: e.activation(out=lf, in_=ps[0:8, 0:T], func=AF.Exp, bias=nfb[l][:, 0:1], scale=-1.0), reads=[r_p], writes=[r_g])
            sc.op("act", lambda e: e.activation(out=lf, in_=lf, func=AF.Ln, bias=ones[0:8, 0:1], scale=1.0), reads=[r_g], writes=[r_g])
            sc.op("dve", lambda e: e.tensor_scalar(out=lf, in0=lf, scalar1=-1.0, scalar2=None, op0=ALU.mult), reads=[r_g], writes=[r_g])
            sc.op("dve", lambda e: e.memset(ones8T, 1.0), writes=[r_g])
            sc.op("dve", lambda e: e.tensor_tensor_scan(out=FF, data0=ones8T, data1=lf, initial=FM8[l][:, 0:1], op0=ALU.mult, op1=ALU.add), reads=[r_g, r_FM8[l]], writes=[r_g])
            sc.op("dve", lambda e: e.tensor_tensor(out=gg, in0=iF, in1=FF, op=ALU.subtract), reads=[r_g], writes=[r_g])
            sc.op("dve", lambda e: e.tensor_tensor_scan(out=MM, data0=gg, data1=gg, initial=FM8[l][:, 1:2], op0=ALU.max, op1=ALU.max), reads=[r_g, r_FM8[l]], writes=[r_g])
            sc.op("dve", lambda e: e.tensor_scalar(out=nMp, in0=FM8[l][:, 1:2], scalar1=-1.0, scalar2=None, op0=ALU.mult), reads=[r_FM8[l]], writes=[r_nMp])
            sc.op("dve", lambda e: e.tensor_copy(out=FM8[l][:, 0:1], in_=FF[:, T - 1:T]), reads=[r_g, r_nMp], writes=[r_FM8[l]])
            sc.op("dve", lambda e: e.tensor_copy(out=FM8[l][:, 1:2], in_=MM[:, T - 1:T]), reads=[r_g, r_nMp], writes=[r_FM8[l]])
            sc.op("dve", lambda e: e.tensor_scalar(out=rF, in0=MM, scalar1=-1.0, scalar2=None, op0=ALU.mult), reads=[r_g], writes=[r_g])
            sc.op("dve", lambda e: e.tensor_scalar(out=cF, in0=gg, scalar1=-1.0, scalar2=None, op0=ALU.mult), reads=[r_g], writes=[r_g])
            sc.op("dve", lambda e: e.tensor_tensor(out=flF, in0=FF, in1=MM, op=ALU.add), reads=[r_g], writes=[r_g])
            sc.op("act", lambda e: e.activation(out=flF, in_=flF, func=AF.Exp, scale=-1.0), reads=[r_g], writes=[r_g])
            for s in range(NSL):
                transpose_to(flT[:, s, :], r_flT, flF[:, s * 128:(s + 1) * 128], r_g, 8, F32)
            termsC = fm_terms(cv, 8, rF, cF, None, r_g, r_nMp, nMp, NSL)
            t1c = cv.get([128, 129])
            r_t1c = Res()
            dn = cv.get([128, 4])
            r_dn = Res()
            hc = cv.get([128, 128])
            r_hc = Res()
            cvmark = cv.off
            for c in range(NSL):
                cv.off = cvmark
                ch = slice(c * 128, (c + 1) * 128)
                scT, r_scT, decB, r_dec = termsC[c]

                def out_cbc(g, psy, r_py, c=c, scT=scT, r_scT=r_scT):
                    h = g
                    sc.op("dve", lambda e: e.tensor_scalar(out=t1c, in0=psy[:, 129:258], scalar1=scT[:, 2, h:h + 1], scalar2=None, op0=ALU.mult),
                          reads=[r_py, r_scT], writes=[r_t1c])
                    sc.op("dve", lambda e: e.tensor_tensor(out=t1c, in0=t1c, in1=psy[:, 0:129], op=ALU.add), reads=[r_py, r_t1c], writes=[r_t1c])
                    sc.op("dve", lambda e: e.tensor_scalar(out=dn[:, 3:4], in0=t1c[:, 128:129], scalar1=-1.0, scalar2=None, op0=ALU.mult), reads=[r_t1c], writes=[r_dn])
                    sc.op("dve", lambda e: e.tensor_tensor(out=dn[:, 0:1], in0=dn[:, 3:4], in1=t1c[:, 128:129], op=ALU.max), reads=[r_t1c, r_dn], writes=[r_dn])
                    sc.op("dve", lambda e: e.tensor_tensor(out=dn[:, 1:2], in0=dn[:, 0:1], in1=flT[:, c, h:h + 1], op=ALU.max), reads=[r_dn, r_flT], writes=[r_dn])
                    sc.op("dve", lambda e: e.reciprocal(out=dn[:, 2:3], in_=dn[:, 1:2]), reads=[r_dn], writes=[r_dn])
                    sc.op("dve", lambda e: e.scalar_tensor_tensor(out=hc, in0=t1c[:, 0:128], scalar=dn[:, 2:3], in1=ocm[:, c, h * 128:(h + 1) * 128], op0=ALU.mult, op1=ALU.mult),
                          reads=[r_t1c, r_dn, r_ocm], writes=[r_hc])
                    sc.op("dve", lambda e: e.tensor_tensor(out=ocb[:, h * 128:(h + 1) * 128], in0=hc, in1=zcm[:, c, h * 128:(h + 1) * 128], op=ALU.mult),
                          reads=[r_hc, r_zcm], writes=[r_ocb])

                dla_chunk(cv, 8, 1, 129,
                          lambda g: kcT[:, g, ch], lambda g: qcT[:, g, ch], r_qk,
                          lambda g: kcM[:, c, g * 128:(g + 1) * 128], vcm[:, c, :].rearrange("p (h e) -> p h e", h=8), r_vcm,
                          rF[:, ch], cF[:, ch], r_g, scT, r_scT,
                          Cst[l][:], Cbf[l][:], r_C[l], r_Cbf[l], decB, r_dec, out_cbc)
                for kc in range(8):
                    transpose_to(ocT[:, kc, ch], r_ocT, ocb[:, kc * 128:(kc + 1) * 128], r_ocb, 128, BF16, evac=("dve" if kc % 2 == 0 else "act"))
            dbg_dump("ocT", ocT[:], [128, 8, T], BF16, reads=[r_ocT])
            sc.barrier(engines=("pe", "act", "dve", "pool"))
            if stop == "C":
                break
            MARKS.append(("M", t, l, PEI[0]))
            cv = Carve()
            sg = [cv.get([128, 2 * T]) for _ in range(2)]
            macc = cv.get([128, 2 * T])
            mtmp = cv.get([128, 2 * T])
            r_sg = [Res(), Res()]
            r_macc, r_mtmp = Res(), Res()
            gi = 0
            for j in range(16):
                wb, r_wb = w_next(("w_branch", 256 * j))
                ybanks = []
                for (k0, k1, actT, r_a) in ((0, 16, oaT, r_oaT), (16, 24, obT, r_obT), (24, 32, ocT, r_ocT)):
                    py, r_py = psum()
                    for half in range(2):
                        mm_group(py[:, half * T:(half + 1) * T], r_py, [(wb[:, kc, half * 128:(half + 1) * 128], actT[:, kc - k0, :]) for kc in range(k0, k1)], [r_wb, r_a])
                    ybanks.append((py, r_py))
                for br in range(3):
                    wg, r_wg = w_next(("w_in", C_G + br * 4096 + 256 * j))
                    pg, r_pg = psum()
                    for half in range(2):
                        mm_group(pg[:, half * T:(half + 1) * T], r_pg, [(wg[:, kc, half * 128:(half + 1) * 128], xT[:, kc, :]) for kc in range(KC)], [r_wg, r_xT])
                    k = gi % 2
                    gi += 1
                    sc.op("act", lambda e: e.activation(out=sg[k], in_=pg[:, 0:2 * T], func=AF.Sigmoid), reads=[r_pg], writes=[r_sg[k]])
                    py, r_py = ybanks[br]
                    if br == 0:
                        sc.op("dve", lambda e: e.tensor_tensor(out=macc, in0=sg[k], in1=py[:, 0:2 * T], op=ALU.mult), reads=[r_sg[k], r_py], writes=[r_macc])
                    else:
                        sc.op("dve", lambda e: e.tensor_tensor(out=mtmp, in0=sg[k], in1=py[:, 0:2 * T], op=ALU.mult), reads=[r_sg[k], r_py], writes=[r_mtmp])
                        if br == 1:
                            sc.op("dve", lambda e: e.tensor_tensor(out=macc, in0=macc, in1=mtmp, op=ALU.add), reads=[r_macc, r_mtmp], writes=[r_macc])
                        else:
                            sc.op("dve", lambda e: e.tensor_tensor(out=mT[:, 2 * j:2 * j + 2, :], in0=macc.rearrange("p (a b) -> p a b", a=2),
                                                                  in1=mtmp.rearrange("p (a b) -> p a b", a=2), op=ALU.add), reads=[r_macc, r_mtmp], writes=[r_mT])
            dbg_dump("mT", mT[:], [128, KC, T], BF16, reads=[r_mT])
            sc.barrier(engines=("pe", "act", "dve", "pool"))
            if stop == "M":
                break
            MARKS.append(("O", t, l, PEI[0]))
            cv = Carve()
            xr = cv.get([128, NSL, D])
            r_xr = [Res() for _ in range(NSL)]
            gsl = [cv.get([128, 512]) for _ in range(2)]
            bsl = [cv.get([128, 512]) for _ in range(2)]
            r_gsl = [Res(), Res()]
            r_bsl = [Res(), Res()]
            wple = cv.get([128, 2, D], BF16)
            r_wple = Res()
            pT = cv.get([128, 2, T], BF16)
            pl = cv.get([128, 256])
            r_pT, r_pl = Res(), Res()
            stats = cv.get([128, 8, 6])
            mv = cv.get([128, 8])
            r_stats, r_mv = Res(), Res()
            ssqp = cv.get([128, NSL, 8])
            rse = cv.get([128, NSL, 4])
            r_ssqp, r_rse = Res(), Res()
            sqj = cv.get([128, 512], BF16)
            r_sqj = Res()
            gt = [cv.get([128, 256]) for _ in range(2)]
            et = [cv.get([128, 256]) for _ in range(2)]
            pgs = [cv.get([128, 256]) for _ in range(2)]
            r_gt = [Res(), Res()]
            r_et = [Res(), Res()]
            r_pgs = [Res(), Res()]
            xsrc = x_d if l == 0 else hb_d
            for s in range(NSL):
                sc.dma("pool", xr[:, s, :], xsrc[t0 + s * 128: t0 + (s + 1) * 128, :], reads=([r_hb] if l > 0 else []), writes=[r_xr[s]])
            pg_, po_ = WB_PAGE[WPLE_B]
            sc.dma("pool", wple[:].rearrange("p a b -> p (a b)"), wsc_view(l, WPLE_B, 2 * D), writes=[r_wple])
            for j in range(16):
                wo, r_wo = w_next(("w_out", 256 * j))
                for s in range(NSL):
                    ps, r_p = psum()
                    mm_group(ps[:, 0:256], r_p, [(mT[:, kc, s * 128:(s + 1) * 128], wo[:, kc, :]) for kc in range(KC)], [r_wo, r_mT])
                    sc.op("dve", lambda e: e.scalar_tensor_tensor(out=xr[:, s, j * 256:(j + 1) * 256], in0=xr[:, s, j * 256:(j + 1) * 256], scalar=ALPHA, in1=ps[:, 0:256],
                                                                  op0=ALU.mult, op1=ALU.add), reads=[r_p, r_xr[s]], writes=[r_xr[s]])
            MARKS.append(("LN", t, l, PEI[0]))
            gi = 0
            for s in range(NSL):
                for cb in range(8):
                    sc.op("dve", lambda e: e.bn_stats(out=stats[:, cb, :], in_=xr[:, s, cb * 512:(cb + 1) * 512]), reads=[r_xr[s]], writes=[r_stats])
                sc.op("dve", lambda e: e.bn_aggr(out=mv[:, 0:2], in_=stats[:].rearrange("p a b -> p (a b)")), reads=[r_stats], writes=[r_mv])
                sc.op("act", lambda e: e.activation(out=mv[:, 2:3], in_=mv[:, 1:2], func=AF.Sqrt, bias=epsc[:, 0:1], scale=1.0), reads=[r_mv], writes=[r_mv])
                sc.op("dve", lambda e: e.reciprocal(out=mv[:, 3:4], in_=mv[:, 2:3]), reads=[r_mv], writes=[r_mv])
                sc.op("dve", lambda e: e.tensor_scalar(out=xr[:, s, :], in0=xr[:, s, :], scalar1=mv[:, 0:1], scalar2=mv[:, 3:4], op0=ALU.subtract, op1=ALU.mult),
                      reads=[r_mv, r_xr[s]], writes=[r_xr[s]])
                for cb in range(8):
                    k = gi % 2
                    gi += 1
                    cols = slice(cb * 512, (cb + 1) * 512)
                    sc.dma("pool", gsl[k], lng_d[l][:, cols].partition_broadcast(128), writes=[r_gsl[k]])
                    sc.dma("pool", bsl[k], lnb_d[l][:, cols].partition_broadcast(128), writes=[r_bsl[k]])
                    sc.op("dve", lambda e: e.tensor_tensor(out=xr[:, s, cols], in0=xr[:, s, cols], in1=gsl[k], op=ALU.mult), reads=[r_gsl[k], r_xr[s]], writes=[r_xr[s]])
                    sc.op("dve", lambda e: e.tensor_tensor(out=xr[:, s, cols], in0=xr[:, s, cols], in1=bsl[k], op=ALU.add), reads=[r_bsl[k], r_xr[s]], writes=[r_xr[s]])
                for kc in range(KC):
                    transpose_to(xT[:, kc, s * 128:(s + 1) * 128], r_xT, xr[:, s, kc * 128:(kc + 1) * 128], r_xr[s], 128, F32, evac=("dve" if kc % 2 == 0 else "act"))
                sc.dma("pool", pl, p_d[l, t0 + s * 128: t0 + (s + 1) * 128, :], writes=[r_pl])
                for k2 in range(2):
                    transpose_to(pT[:, k2, s * 128:(s + 1) * 128], r_pT, pl[:, k2 * 128:(k2 + 1) * 128], r_pl, 128, F32)
                sc.op("dve", lambda e: e.memset(ssqp[:, s, :], 0.0), writes=[r_ssqp])
                for cb in range(8):
                    ps, r_p = psum()
                    mm_group(ps[:, 0:512], r_p, [(pT[:, k2, s * 128:(s + 1) * 128], wple[:, k2, cb * 512:(cb + 1) * 512]) for k2 in range(2)], [r_pT, r_wple])
                    sc.op("act", lambda e: e.activation(out=sqj, in_=ps[:, 0:512], func=AF.Square, accum_out=ssqp[:, s, cb:cb + 1]), reads=[r_p], writes=[r_sqj, r_ssqp])
                sc.op("dve", lambda e: e.reduce_sum(out=rse[:, s, 0:1], in_=ssqp[:, s, :], axis=mybir.AxisListType.X), reads=[r_ssqp], writes=[r_rse])
                sc.op("act", lambda e: e.activation(out=rse[:, s, 1:2], in_=rse[:, s, 0:1], func=AF.Sqrt, bias=epsc[:, 0:1], scale=1.0 / D), reads=[r_rse], writes=[r_rse])
                sc.op("dve", lambda e: e.reciprocal(out=rse[:, s, 2:3], in_=rse[:, s, 1:2]), reads=[r_rse], writes=[r_rse])
            MARKS.append(("G", t, l, PEI[0]))
            gi = 0
            for j in range(16):
                wg, r_wg = w_next(("w_ple_gate", 256 * j))
                cols = slice(j * 256, (j + 1) * 256)
                kk = j % 2
                sc.dma("pool", pgs[kk], pleg_d[l][:, cols].partition_broadcast(128), writes=[r_pgs[kk]])
                for s in range(NSL):
                    k = gi % 2
                    gi += 1
                    ps1, r_p1 = psum()
                    mm_group(ps1[:, 0:256], r_p1, [(xT[:, kc, s * 128:(s + 1) * 128], wg[:, kc, :]) for kc in range(KC)], [r_wg, r_xT])
                    ps2, r_p2 = psum()
                    mm_group(ps2[:, 0:256], r_p2, [(pT[:, k2, s * 128:(s + 1) * 128], wple[:, k2, cols]) for k2 in range(2)], [r_pT, r_wple])
                    sc.op("act", lambda e: e.activation(out=gt[k], in_=ps1[:, 0:256], func=AF.Sigmoid), reads=[r_p1], writes=[r_gt[k]])
                    sc.op("dve", lambda e: e.scalar_tensor_tensor(out=et[k], in0=ps2[:, 0:256], scalar=rse[:, s, 2:3], in1=pgs[kk], op0=ALU.mult, op1=ALU.mult),
                          reads=[r_p2, r_rse, r_pgs[kk]], writes=[r_et[k]])
                    sc.op("dve", lambda e: e.tensor_tensor(out=et[k], in0=et[k], in1=gt[k], op=ALU.mult), reads=[r_et[k], r_gt[k]], writes=[r_et[k]])
                    sc.op("dve", lambda e: e.tensor_tensor(out=xr[:, s, cols], in0=xr[:, s, cols], in1=et[k], op=ALU.add), reads=[r_et[k], r_xr[s]], writes=[r_xr[s]])
            for s in range(NSL):
                rows = slice(t0 + s * 128, t0 + (s + 1) * 128)
                if last:
                    sc.dma("pool", out_d[rows, :], xr[:, s, :], reads=[r_xr[s]])
                else:
                    sc.dma("pool", hb_d[rows, :], xr[:, s, :], reads=[r_xr[s]], writes=[r_hb])
                    for kc in range(KC):
                        transpose_to(xT[:, kc, s * 128:(s + 1) * 128], r_xT, xr[:, s, kc * 128:(kc + 1) * 128], r_xr[s], 128, F32, evac=("dve" if kc % 2 == 0 else "act"))
            sc.barrier(engines=("pe", "act", "dve", "pool"))
        if stop is not None:
            break
    sc.barrier()
    return nc, dbg_out


def make_consts():
    c = np.zeros((128, 128 * 3 + 8 * 32 + 8 * 128), np.float32)
    c[:, 0:128] = np.eye(128, dtype=np.float32)
    j = np.arange(128)[:, None]
    i = np.arange(128)[None, :]
    c[:, 128:256] = (j <= i).astype(np.float32)
    c[:, 256:384] = 1.0
    slopes = 2.0 ** (-(np.arange(8) + 1.0))
    pp = np.arange(128, dtype=np.float64)[:, None, None]
    rr = np.arange(32, dtype=np.float64)[None, None, :]
    tab = slopes[None, :, None] * (pp - 128.0 * rr - 64.0)
    c[:, 384:640] = tab.reshape(128, 256).astype(np.float32)
    ki = np.arange(128, dtype=np.float64)[:, None, None]
    qi = np.arange(128, dtype=np.float64)[None, None, :]
    dh = slopes[None, :, None] * (qi - 64.0 - np.abs(qi - ki))
    dh = np.where((ki >= 64) & (qi < 64), -30000.0, dh)
    c[:, 640:1664] = dh.reshape(128, 1024).astype(np.float32)
    return c


def prep_inputs(inp, b, NL, NTOK):
    f = np.float32
    m = {}
    m["x"] = np.ascontiguousarray(inp["x"][b, :NTOK]).astype(f, copy=False)
    m["p"] = np.ascontiguousarray(inp["p"][:NL, b, :NTOK]).astype(f, copy=False)
    for k in ("w_in", "w_branch", "w_out", "w_ple", "w_ple_gate"):
        m[k] = np.ascontiguousarray(inp[k][:NL])
    m["convw"] = np.ascontiguousarray(inp["conv_w"][:NL].reshape(NL, 4, 32, 128).transpose(0, 3, 2, 1))
    m["convb"] = np.ascontiguousarray(inp["conv_b"][:NL].reshape(NL, 32, 128).transpose(0, 2, 1))
    hp = np.zeros((NL, 32, 4), f)
    hp[:, :, 0] = inp["dt_bias"][:NL]
    hp[:, :, 1] = inp["a_log"][:NL]
    m["hp"] = hp
    m["dskip"] = np.ascontiguousarray(inp["d_skip"][:NL].reshape(NL, 1, 32))
    m["ssmg"] = np.ascontiguousarray(inp["ssm_norm_g"][:NL].reshape(NL, 1, 2048))
    m["dlam"] = np.ascontiguousarray(inp["diff_lambda"][:NL].reshape(NL, 1, 256))
    m["dng"] = np.ascontiguousarray(inp["diff_norm_g"][:NL].reshape(NL, 1, 128))
    m["mgb"] = np.ascontiguousarray(inp["mlstm_gate_b"][:NL].transpose(0, 2, 1))
    m["lng"] = np.ascontiguousarray(inp["ln_g"][:NL].reshape(NL, 1, D))
    m["lnb"] = np.ascontiguousarray(inp["ln_b"][:NL].reshape(NL, 1, D))
    m["pleg"] = np.ascontiguousarray(inp["ple_norm_g"][:NL].reshape(NL, 1, D))
    m["cst"] = make_consts()
    return m


def kernel(**inputs):
    NL = 2
    nb = inputs["x"].shape[0]
    nc, _ = build(NL=NL, NTOK=SEQ)
    in_maps = [prep_inputs(inputs, b, NL, SEQ) for b in range(nb)]
    res = run_bass_kernel_spmd(nc, in_maps, core_ids=list(range(nb)))
    out = np.stack([np.asarray(r["out"], dtype=np.float32) for r in res.results], axis=0)
    return out
```

```python
import math
import numpy as np
import concourse.bass as bass
import concourse.mybir as mybir
from concourse.bass_utils import run_bass_kernel_spmd

F32 = mybir.dt.float32
BF16 = mybir.dt.bfloat16
AF = mybir.ActivationFunctionType
ALU = mybir.AluOpType

D = 4096
SEQ = 4096
T = 256
NSL = T // 128
KC = 32
WIN = 27696
C_XBC, C_ZA, C_DT, C_QB, C_KB, C_VB, C_ZB = 0, 4096, 6144, 6176, 7200, 8224, 9248
C_QC, C_KCC, C_VC, C_OC, C_ZC, C_I, C_F, C_G = 10272, 11296, 12320, 13344, 14368, 15392, 15400, 15408
ALPHA = (2.0 * 2) ** 0.25
ARENA_MAX = [0]
PEI = [0]
MARKS = []
EPS = 1e-5


class Res:
    __slots__ = ("w", "r")

    def __init__(self):
        self.w = None
        self.r = {}


class Sched:
    def __init__(self, nc, ndma=20):
        self.nc = nc
        self.eng = dict(pe=nc.tensor, act=nc.scalar, dve=nc.vector, pool=nc.gpsimd, sp=nc.sync)
        self.sems = {}
        self.cnt = {}
        for k in ("pe", "act", "dve", "pool"):
            self.sems[k] = nc.semaphore("s_" + k).__enter__()
            self.cnt[k] = 0
        self.dsem = [nc.semaphore("d%d" % i).__enter__() for i in range(ndma)]
        self.dcnt = [0] * ndma
        self.dnext = 0
        self.seen = {e: {} for e in self.eng}
        self.nops = 0

    def _sem(self, k):
        return self.sems[k] if isinstance(k, str) else self.dsem[k]

    def _wait(self, e, need, keep_one=False):
        sn = self.seen[e]
        todo = [(k, v) for k, v in need.items() if sn.get(k, 0) < v]
        fused = None
        if keep_one and todo:
            k, v = todo.pop()
            fused = (self._sem(k), v)
            sn[k] = v
        for k, v in todo:
            self.eng[e].wait_ge(self._sem(k), v)
            sn[k] = v
        return fused

    def _collect(self, reads, writes):
        need = {}

        def add(t):
            if t is None:
                return
            k, v = t
            if need.get(k, 0) < v:
                need[k] = v

        for r in reads:
            add(r.w)
        for w in writes:
            add(w.w)
            for k, v in w.r.items():
                add((k, v))
        return need

    def _commit(self, tok, reads, writes):
        k, v = tok
        for r in reads:
            if r.r.get(k, 0) < v:
                r.r[k] = v
        for w in writes:
            w.w = tok
            w.r = {}

    def op(self, e, fn, reads=(), writes=()):
        need = self._collect(reads, writes)
        if e == "pe":
            need.pop("pe", None)
        fused = self._wait(e, need, keep_one=True)
        ins = fn(self.eng[e])
        first = ins
        if isinstance(ins, tuple):
            first, ins = ins
        if fused is not None:
            first._wait_ge(fused[0], fused[1])
        self.cnt[e] += 1
        ins.then_inc(self.sems[e], 1)
        tok = (e, self.cnt[e])
        self._commit(tok, reads, writes)
        self.nops += 1
        return tok

    def dma(self, e, out, in_, reads=(), writes=(), **kw):
        i = self.dnext
        self.dnext = (self.dnext + 1) % len(self.dsem)
        need = self._collect(reads, writes)
        if self.dcnt[i]:
            if need.get(i, 0) < self.dcnt[i]:
                need[i] = self.dcnt[i]
        fused = self._wait(e, need, keep_one=True)
        ins = self.eng[e].dma_start(out=out, in_=in_, **kw)
        if fused is not None:
            ins._wait_ge(fused[0], fused[1])
        ins.then_inc(self.dsem[i], 16)
        self.dcnt[i] += 16
        tok = (i, self.dcnt[i])
        self._commit(tok, reads, writes)
        return tok

    def barrier(self, engines=("pe", "act", "dve", "pool", "sp"), skip_sems=()):
        need = {k: v for k, v in self.cnt.items() if v}
        for i, v in enumerate(self.dcnt):
            if v:
                need[i] = v
        for e in engines:
            self._wait(e, dict(need))


def _wblocks():
    b = []
    for i in range(16):
        b.append(("w_in", C_XBC + 256 * i, 256, 32))
    b.append(("w_in", C_DT, 32, 32))
    for i in range(8):
        b.append(("w_in", C_ZA + 256 * i, 256, 32))
    for c0 in (C_QB, C_KB, C_VB, C_ZB):
        for i in range(4):
            b.append(("w_in", c0 + 256 * i, 256, 32))
    for c0 in (C_QC, C_KCC, C_VC, C_OC, C_ZC):
        for i in range(4):
            b.append(("w_in", c0 + 256 * i, 256, 32))
    b.append(("w_in", C_I, 16, 32))
    for j in range(16):
        b.append(("w_branch", 256 * j, 256, 32))
        for br in range(3):
            b.append(("w_in", C_G + br * 4096 + 256 * j, 256, 32))
    for j in range(16):
        b.append(("w_out", 256 * j, 256, 32))
    b.append(("w_ple", 0, 4096, 2))
    for j in range(16):
        b.append(("w_ple_gate", 256 * j, 256, 32))
    return b


WBLOCKS = _wblocks()
WB_ELEMS = [128 * kc * n for (_, _, n, kc) in WBLOCKS]
WB_OFF = np.concatenate([[0], np.cumsum(WB_ELEMS)]).astype(np.int64)
WL_ELEMS = int(WB_OFF[-1])
PAGE_ELEMS = 48 * 1024 * 1024
WB_PAGE = []
_pg, _po = 0, 0
for _n in WB_ELEMS:
    if _po + _n > PAGE_ELEMS:
        _pg += 1
        _po = 0
    WB_PAGE.append((_pg, _po))
    _po += _n
NPAGES = _pg + 1


def build(NL=2, NTOK=SEQ, dbg=None, stop=None):
    NT = NTOK // T
    nc = bass.Bass("TRN2", target_bir_lowering=False)
    sc = Sched(nc)
    E = sc.eng
    dbg_out = {}

    def din(name, shape, dt=F32):
        return nc.dram_tensor(name, list(shape), dt, kind="ExternalInput").ap()

    x_d = din("x", [NTOK, D])
    p_d = din("p", [NL, NTOK, 256])
    wmat = dict(w_in=din("w_in", [NL, D, WIN]), w_branch=din("w_branch", [NL, D, D]), w_out=din("w_out", [NL, D, D]),
                w_ple=din("w_ple", [NL, 256, D]), w_ple_gate=din("w_ple_gate", [NL, D, D]))
    convw_d = din("convw", [NL, 128, 32, 4])
    convb_d = din("convb", [NL, 128, 32])
    hp_d = din("hp", [NL, 32, 4])
    dskip_d = din("dskip", [NL, 1, 32])
    ssmg_d = din("ssmg", [NL, 128, 16])
    dlam_d = din("dlam", [NL, 1, 256])
    dng_d = din("dng", [NL, 1, 128])
    mgb_d = din("mgb", [NL, 8, 2])
    lng_d = din("lng", [NL, 1, D])
    lnb_d = din("lnb", [NL, 1, D])
    pleg_d = din("pleg", [NL, 1, D])
    cst_d = din("cst", [128, 128 * 3 + 8 * 32 + 8 * 128])
    out_d = nc.dram_tensor("out", [NTOK, D], F32, kind="ExternalOutput").ap()
    wsc_pages = [[nc.dram_tensor("wsc%d_%d" % (l, g), [PAGE_ELEMS], BF16, kind="Internal").ap() for g in range(NPAGES)] for l in range(NL)]

    def wsc_view(l, b, ne):
        pg, po = WB_PAGE[b]
        return wsc_pages[l][pg][po: po + 128 * ne].rearrange("(p e) -> p e", p=128)
    kc_d = nc.dram_tensor("kcache", [NL, 8, 128, NTOK], BF16, kind="Internal").ap()
    vc_d = nc.dram_tensor("vcache", [NL, 8, NTOK, 129], BF16, kind="Internal").ap()
    hb_d = nc.dram_tensor("hbuf", [NTOK, D], F32, kind="Internal").ap()
    r_wsc, r_kc, r_vc, r_hb = Res(), Res(), Res(), Res()

    def dbg_dump(name, ap, shape, dt=F32, reads=()):
        if dbg is None or name not in dbg:
            return
        o = nc.dram_tensor("dbg_" + name, list(shape), dt, kind="ExternalOutput").ap()
        dbg_out[name] = sc.dma("pool", o, ap, reads=reads)

    def sb(name, shape, dt=F32):
        return nc.sbuf_tensor(name, list(shape), dt).__enter__()

    st_cm = [nc.sbuf_tensor("pst%d" % i, [128, 8192], F32) for i in range(2)]
    sb_cm = [nc.sbuf_tensor("psb%d" % i, [128, 8192], BF16) for i in range(2)]
    stg = [c.__enter__() for c in st_cm]
    sbg = [c.__enter__() for c in sb_cm]
    r_stg = [Res(), Res()]
    r_sbg = [Res(), Res()]
    bi = 0
    for l in range(NL):
        for bidx, (mat, c0, n, kc) in enumerate(WBLOCKS):
            s = bi % 2
            W = wmat[mat][l]
            src = W[:, c0:c0 + n].rearrange("(k p) c -> p k c", p=128)
            stv = stg[s][:, 0:kc * n].rearrange("p (k c) -> p k c", k=kc)
            step = 8 if kc >= 8 else kc
            if n > 1024:
                step = 1
            for k0 in range(0, kc, step):
                if n > 1024:
                    for c1 in range(0, n, 1024):
                        sc.dma("sp", stv[:, k0:k0 + step, c1:c1 + 1024], src[:, k0:k0 + step, c1:c1 + 1024], writes=[r_stg[s]])
                else:
                    sc.dma("sp", stv[:, k0:k0 + step, :], src[:, k0:k0 + step, :], writes=[r_stg[s]])
            ne = kc * n
            if bi % 2 == 0:
                sc.op("dve", lambda e: e.tensor_copy(out=sbg[s][:, 0:ne], in_=stg[s][:, 0:ne]), reads=[r_stg[s]], writes=[r_sbg[s]])
            else:
                sc.op("act", lambda e: e.copy(out=sbg[s][:, 0:ne], in_=stg[s][:, 0:ne]), reads=[r_stg[s]], writes=[r_sbg[s]])
            dst = wsc_view(l, bidx, ne)
            sc.dma("pool", dst, sbg[s][:, 0:ne], reads=[r_sbg[s]], writes=[r_wsc])
            bi += 1
    sc.barrier()
    for c in reversed(sb_cm):
        c.__exit__(None, None, None)
    for c in reversed(st_cm):
        c.__exit__(None, None, None)

    cst = sb("cst_sb", [128, 128 * 3 + 8 * 32 + 8 * 128])
    ident = cst[:, 0:128]
    trimask = cst[:, 128:256]
    ones = cst[:, 256:384]
    tab = cst[:, 384:384 + 256].rearrange("p (h r) -> p h r", h=8)
    Dh = cst[:, 640:640 + 1024].rearrange("p (h q) -> p h q", h=8)
    identb = sb("identb", [128, 128], BF16)
    epsc = sb("epsc", [128, 1])
    zero32 = sb("zero32", [32, 1])
    xT = sb("xT", [128, KC, T], BF16)
    NWB = 3
    wbuf = [sb("wbuf%d" % i, [128, 8192], BF16) for i in range(NWB)]
    r_wbuf = [Res() for _ in range(NWB)]
    oaT = sb("oaT", [128, 16, T], BF16)
    obT = sb("obT", [128, 8, T], BF16)
    ocT = sb("ocT", [128, 8, T], BF16)
    mT = sb("mT", [128, KC, T], BF16)
    r_xT, r_oaT, r_obT, r_ocT, r_mT = Res(), Res(), Res(), Res(), Res()
    Sst = [sb("Sst%d" % l, [128, 2048]) for l in range(NL)]
    Sbf = [sb("Sbf%d" % l, [128, 2048], BF16) for l in range(NL)]
    tails = [sb("tails%d" % l, [128, 32, 3]) for l in range(NL)]
    Cst = [sb("Cst%d" % l, [128, 8, 129]) for l in range(NL)]
    Cbf = [sb("Cbf%d" % l, [128, 8, 129], BF16) for l in range(NL)]
    FM8 = [sb("fm8_%d" % l, [8, 2]) for l in range(NL)]
    r_S = [Res() for _ in range(NL)]
    r_Sbf = [Res() for _ in range(NL)]
    r_tails = [Res() for _ in range(NL)]
    r_C = [Res() for _ in range(NL)]
    r_Cbf = [Res() for _ in range(NL)]
    r_FM8 = [Res() for _ in range(NL)]
    convw = [sb("convw%d" % l, [128, 32, 4]) for l in range(NL)]
    convb = [sb("convb%d" % l, [128, 32]) for l in range(NL)]
    hp = [sb("hp%d" % l, [32, 4]) for l in range(NL)]
    nega = [sb("nega%d" % l, [32, 1]) for l in range(NL)]
    dskB = [sb("dskB%d" % l, [128, 32]) for l in range(NL)]
    ssmgF = [sb("ssmgF%d" % l, [128, 16]) for l in range(NL)]
    dngB = [sb("dngB%d" % l, [128, 128]) for l in range(NL)]
    lamt = [sb("lam%d" % l, [128, 4]) for l in range(NL)]
    dlt = sb("dlt", [128, 256])
    mgb = [sb("mgb%d" % l, [8, 2]) for l in range(NL)]
    nfb = [sb("nfb%d" % l, [8, 1]) for l in range(NL)]
    ARENA = 60 * 1024
    arena = sb("arena", [128, ARENA // 2], BF16)
    psums = [nc.psum_tensor("ps%d" % i, [128, 512], F32).__enter__() for i in range(8)]
    r_ps = [Res() for _ in range(8)]
    psn = [0]

    psrot = [8]

    def psum():
        i = psn[0] % psrot[0]
        psn[0] += 1
        return psums[i], r_ps[i]

    class Carve:
        def __init__(self):
            self.off = 0

        def get(self, shape, dt=F32):
            n = int(np.prod(shape[1:]))
            nb = n * (4 if dt == F32 else 2)
            nb_al = (nb + 63) // 64 * 64
            assert self.off + nb_al <= ARENA, ("arena overflow", self.off, nb_al)
            ARENA_MAX[0] = max(ARENA_MAX[0], self.off + nb_al)
            v = arena[0:shape[0], self.off // 2: self.off // 2 + nb // 2]
            self.off += nb_al
            if dt == F32:
                v = v.bitcast(F32)
            if len(shape) == 3:
                v = v.rearrange("p (a b) -> p a b", a=shape[1])
            return v

    r_c = Res()
    sc.dma("pool", cst[:], cst_d, writes=[r_c])
    sc.op("dve", lambda e: e.tensor_copy(out=identb[:], in_=ident), reads=[r_c], writes=[r_c])
    sc.op("dve", lambda e: e.memset(epsc[:], EPS), writes=[r_c])
    sc.op("dve", lambda e: e.memset(zero32[:], 0.0), writes=[r_c])
    for l in range(NL):
        sc.dma("pool", convw[l][:], convw_d[l], writes=[r_c])
        sc.dma("pool", convb[l][:], convb_d[l], writes=[r_c])
        sc.dma("pool", hp[l][:], hp_d[l], writes=[r_c])
        sc.dma("pool", dskB[l][:], dskip_d[l].partition_broadcast(128), writes=[r_c])
        sc.dma("pool", ssmgF[l][:], ssmg_d[l], writes=[r_c])
        sc.dma("pool", dngB[l][:], dng_d[l].partition_broadcast(128), writes=[r_c])
        sc.dma("pool", mgb[l][:], mgb_d[l], writes=[r_c])
        sc.dma("pool", dlt[:], dlam_d[l].partition_broadcast(128), writes=[r_c])
        sc.op("act", lambda e: e.activation(out=nega[l][:], in_=hp[l][:, 1:2], func=AF.Exp), reads=[r_c], writes=[r_c])
        sc.op("dve", lambda e: e.tensor_scalar(out=nega[l][:], in0=nega[l][:], scalar1=-1.0, scalar2=None, op0=ALU.mult), reads=[r_c], writes=[r_c])
        sc.op("dve", lambda e: e.tensor_scalar(out=nfb[l][:], in0=mgb[l][:, 1:2], scalar1=-1.0, scalar2=None, op0=ALU.mult), reads=[r_c], writes=[r_c])
        lam_init = 0.8 - 0.6 * math.exp(-0.3 * l)
        sc.op("dve", lambda e: e.tensor_tensor(out=dlt[:, 0:64], in0=dlt[:, 0:64], in1=dlt[:, 64:128], op=ALU.mult), reads=[r_c], writes=[r_c])
        sc.op("dve", lambda e: e.tensor_tensor(out=dlt[:, 128:192], in0=dlt[:, 128:192], in1=dlt[:, 192:256], op=ALU.mult), reads=[r_c], writes=[r_c])
        sc.op("dve", lambda e: e.reduce_sum(out=lamt[l][:, 2:3], in_=dlt[:, 0:64], axis=mybir.AxisListType.X), reads=[r_c], writes=[r_c])
        sc.op("dve", lambda e: e.reduce_sum(out=lamt[l][:, 3:4], in_=dlt[:, 128:192], axis=mybir.AxisListType.X), reads=[r_c], writes=[r_c])
        sc.op("act", lambda e: e.activation(out=lamt[l][:, 2:4], in_=lamt[l][:, 2:4], func=AF.Exp), reads=[r_c], writes=[r_c])
        sc.op("dve", lambda e: e.tensor_tensor(out=lamt[l][:, 0:1], in0=lamt[l][:, 2:3], in1=lamt[l][:, 3:4], op=ALU.subtract), reads=[r_c], writes=[r_c])
        sc.op("dve", lambda e: e.tensor_scalar(out=lamt[l][:, 0:1], in0=lamt[l][:, 0:1], scalar1=lam_init, scalar2=None, op0=ALU.add), reads=[r_c], writes=[r_c])
        sc.op("dve", lambda e: e.tensor_scalar(out=lamt[l][:, 1:2], in0=lamt[l][:, 0:1], scalar1=-1.0, scalar2=None, op0=ALU.mult), reads=[r_c], writes=[r_c])
        sc.op("dve", lambda e: e.memset(Sst[l][:], 0.0), writes=[r_S[l]])
        sc.op("dve", lambda e: e.memset(Sbf[l][:], 0.0), writes=[r_Sbf[l]])
        sc.op("dve", lambda e: e.memset(tails[l][:], 0.0), writes=[r_tails[l]])
        sc.op("dve", lambda e: e.memset(Cst[l][:], 0.0), writes=[r_C[l]])
        sc.op("dve", lambda e: e.memset(Cbf[l][:], 0.0), writes=[r_Cbf[l]])
        sc.op("dve", lambda e: e.memset(FM8[l][:], 0.0), writes=[r_FM8[l]])
    sc.barrier()

    WPLE_B = WBLOCKS.index(("w_ple", 0, 4096, 2))
    wplan = [(l, b) for _t in range(NT) for l in range(NL) for b in range(len(WBLOCKS)) if b != WPLE_B]
    wstate = dict(issued=0, used=0)

    def w_issue():
        i = wstate["issued"]
        if i >= len(wplan):
            return
        l, b = wplan[i]
        _, _, n, kc = WBLOCKS[b]
        ne = kc * n
        s = i % NWB
        src = wsc_view(l, b, ne)
        sc.dma("sp", wbuf[s][:, 0:ne], src, writes=[r_wbuf[s]])
        wstate["issued"] = i + 1

    def w_next(expect):
        i = wstate["used"]
        l, b = wplan[i]
        mat, c0, n, kc = WBLOCKS[b]
        assert (mat, c0) == expect, (WBLOCKS[b], expect)
        while wstate["issued"] <= min(i + NWB - 1, len(wplan) - 1):
            w_issue()
        wstate["used"] = i + 1
        s = i % NWB
        return wbuf[s][:, 0:kc * n].rearrange("p (k c) -> p k c", k=kc), r_wbuf[s]

    def mm_group(ps_ap, r_p, pairs, reads):
        n = len(pairs)
        PEI[0] += n

        def fn(e):
            ins = None
            first = None
            for i, (a, b) in enumerate(pairs):
                ins = e.matmul(ps_ap, lhsT=a, rhs=b, start=(i == 0), stop=(i == n - 1))
                if first is None:
                    first = ins
            return (first, ins)
        return sc.op("pe", fn, reads=reads, writes=[r_p])

    def transpose_to(dst_ap, r_dst, src_ap, r_src, kpart, dt, evac="dve", scale_ap=None):
        ps, r_p = psum()
        PEI[0] += 1
        nfree = src_ap.shape[-1]
        if dt == BF16:
            pv = ps[:].bitcast(BF16)[0:nfree, 0:kpart]
            idn = identb[0:kpart, 0:kpart]
        else:
            pv = ps[0:nfree, 0:kpart]
            idn = ident[0:kpart, 0:kpart]
        sc.op("pe", lambda e: e.transpose(out=pv, in_=src_ap, identity=idn), reads=[r_src], writes=[r_p])
        if scale_ap is not None:
            sc.op("dve", lambda e: e.tensor_scalar(out=dst_ap, in0=pv, scalar1=scale_ap, scalar2=None, op0=ALU.mult), reads=[r_p], writes=[r_dst])
        elif evac == "dve":
            sc.op("dve", lambda e: e.tensor_copy(out=dst_ap, in_=pv), reads=[r_p], writes=[r_dst])
        else:
            sc.op("act", lambda e: e.copy(out=dst_ap, in_=pv), reads=[r_p], writes=[r_dst])

    def dla_chunk(cv, nh, hpg, vw, bT, cT_, r_bc, b_tm, val_tm, r_val, rFM, cFM, r_fm, scT, r_scT, state, state_bf,
                  r_state, r_state_bf, decB, r_dec, out_cb):
        ng = nh // hpg
        gw = hpg * vw
        GTm = cv.get([128, 128])
        r_GTm = Res()
        xin = cv.get([128, hpg, vw], BF16)
        xcs = cv.get([128, hpg, vw], BF16)
        r_xin, r_xcs = Res(), Res()
        rh = [cv.get([nh, 128]) for _ in range(2)]
        argb = [cv.get([128, 128]) for _ in range(2)]
        PTb = [cv.get([128, 128], BF16) for _ in range(2)]
        r_rh = [Res(), Res()]
        r_arg = [Res(), Res()]
        r_PT = [Res(), Res()]
        t4 = cv.get([128, hpg, vw])
        r_t4 = Res()
        hi = 0
        for g in range(ng):
            hs = slice(g * hpg, (g + 1) * hpg)
            ps1, r_p1 = psum()
            mm_group(ps1[:, 0:128], r_p1, [(bT(g), cT_(g))], [r_bc])
            sc.op("dve", lambda e: e.tensor_tensor(out=GTm, in0=ps1[:, 0:128], in1=trimask, op=ALU.mult), reads=[r_p1], writes=[r_GTm])
            sc.op("dve", lambda e: e.tensor_tensor(out=xin, in0=val_tm[:, hs, :], in1=scT[:, 1, hs].unsqueeze(2).to_broadcast([128, hpg, vw]), op=ALU.mult),
                  reads=[r_val, r_scT], writes=[r_xin])
            sc.op("dve", lambda e: e.tensor_tensor(out=xcs, in0=val_tm[:, hs, :], in1=scT[:, 3, hs].unsqueeze(2).to_broadcast([128, hpg, vw]), op=ALU.mult),
                  reads=[r_val, r_scT], writes=[r_xcs])
            psy, r_py = psum()
            psyv = psy[:, 0:2 * gw]
            for hh in range(hpg):
                h = g * hpg + hh
                k = hi % 2
                hi += 1
                sc.op("dve", lambda e: e.tensor_scalar(out=rh[k], in0=rFM, scalar1=ident[0:nh, h:h + 1], scalar2=None, op0=ALU.mult),
                      reads=[r_fm], writes=[r_rh[k]])
                ps2, r_p2 = psum()
                mm_group(ps2[:, 0:128], r_p2, [(ones[0:nh, :], rh[k])], [r_rh[k]])
                sc.op("dve", lambda e: e.tensor_scalar(out=argb[k], in0=ps2[:, 0:128], scalar1=scT[:, 0, h:h + 1], scalar2=0.0, op0=ALU.subtract, op1=ALU.min),
                      reads=[r_p2, r_scT], writes=[r_arg[k]])
                sc.op("act", lambda e: e.activation(out=argb[k], in_=argb[k], func=AF.Exp), reads=[r_arg[k]], writes=[r_arg[k]])
                sc.op("dve", lambda e: e.tensor_tensor(out=PTb[k], in0=argb[k], in1=GTm, op=ALU.mult), reads=[r_arg[k], r_GTm], writes=[r_PT[k]])
                mm_group(psy[:, hh * vw:(hh + 1) * vw], r_py, [(PTb[k], xin[:, hh, :])], [r_PT[k], r_xin])
            mm_group(psy[:, gw:2 * gw], r_py, [(cT_(g), state_bf[:, hs, :])], [r_bc, r_state_bf])
            out_cb(g, psy, r_py)
            ps3, r_p3 = psum()
            mm_group(ps3[:, 0:gw], r_p3, [(b_tm(g), xcs[:])], [r_bc, r_xcs])
            sc.op("dve", lambda e: e.tensor_tensor(out=t4, in0=state[:, hs, :], in1=decB[:, hs].unsqueeze(2).to_broadcast([128, hpg, vw]), op=ALU.mult),
                  reads=[r_state, r_dec], writes=[r_t4])
            sc.op("dve", lambda e: e.tensor_tensor(out=state[:, hs, :], in0=t4, in1=ps3[:, 0:gw].rearrange("p (a b) -> p a b", a=hpg), op=ALU.add),
                  reads=[r_t4, r_p3], writes=[r_state])
            sc.op("act", lambda e: e.copy(out=state_bf[:, hs, :], in_=state[:, hs, :]), reads=[r_state], writes=[r_state_bf])

    def fm_terms(cv, nh, rFM_t, cFM_t, extra_t, r_fm, r_prev, rprev_ap, nchunks):
        outs = []
        nrp = cv.get([nh, 1])
        rs = cv.get([nh, 128])
        cs = cv.get([nh, 128])
        dec = cv.get([nh, 1])
        dg = cv.get([nh, nh])
        r_t = Res()
        for c in range(nchunks):
            ch = slice(c * 128, (c + 1) * 128)
            prev = rprev_ap if c == 0 else rFM_t[:, c * 128 - 1:c * 128]
            rend = rFM_t[:, (c + 1) * 128 - 1:(c + 1) * 128]
            scT = cv.get([128, 4, nh])
            decB = cv.get([128, nh])
            r_scT, r_dec = Res(), Res()
            sc.op("dve", lambda e: e.tensor_scalar(out=nrp, in0=prev, scalar1=-1.0, scalar2=None, op0=ALU.mult), reads=[r_fm, r_prev], writes=[r_t])
            sc.op("act", lambda e: e.activation(out=rs, in_=rFM_t[:, ch], func=AF.Exp, bias=nrp, scale=1.0), reads=[r_fm, r_t], writes=[r_t])
            sc.op("act", lambda e: e.activation(out=cs, in_=cFM_t[:, ch], func=AF.Exp, bias=rend, scale=-1.0), reads=[r_fm, r_t], writes=[r_t])
            if extra_t is not None:
                sc.op("dve", lambda e: e.tensor_tensor(out=cs, in0=cs, in1=extra_t[:, ch], op=ALU.mult), reads=[r_fm, r_t], writes=[r_t])
            sc.op("act", lambda e: e.activation(out=dec, in_=rend, func=AF.Exp, bias=nrp, scale=1.0), reads=[r_fm, r_t], writes=[r_t])
            sc.op("dve", lambda e: e.tensor_scalar(out=dg, in0=ident[0:nh, 0:nh], scalar1=dec, scalar2=None, op0=ALU.mult), reads=[r_t], writes=[r_t])
            ps, r_p = psum()
            mm_group(ps[:, 0:nh], r_p, [(ones[0:nh, :], dg)], [r_t])
            sc.op("dve", lambda e: e.tensor_copy(out=decB, in_=ps[:, 0:nh]), reads=[r_p], writes=[r_dec])
            ps, r_p = psum()
            srcs = [cFM_t[:, ch], (extra_t[:, ch] if extra_t is not None else None), rs, cs]
            PEI[0] += sum(1 for s_ in srcs if s_ is not None)

            def fn(e):
                ins = None
                first = None
                for i, s_ in enumerate(srcs):
                    if s_ is None:
                        continue
                    ins = e.transpose(out=ps[:, i * nh:(i + 1) * nh], in_=s_, identity=ident[0:nh, 0:nh])
                    if first is None:
                        first = ins
                return (first, ins)
            sc.op("pe", fn, reads=[r_fm, r_t], writes=[r_p])
            if extra_t is None:
                sc.op("dve", lambda e: e.memset(scT[:, 1, :], 1.0), writes=[r_scT])
                sc.op("dve", lambda e: e.tensor_copy(out=scT[:, 0, :], in_=ps[:, 0:nh]), reads=[r_p], writes=[r_scT])
                sc.op("dve", lambda e: e.tensor_copy(out=scT[:, 2:4, :], in_=ps[:, 2 * nh:4 * nh].rearrange("p (a b) -> p a b", a=2)), reads=[r_p], writes=[r_scT])
            else:
                sc.op("dve", lambda e: e.tensor_copy(out=scT, in_=ps[:, 0:4 * nh].rearrange("p (a b) -> p a b", a=4)), reads=[r_p], writes=[r_scT])
            outs.append((scT, r_scT, decB, r_dec))
        return outs

    for t in range(NT):
        t0 = t * T
        for l in range(NL):
            last = (l == NL - 1)
            lam_init = 0.8 - 0.6 * math.exp(-0.3 * l)
            if l == 0:
                cv = Carve()
                xs = [cv.get([128, D]) for _ in range(2)]
                r_xs = [Res(), Res()]
                for s in range(NSL):
                    k = s % 2
                    sc.dma("pool", xs[k], x_d[t0 + s * 128: t0 + (s + 1) * 128, :], writes=[r_xs[k]])
                    for kc in range(KC):
                        transpose_to(xT[:, kc, s * 128:(s + 1) * 128], r_xT, xs[k][:, kc * 128:(kc + 1) * 128], r_xs[k], 128, F32,
                                     evac=("dve" if kc % 2 == 0 else "act"))
                sc.barrier(engines=("pe", "act", "dve", "pool"))
            MARKS.append(("A", t, l, PEI[0]))
            cv = Carve()
            xh = cv.get([128, NSL, 2048], BF16)
            BF = cv.get([128, 8, T], BF16)
            CF = cv.get([128, 8, T], BF16)
            BT = cv.get([128, NSL, 1024], BF16)
            za = cv.get([128, NSL, 2048], BF16)
            ub = [cv.get([128, T + 3]) for _ in range(2)]
            acc = [cv.get([128, T]) for _ in range(2)]
            xc = [cv.get([128, T], BF16) for _ in range(2)]
            r_xh, r_BC, r_za = Res(), Res(), Res()
            r_ub = [Res(), Res()]
            r_acc = [Res(), Res()]
            r_xc = [Res(), Res()]
            blk = 0
            pendA = []
            for wi in range(16):
                wv, r_w = w_next(("w_in", C_XBC + 256 * wi))
                for half in range(2):
                    k = blk % 2
                    ps, r_p = psum()
                    mm_group(ps[:, 0:T], r_p, [(wv[:, kc, half * 128:(half + 1) * 128], xT[:, kc, :]) for kc in range(KC)], [r_w, r_xT])
                    for f_ in pendA:
                        f_()
                    pendA = []
                    sc.op("act", lambda e: e.copy(out=ub[k][:, 3:3 + T], in_=ps[:, 0:T]), reads=[r_p], writes=[r_ub[k]])
                    sc.op("dve", lambda e: e.tensor_copy(out=ub[k][:, 0:3], in_=tails[l][:, blk, :]), reads=[r_tails[l]], writes=[r_ub[k]])
                    sc.op("dve", lambda e: e.tensor_scalar(out=acc[k], in0=ub[k][:, 0:T], scalar1=convw[l][:, blk, 0:1], scalar2=None, op0=ALU.mult),
                          reads=[r_ub[k]], writes=[r_acc[k]])
                    for j in range(1, 4):
                        sc.op("dve", lambda e: e.scalar_tensor_tensor(out=acc[k], in0=ub[k][:, j:j + T], scalar=convw[l][:, blk, j:j + 1], in1=acc[k],
                                                                      op0=ALU.mult, op1=ALU.add), reads=[r_ub[k], r_acc[k]], writes=[r_acc[k]])
                    sc.op("dve", lambda e: e.tensor_copy(out=tails[l][:, blk, :], in_=ub[k][:, T:T + 3]), reads=[r_ub[k]], writes=[r_tails[l]])
                    if blk < 16:
                        dst, r_dst = xc[k], r_xc[k]
                    elif blk < 24:
                        dst, r_dst = BF[:, blk - 16, :], r_BC
                    else:
                        dst, r_dst = CF[:, blk - 24, :], r_BC
                    sc.op("act", lambda e: e.activation(out=dst, in_=acc[k], func=AF.Silu, bias=convb[l][:, blk:blk + 1], scale=1.0),
                          reads=[r_acc[k]], writes=[r_dst])
                    if blk < 16:
                        for s in range(NSL):
                            pendA.append(lambda s=s, blk=blk, k=k: transpose_to(xh[:, s, blk * 128:(blk + 1) * 128], r_xh, xc[k][:, s * 128:(s + 1) * 128], r_xc[k], 128, BF16,
                                                                                evac=("dve" if s % 2 == 0 else "act")))
                    elif blk < 24:
                        for s in range(NSL):
                            pendA.append(lambda s=s, blk=blk: transpose_to(BT[:, s, (blk - 16) * 128:(blk - 15) * 128], r_BC, BF[:, blk - 16, s * 128:(s + 1) * 128], r_BC, 128, BF16,
                                                                           evac=("dve" if s % 2 == 0 else "act")))
                    blk += 1
            for f_ in pendA:
                f_()
            pendA = []
            MARKS.append(("A_dt", t, l, PEI[0]))
            dtF = cv.get([32, T])
            laF = cv.get([32, T])
            AF_ = cv.get([32, T])
            ones32T = cv.get([32, T])
            r_dtf = Res()
            wv, r_w = w_next(("w_in", C_DT))
            ps, r_p = psum()
            mm_group(ps[0:32, 0:T], r_p, [(wv[:, kc, 0:32], xT[:, kc, :]) for kc in range(KC)], [r_w, r_xT])
            sc.op("act", lambda e: e.activation(out=dtF, in_=ps[0:32, 0:T], func=AF.Exp, bias=hp[l][:, 0:1], scale=1.0), reads=[r_p], writes=[r_dtf])
            sc.op("act", lambda e: e.activation(out=dtF, in_=dtF, func=AF.Ln, bias=ones[0:32, 0:1], scale=1.0), reads=[r_dtf], writes=[r_dtf])
            sc.op("dve", lambda e: e.tensor_scalar(out=laF, in0=dtF, scalar1=nega[l][:, 0:1], scalar2=None, op0=ALU.mult), reads=[r_dtf], writes=[r_dtf])
            sc.op("dve", lambda e: e.memset(ones32T, 1.0), writes=[r_dtf])
            sc.op("dve", lambda e: e.tensor_tensor_scan(out=AF_, data0=ones32T, data1=laF, initial=0.0, op0=ALU.mult, op1=ALU.add), reads=[r_dtf], writes=[r_dtf])
            r_z32 = Res()
            terms = fm_terms(cv, 32, AF_, AF_, dtF, r_dtf, r_z32, zero32[:], NSL)
            for wi in range(8):
                wv, r_w = w_next(("w_in", C_ZA + 256 * wi))
                for s in range(NSL):
                    ps, r_p = psum()
                    mm_group(ps[:, 0:256], r_p, [(xT[:, kc, s * 128:(s + 1) * 128], wv[:, kc, :]) for kc in range(KC)], [r_w, r_xT])
                    sc.op("act", lambda e: e.activation(out=za[:, s, wi * 256:(wi + 1) * 256], in_=ps[:, 0:256], func=AF.Silu), reads=[r_p], writes=[r_za])
            MARKS.append(("A_ssd", t, l, PEI[0]))
            ysb = cv.get([128, 2048])
            r_y = Res()
            t1 = cv.get([128, 4, 64])
            t3 = cv.get([128, 4, 64])
            r_t1, r_t3 = Res(), Res()
            oab = cv.get([128, 2048], BF16)
            r_oab = Res()
            st2 = cv.get([128, 4])
            r_st2 = Res()
            cvmark = cv.off
            for c in range(NSL):
                cv.off = cvmark
                ch = slice(c * 128, (c + 1) * 128)
                scT, r_scT, decB, r_dec = terms[c]
                xh_c = xh[:, c, :].rearrange("p (h v) -> p h v", h=32)

                def out_cb(g, psy, r_py, c=c, xh_c=xh_c, scT=scT, r_scT=r_scT):
                    hs = slice(4 * g, 4 * g + 4)
                    sc.op("dve", lambda e: e.tensor_tensor(out=t1, in0=psy[:, 256:512].rearrange("p (a b) -> p a b", a=4),
                                                          in1=scT[:, 2, hs].unsqueeze(2).to_broadcast([128, 4, 64]), op=ALU.mult),
                          reads=[r_py, r_scT], writes=[r_t1])
                    sc.op("dve", lambda e: e.tensor_tensor(out=t1, in0=t1, in1=psy[:, 0:256].rearrange("p (a b) -> p a b", a=4), op=ALU.add),
                          reads=[r_py, r_t1], writes=[r_t1])
                    sc.op("dve", lambda e: e.tensor_tensor(out=t3, in0=xh_c[:, hs, :], in1=dskB[l][:, hs].unsqueeze(2).to_broadcast([128, 4, 64]), op=ALU.mult),
                          reads=[r_xh], writes=[r_t3])
                    sc.op("dve", lambda e: e.tensor_tensor(out=ysb[:, g * 256:(g + 1) * 256].rearrange("p (a b) -> p a b", a=4), in0=t1, in1=t3, op=ALU.add),
                          reads=[r_t1, r_t3], writes=[r_y])

                dla_chunk(cv, 32, 4, 64,
                          lambda g: BF[:, g, ch], lambda g: CF[:, g, ch], r_BC,
                          lambda g: BT[:, c, g * 128:(g + 1) * 128], xh_c, r_xh,
                          AF_[:, ch], AF_[:, ch], r_dtf, scT, r_scT,
                          Sst[l][:].rearrange("p (h v) -> p h v", h=32), Sbf[l][:].rearrange("p (h v) -> p h v", h=32),
                          r_S[l], r_Sbf[l], decB, r_dec, out_cb)
                sc.op("dve", lambda e: e.tensor_tensor(out=ysb, in0=ysb, in1=za[:, c, :], op=ALU.mult), reads=[r_y, r_za], writes=[r_y])
                sc.op("dve", lambda e: e.memset(st2[:], 0.0), writes=[r_st2])
                sc.op("act", lambda e: e.activation(out=oab, in_=ysb, func=AF.Square, accum_out=st2[:, 0:1]), reads=[r_y], writes=[r_oab, r_st2])
                sc.op("act", lambda e: e.activation(out=st2[:, 1:2], in_=st2[:, 0:1], func=AF.Sqrt, bias=epsc[:, 0:1], scale=1.0 / 2048), reads=[r_st2], writes=[r_st2])
                sc.op("dve", lambda e: e.reciprocal(out=st2[:, 2:3], in_=st2[:, 1:2]), reads=[r_st2], writes=[r_st2])
                sc.op("dve", lambda e: e.tensor_scalar(out=oab, in0=ysb, scalar1=st2[:, 2:3], scalar2=None, op0=ALU.mult),
                      reads=[r_y, r_st2], writes=[r_oab])
                for kc in range(16):
                    transpose_to(oaT[:, kc, ch], r_oaT, oab[:, kc * 128:(kc + 1) * 128], r_oab, 128, BF16, scale_ap=ssmgF[l][:, kc:kc + 1])
            dbg_dump("oaT", oaT[:], [128, 16, T], BF16, reads=[r_oaT])
            sc.barrier(engines=("pe", "act", "dve", "pool"))
            if stop == "A":
                break

            MARKS.append(("B", t, l, PEI[0]))
            cv = Carve()
            qT = cv.get([128, 8, T], BF16)
            kTn = cv.get([128, 8, T], BF16)
            vn = cv.get([128, NSL, 8 * 129], BF16)
            zb = cv.get([128, NSL, 1024], BF16)
            obb = cv.get([128, NSL, 1024], BF16)
            r_q, r_kn, r_vn, r_zb, r_obb = Res(), Res(), Res(), Res(), Res()
            sc.op("dve", lambda e: e.memset(vn[:], 1.0), writes=[r_vn])
            for (c0, dstT, r_d) in ((C_QB, qT, r_q), (C_KB, kTn, r_kn)):
                for wi in range(4):
                    wv, r_w = w_next(("w_in", c0 + 256 * wi))
                    for half in range(2):
                        h = 2 * wi + half
                        ps, r_p = psum()
                        mm_group(ps[:, 0:T], r_p, [(wv[:, kc, half * 128:(half + 1) * 128], xT[:, kc, :]) for kc in range(KC)], [r_w, r_xT])
                        if half == 0:
                            sc.op("act", lambda e: e.copy(out=dstT[:, h, :], in_=ps[:, 0:T]), reads=[r_p], writes=[r_d])
                        else:
                            sc.op("dve", lambda e: e.tensor_copy(out=dstT[:, h, :], in_=ps[:, 0:T]), reads=[r_p], writes=[r_d])
            sc.dma("pool", kc_d[l][:, :, t0:t0 + T].rearrange("h p t -> p h t"), kTn[:], reads=[r_kn], writes=[r_kc])
            for wi in range(4):
                wv, r_w = w_next(("w_in", C_VB + 256 * wi))
                for s in range(NSL):
                    ps, r_p = psum()
                    mm_group(ps[:, 0:256], r_p, [(xT[:, kc, s * 128:(s + 1) * 128], wv[:, kc, :]) for kc in range(KC)], [r_w, r_xT])
                    sc.op("dve", lambda e: e.tensor_copy(out=vn[:, s, :].rearrange("p (h e) -> p h e", h=8)[:, 2 * wi:2 * wi + 2, 0:128],
                                                        in_=ps[:, 0:256].rearrange("p (h e) -> p h e", h=2)), reads=[r_p], writes=[r_vn])
            for s in range(NSL):
                sc.dma("pool", vc_d[l][:, t0 + s * 128: t0 + (s + 1) * 128, :].rearrange("h p e -> p h e"),
                       vn[:, s, :].rearrange("p (h e) -> p h e", h=8), reads=[r_vn], writes=[r_vc])
            for wi in range(4):
                wv, r_w = w_next(("w_in", C_ZB + 256 * wi))
                for s in range(NSL):
                    ps, r_p = psum()
                    mm_group(ps[:, 0:256], r_p, [(xT[:, kc, s * 128:(s + 1) * 128], wv[:, kc, :]) for kc in range(KC)], [r_w, r_xT])
                    sc.op("act", lambda e: e.activation(out=zb[:, s, wi * 256:(wi + 1) * 256], in_=ps[:, 0:256], func=AF.Silu), reads=[r_p], writes=[r_zb])
            MARKS.append(("B_att", t, l, PEI[0]))
            nkb = (t0 + T) // 128
            kbuf = [cv.get([128, NTOK], BF16) for _ in range(2)]
            vbuf = [cv.get([128, NTOK // 128, 129], BF16) for _ in range(2)]
            r_kb = [Res(), Res()]
            r_vb = [Res(), Res()]
            ptb = [cv.get([128, 128], BF16) for _ in range(2)]
            r_pt = [Res(), Res()]
            dtmp = cv.get([128, 128])
            r_dtmp = Res()
            a0 = cv.get([128, 128])
            att = cv.get([128, 128])
            r_a0, r_att = Res(), Res()
            rec = cv.get([128, 8])
            r_rec = Res()
            psrot[0] = 6
            pti = 0
            oi = 0
            for h in range(8):
                k = h % 2
                sc.dma("pool", kbuf[k][:, 0:t0 + T], kc_d[l, h, :, 0:t0 + T], reads=[r_kc], writes=[r_kb[k]])
                sc.dma("pool", vbuf[k][:, 0:nkb, :], vc_d[l, h, 0:t0 + T, :].rearrange("(b p) e -> p b e", p=128), reads=[r_vc], writes=[r_vb[k]])
                for qb in range(NSL):
                    qabs = t0 // 128 + qb
                    psO, r_pO = psums[6 + oi % 2], r_ps[6 + oi % 2]
                    oi += 1
                    for c in range(2):
                        Oc = psO[:, c * 256: c * 256 + 129]
                        for kb in range(qabs + 1):
                            ps, r_p = psum()
                            mm_group(ps[:, 0:128], r_p, [(kbuf[k][c * 64:(c + 1) * 64, kb * 128:(kb + 1) * 128],
                                                          qT[c * 64:(c + 1) * 64, h, qb * 128:(qb + 1) * 128])], [r_kb[k], r_q])
                            pk = pti % 2
                            pti += 1
                            if kb < qabs:
                                r_ = qabs - kb
                                sc.op("act", lambda e: e.activation(out=ptb[pk], in_=ps[:, 0:128], func=AF.Exp, bias=tab[:, h, r_:r_ + 1], scale=0.125),
                                      reads=[r_p], writes=[r_pt[pk]])
                            else:
                                sc.op("dve", lambda e: e.scalar_tensor_tensor(out=dtmp, in0=ps[:, 0:128], scalar=0.125, in1=Dh[:, h, :], op0=ALU.mult, op1=ALU.add),
                                      reads=[r_p], writes=[r_dtmp])
                                sc.op("act", lambda e: e.activation(out=ptb[pk], in_=dtmp, func=AF.Exp), reads=[r_dtmp], writes=[r_pt[pk]])
                            PEI[0] += 1
                            sc.op("pe", lambda e: e.matmul(Oc, lhsT=ptb[pk], rhs=vbuf[k][:, kb, :], start=(kb == 0), stop=(kb == qabs)),
                                  reads=[r_pt[pk], r_vb[k]], writes=[r_pO])
                    sc.op("dve", lambda e: e.reciprocal(out=rec[:, 0:1], in_=psO[:, 128:129]), reads=[r_pO], writes=[r_rec])
                    sc.op("dve", lambda e: e.reciprocal(out=rec[:, 1:2], in_=psO[:, 384:385]), reads=[r_pO], writes=[r_rec])
                    sc.op("dve", lambda e: e.tensor_tensor(out=rec[:, 2:3], in0=rec[:, 1:2], in1=lamt[l][:, 1:2], op=ALU.mult), reads=[r_rec], writes=[r_rec])
                    sc.op("dve", lambda e: e.tensor_scalar(out=a0, in0=psO[:, 0:128], scalar1=rec[:, 0:1], scalar2=None, op0=ALU.mult), reads=[r_pO, r_rec], writes=[r_a0])
                    sc.op("dve", lambda e: e.scalar_tensor_tensor(out=att, in0=psO[:, 256:384], scalar=rec[:, 2:3], in1=a0, op0=ALU.mult, op1=ALU.add),
                          reads=[r_pO, r_rec, r_a0], writes=[r_att])
                    sc.op("dve", lambda e: e.memset(rec[:, 3:4], 0.0), writes=[r_rec])
                    sc.op("act", lambda e: e.activation(out=a0, in_=att, func=AF.Square, accum_out=rec[:, 3:4]), reads=[r_att], writes=[r_a0, r_rec])
                    sc.op("act", lambda e: e.activation(out=rec[:, 4:5], in_=rec[:, 3:4], func=AF.Sqrt, bias=epsc[:, 0:1], scale=1.0 / 128), reads=[r_rec], writes=[r_rec])
                    sc.op("dve", lambda e: e.reciprocal(out=rec[:, 5:6], in_=rec[:, 4:5]), reads=[r_rec], writes=[r_rec])
                    sc.op("dve", lambda e: e.scalar_tensor_tensor(out=att, in0=att, scalar=rec[:, 5:6], in1=dngB[l][:], op0=ALU.mult, op1=ALU.mult),
                          reads=[r_att, r_rec], writes=[r_att])
                    sc.op("dve", lambda e: e.scalar_tensor_tensor(out=obb[:, qb, h * 128:(h + 1) * 128], in0=att, scalar=(1.0 - lam_init), in1=zb[:, qb, h * 128:(h + 1) * 128],
                                                                  op0=ALU.mult, op1=ALU.mult), reads=[r_att, r_zb], writes=[r_obb])
            psrot[0] = 8
            for s in range(NSL):
                for kc in range(8):
                    transpose_to(obT[:, kc, s * 128:(s + 1) * 128], r_obT, obb[:, s, kc * 128:(kc + 1) * 128], r_obb, 128, BF16, evac=("dve" if kc % 2 == 0 else "act"))
            dbg_dump("obT", obT[:], [128, 8, T], BF16, reads=[r_obT])
            sc.barrier(engines=("pe", "act", "dve", "pool"))
            if stop == "B":
                break
            MARKS.append(("C", t, l, PEI[0]))
            cv = Carve()
            qcT = cv.get([128, 8, T], BF16)
            kcT = cv.get([128, 8, T], BF16)
            kcM = cv.get([128, NSL, 1024], BF16)
            vcm = cv.get([128, NSL, 8 * 129], BF16)
            ocm = cv.get([128, NSL, 1024], BF16)
            zcm = cv.get([128, NSL, 1024], BF16)
            ocb = cv.get([128, 1024], BF16)
            r_qk, r_kcM, r_vcm, r_ocm, r_zcm, r_ocb = Res(), Res(), Res(), Res(), Res(), Res()
            sc.op("dve", lambda e: e.memset(vcm[:], 1.0), writes=[r_vcm])
            pendC = []
            for (c0, dstT, scl) in ((C_QC, qcT, 1.0), (C_KCC, kcT, 128.0 ** -0.5)):
                for wi in range(4):
                    wv, r_w = w_next(("w_in", c0 + 256 * wi))
                    for half in range(2):
                        h = 2 * wi + half
                        ps, r_p = psum()
                        mm_group(ps[:, 0:T], r_p, [(wv[:, kc, half * 128:(half + 1) * 128], xT[:, kc, :]) for kc in range(KC)], [r_w, r_xT])
                        for f_ in pendC:
                            f_()
                        pendC = []
                        sc.op("dve", lambda e: e.tensor_scalar(out=dstT[:, h, :], in0=ps[:, 0:T], scalar1=scl, scalar2=None, op0=ALU.mult), reads=[r_p], writes=[r_qk])
                        if c0 == C_KCC:
                            for s in range(NSL):
                                pendC.append(lambda s=s, h=h: transpose_to(kcM[:, s, h * 128:(h + 1) * 128], r_kcM, kcT[:, h, s * 128:(s + 1) * 128], r_qk, 128, BF16, evac="act"))
            for (c0, kind) in ((C_VC, "v"), (C_OC, "o"), (C_ZC, "z")):
                for wi in range(4):
                    wv, r_w = w_next(("w_in", c0 + 256 * wi))
                    for s in range(NSL):
                        ps, r_p = psum()
                        mm_group(ps[:, 0:256], r_p, [(xT[:, kc, s * 128:(s + 1) * 128], wv[:, kc, :]) for kc in range(KC)], [r_w, r_xT])
                        for f_ in pendC:
                            f_()
                        pendC = []
                        if kind == "v":
                            sc.op("dve", lambda e: e.tensor_copy(out=vcm[:, s, :].rearrange("p (h e) -> p h e", h=8)[:, 2 * wi:2 * wi + 2, 0:128],
                                                                in_=ps[:, 0:256].rearrange("p (h e) -> p h e", h=2)), reads=[r_p], writes=[r_vcm])
                        elif kind == "o":
                            sc.op("act", lambda e: e.activation(out=ocm[:, s, wi * 256:(wi + 1) * 256], in_=ps[:, 0:256], func=AF.Sigmoid), reads=[r_p], writes=[r_ocm])
                        else:
                            sc.op("act", lambda e: e.activation(out=zcm[:, s, wi * 256:(wi + 1) * 256], in_=ps[:, 0:256], func=AF.Silu), reads=[r_p], writes=[r_zcm])
            MARKS.append(("C_chunks", t, l, PEI[0]))
            iF = cv.get([8, T])
            lf = cv.get([8, T])
            FF = cv.get([8, T])
            gg = cv.get([8, T])
            MM = cv.get([8, T])
            rF = cv.get([8, T])
            cF = cv.get([8, T])
            flF = cv.get([8, T])
            ones8T = cv.get([8, T])
            nMp = cv.get([8, 1])
            flT = cv.get([128, NSL, 8])
            r_g = Res()
            r_nMp = Res()
            r_flT = Res()
            wv, r_w = w_next(("w_in", C_I))
            ps, r_p = psum()
            mm_group(ps[0:8, 0:T], r_p, [(wv[:, kc, 0:8], xT[:, kc, :]) for kc in range(KC)], [r_w, r_xT])
            sc.op("act", lambda e: e.activation(out=iF, in_=ps[0:8, 0:T], func=AF.Identity, bias=mgb[l][:, 0:1], scale=1.0), reads=[r_p], writes=[r_g])
            ps, r_p = psum()
            mm_group(ps[0:8, 0:T], r_p, [(wv[:, kc, 8:16], xT[:, kc, :]) for kc in range(KC)], [r_w, r_xT])
            sc.op("act", lambda e: e.activation(out=lf, in_=ps[0:8, 0:T], func=AF.Exp, bias=nfb[l][:, 0:1], scale=-1.0), reads=[r_p], writes=[r_g])
            sc.op("act", lambda e: e.activation(out=lf, in_=lf, func=AF.Ln, bias=ones[0:8, 0:1], scale=1.0), reads=[r_g], writes=[r_g])
            sc.op("dve", lambda e: e.tensor_scalar(out=lf, in0=lf, scalar1=-1.0, scalar2=None, op0=ALU.mult), reads=[r_g], writes=[r_g])
            sc.op("dve", lambda e: e.memset(ones8T, 1.0), writes=[r_g])
            sc.op("dve", lambda e: e.tensor_tensor_scan(out=FF, data0=ones8T, data1=lf, initial=FM8[l][:, 0:1], op0=ALU.mult, op1=ALU.add), reads=[r_g, r_FM8[l]], writes=[r_g])
            sc.op("dve", lambda e: e.tensor_tensor(out=gg, in0=iF, in1=FF, op=ALU.subtract), reads=[r_g], writes=[r_g])
            sc.op("dve", lambda e: e.tensor_tensor_scan(out=MM, data0=gg, data1=gg, initial=FM8[l][:, 1:2], op0=ALU.max, op1=ALU.max), reads=[r_g, r_FM8[l]], writes=[r_g])
            sc.op("dve", lambda e: e.tensor_scalar(out=nMp, in0=FM8[l][:, 1:2], scalar1=-1.0, scalar2=None, op0=ALU.mult), reads=[r_FM8[l]], writes=[r_nMp])
            sc.op("dve", lambda e: e.tensor_copy(out=FM8[l][:, 0:1], in_=FF[:, T - 1:T]), reads=[r_g, r_nMp], writes=[r_FM8[l]])
            sc.op("dve", lambda e: e.tensor_copy(out=FM8[l][:, 1:2], in_=MM[:, T - 1:T]), reads=[r_g, r_nMp], writes=[r_FM8[l]])
            sc.op("dve", lambda e: e.tensor_scalar(out=rF, in0=MM, scalar1=-1.0, scalar2=None, op0=ALU.mult), reads=[r_g], writes=[r_g])
            sc.op("dve", lambda e: e.tensor_scalar(out=cF, in0=gg, scalar1=-1.0, scalar2=None, op0=ALU.mult), reads=[r_g], writes=[r_g])
            sc.op("dve", lambda e: e.tensor_tensor(out=flF, in0=FF, in1=MM, op=ALU.add), reads=[r_g], writes=[r_g])
            sc.op("act", lambda e: e.activation(out=flF, in_=flF, func=AF.Exp, scale=-1.0), reads=[r_g], writes=[r_g])
            for s in range(NSL):
                transpose_to(flT[:, s, :], r_flT, flF[:, s * 128:(s + 1) * 128], r_g, 8, F32)
            termsC = fm_terms(cv, 8, rF, cF, None, r_g, r_nMp, nMp, NSL)
            t1c = cv.get([128, 129])
            r_t1c = Res()
            dn = cv.get([128, 4])
            r_dn = Res()
            hc = cv.get([128, 128])
            r_hc = Res()
            cvmark = cv.off
            for c in range(NSL):
                cv.off = cvmark
                ch = slice(c * 128, (c + 1) * 128)
                scT, r_scT, decB, r_dec = termsC[c]

                def out_cbc(g, psy, r_py, c=c, scT=scT, r_scT=r_scT):
                    h = g
                    sc.op("dve", lambda e: e.tensor_scalar(out=t1c, in0=psy[:, 129:258], scalar1=scT[:, 2, h:h + 1], scalar2=None, op0=ALU.mult),
                          reads=[r_py, r_scT], writes=[r_t1c])
                    sc.op("dve", lambda e: e.tensor_tensor(out=t1c, in0=t1c, in1=psy[:, 0:129], op=ALU.add), reads=[r_py, r_t1c], writes=[r_t1c])
                    sc.op("dve", lambda e: e.tensor_scalar(out=dn[:, 3:4], in0=t1c[:, 128:129], scalar1=-1.0, scalar2=None, op0=ALU.mult), reads=[r_t1c], writes=[r_dn])
                    sc.op("dve", lambda e: e.tensor_tensor(out=dn[:, 0:1], in0=dn[:, 3:4], in1=t1c[:, 128:129], op=ALU.max), reads=[r_t1c, r_dn], writes=[r_dn])
                    sc.op("dve", lambda e: e.tensor_tensor(out=dn[:, 1:2], in0=dn[:, 0:1], in1=flT[:, c, h:h + 1], op=ALU.max), reads=[r_dn, r_flT], writes=[r_dn])
                    sc.op("dve", lambda e: e.reciprocal(out=dn[:, 2:3], in_=dn[:, 1:2]), reads=[r_dn], writes=[r_dn])
                    sc.op("dve", lambda e: e.scalar_tensor_tensor(out=hc, in0=t1c[:, 0:128], scalar=dn[:, 2:3], in1=ocm[:, c, h * 128:(h + 1) * 128], op0=ALU.mult, op1=ALU.mult),
                          reads=[r_t1c, r_dn, r_ocm], writes=[r_hc])
                    sc.op("dve", lambda e: e.tensor_tensor(out=ocb[:, h * 128:(h + 1) * 128], in0=hc, in1=zcm[:, c, h * 128:(h + 1) * 128], op=ALU.mult),
                          reads=[r_hc, r_zcm], writes=[r_ocb])

                dla_chunk(cv, 8, 1, 129,
                          lambda g: kcT[:, g, ch], lambda g: qcT[:, g, ch], r_qk,
                          lambda g: kcM[:, c, g * 128:(g + 1) * 128], vcm[:, c, :].rearrange("p (h e) -> p h e", h=8), r_vcm,
                          rF[:, ch], cF[:, ch], r_g, scT, r_scT,
                          Cst[l][:], Cbf[l][:], r_C[l], r_Cbf[l], decB, r_dec, out_cbc)
                for kc in range(8):
                    transpose_to(ocT[:, kc, ch], r_ocT, ocb[:, kc * 128:(kc + 1) * 128], r_ocb, 128, BF16, evac=("dve" if kc % 2 == 0 else "act"))
            dbg_dump("ocT", ocT[:], [128, 8, T], BF16, reads=[r_ocT])
            sc.barrier(engines=("pe", "act", "dve", "pool"))
            if stop == "C":
                break
            MARKS.append(("M", t, l, PEI[0]))
            cv = Carve()
            sg = [cv.get([128, 2 * T]) for _ in range(2)]
            macc = cv.get([128, 2 * T])
            mtmp = cv.get([128, 2 * T])
            r_sg = [Res(), Res()]
            r_macc, r_mtmp = Res(), Res()
            gi = 0
            for j in range(16):
                wb, r_wb = w_next(("w_branch", 256 * j))
                ybanks = []
                for (k0, k1, actT, r_a) in ((0, 16, oaT, r_oaT), (16, 24, obT, r_obT), (24, 32, ocT, r_ocT)):
                    py, r_py = psum()
                    for half in range(2):
                        mm_group(py[:, half * T:(half + 1) * T], r_py, [(wb[:, kc, half * 128:(half + 1) * 128], actT[:, kc - k0, :]) for kc in range(k0, k1)], [r_wb, r_a])
                    ybanks.append((py, r_py))
                for br in range(3):
                    wg, r_wg = w_next(("w_in", C_G + br * 4096 + 256 * j))
                    pg, r_pg = psum()
                    for half in range(2):
                        mm_group(pg[:, half * T:(half + 1) * T], r_pg, [(wg[:, kc, half * 128:(half + 1) * 128], xT[:, kc, :]) for kc in range(KC)], [r_wg, r_xT])
                    k = gi % 2
                    gi += 1
                    sc.op("act", lambda e: e.activation(out=sg[k], in_=pg[:, 0:2 * T], func=AF.Sigmoid), reads=[r_pg], writes=[r_sg[k]])
                    py, r_py = ybanks[br]
                    if br == 0:
                        sc.op("dve", lambda e: e.tensor_tensor(out=macc, in0=sg[k], in1=py[:, 0:2 * T], op=ALU.mult), reads=[r_sg[k], r_py], writes=[r_macc])
                    else:
                        sc.op("dve", lambda e: e.tensor_tensor(out=mtmp, in0=sg[k], in1=py[:, 0:2 * T], op=ALU.mult), reads=[r_sg[k], r_py], writes=[r_mtmp])
                        if br == 1:
                            sc.op("dve", lambda e: e.tensor_tensor(out=macc, in0=macc, in1=mtmp, op=ALU.add), reads=[r_macc, r_mtmp], writes=[r_macc])
                        else:
                            sc.op("dve", lambda e: e.tensor_tensor(out=mT[:, 2 * j:2 * j + 2, :], in0=macc.rearrange("p (a b) -> p a b", a=2),
                                                                  in1=mtmp.rearrange("p (a b) -> p a b", a=2), op=ALU.add), reads=[r_macc, r_mtmp], writes=[r_mT])
            dbg_dump("mT", mT[:], [128, KC, T], BF16, reads=[r_mT])
            sc.barrier(engines=("pe", "act", "dve", "pool"))
            if stop == "M":
                break
            MARKS.append(("O", t, l, PEI[0]))
            cv = Carve()
            xr = cv.get([128, NSL, D])
            r_xr = [Res() for _ in range(NSL)]
            gsl = [cv.get([128, 256]) for _ in range(2)]
            bsl = [cv.get([128, 256]) for _ in range(2)]
            r_gsl = [Res(), Res()]
            r_bsl = [Res(), Res()]
            wple = cv.get([128, 2, D], BF16)
            r_wple = Res()
            pT = cv.get([128, 2, T], BF16)
            pl = cv.get([128, 256])
            r_pT, r_pl = Res(), Res()
            stats = cv.get([128, 8, 6])
            mv = cv.get([128, 8])
            r_stats, r_mv = Res(), Res()
            ssqp = cv.get([128, NSL, 8])
            rse = cv.get([128, NSL, 4])
            r_ssqp, r_rse = Res(), Res()
            gt = [cv.get([128, 256]) for _ in range(2)]
            et = [cv.get([128, 256]) for _ in range(2)]
            pgs_ = cv.get([128, 256])
            pgs = [pgs_, pgs_]
            r_gt = [Res(), Res()]
            r_et = [Res(), Res()]
            r_pgs_ = Res()
            r_pgs = [r_pgs_, r_pgs_]
            xsrc = x_d if l == 0 else hb_d
            for s in range(NSL):
                sc.dma("pool", xr[:, s, :], xsrc[t0 + s * 128: t0 + (s + 1) * 128, :], reads=([r_hb] if l > 0 else []), writes=[r_xr[s]])
            pg_, po_ = WB_PAGE[WPLE_B]
            sc.dma("pool", wple[:].rearrange("p a b -> p (a b)"), wsc_view(l, WPLE_B, 2 * D), writes=[r_wple])
            for j in range(16):
                wo, r_wo = w_next(("w_out", 256 * j))
                for s in range(NSL):
                    ps, r_p = psum()
                    mm_group(ps[:, 0:256], r_p, [(mT[:, kc, s * 128:(s + 1) * 128], wo[:, kc, :]) for kc in range(KC)], [r_wo, r_mT])
                    sc.op("dve", lambda e: e.scalar_tensor_tensor(out=xr[:, s, j * 256:(j + 1) * 256], in0=xr[:, s, j * 256:(j + 1) * 256], scalar=ALPHA, in1=ps[:, 0:256],
                                                                  op0=ALU.mult, op1=ALU.add), reads=[r_p, r_xr[s]], writes=[r_xr[s]])
            MARKS.append(("LN", t, l, PEI[0]))
            gi = 0
            for s in range(NSL):
                for cb in range(8):
                    sc.op("dve", lambda e: e.bn_stats(out=stats[:, cb, :], in_=xr[:, s, cb * 512:(cb + 1) * 512]), reads=[r_xr[s]], writes=[r_stats])
                sc.op("dve", lambda e: e.bn_aggr(out=mv[:, 0:2], in_=stats[:].rearrange("p a b -> p (a b)")), reads=[r_stats], writes=[r_mv])
                sc.op("act", lambda e: e.activation(out=mv[:, 2:3], in_=mv[:, 1:2], func=AF.Sqrt, bias=epsc[:, 0:1], scale=1.0), reads=[r_mv], writes=[r_mv])
                sc.op("dve", lambda e: e.reciprocal(out=mv[:, 3:4], in_=mv[:, 2:3]), reads=[r_mv], writes=[r_mv])
                sc.op("dve", lambda e: e.tensor_scalar(out=xr[:, s, :], in0=xr[:, s, :], scalar1=mv[:, 0:1], scalar2=mv[:, 3:4], op0=ALU.subtract, op1=ALU.mult),
                      reads=[r_mv, r_xr[s]], writes=[r_xr[s]])
                for cb in range(16):
                    k = gi % 2
                    gi += 1
                    cols = slice(cb * 256, (cb + 1) * 256)
                    sc.dma("pool", gsl[k], lng_d[l][:, cols].partition_broadcast(128), writes=[r_gsl[k]])
                    sc.dma("pool", bsl[k], lnb_d[l][:, cols].partition_broadcast(128), writes=[r_bsl[k]])
                    sc.op("dve", lambda e: e.tensor_tensor(out=xr[:, s, cols], in0=xr[:, s, cols], in1=gsl[k], op=ALU.mult), reads=[r_gsl[k], r_xr[s]], writes=[r_xr[s]])
                    sc.op("dve", lambda e: e.tensor_tensor(out=xr[:, s, cols], in0=xr[:, s, cols], in1=bsl[k], op=ALU.add), reads=[r_bsl[k], r_xr[s]], writes=[r_xr[s]])
                for kc in range(KC):
                    transpose_to(xT[:, kc, s * 128:(s + 1) * 128], r_xT, xr[:, s, kc * 128:(kc + 1) * 128], r_xr[s], 128, F32, evac=("dve" if kc % 2 == 0 else "act"))
                sc.dma("pool", pl, p_d[l, t0 + s * 128: t0 + (s + 1) * 128, :], writes=[r_pl])
                for k2 in range(2):
                    transpose_to(pT[:, k2, s * 128:(s + 1) * 128], r_pT, pl[:, k2 * 128:(k2 + 1) * 128], r_pl, 128, F32)
                sc.op("dve", lambda e: e.memset(ssqp[:, s, :], 0.0), writes=[r_ssqp])
                for cb in range(8):
                    ps, r_p = psum()
                    mm_group(ps[:, 0:512], r_p, [(pT[:, k2, s * 128:(s + 1) * 128], wple[:, k2, cb * 512:(cb + 1) * 512]) for k2 in range(2)], [r_pT, r_wple])
                    sc.op("act", lambda e: e.activation(out=ps[:, 0:512], in_=ps[:, 0:512], func=AF.Square, accum_out=ssqp[:, s, cb:cb + 1]), reads=[r_p], writes=[r_p, r_ssqp])
                sc.op("dve", lambda e: e.reduce_sum(out=rse[:, s, 0:1], in_=ssqp[:, s, :], axis=mybir.AxisListType.X), reads=[r_ssqp], writes=[r_rse])
                sc.op("act", lambda e: e.activation(out=rse[:, s, 1:2], in_=rse[:, s, 0:1], func=AF.Sqrt, bias=epsc[:, 0:1], scale=1.0 / D), reads=[r_rse], writes=[r_rse])
                sc.op("dve", lambda e: e.reciprocal(out=rse[:, s, 2:3], in_=rse[:, s, 1:2]), reads=[r_rse], writes=[r_rse])
            MARKS.append(("G", t, l, PEI[0]))
            gi = 0
            for j in range(16):
                wg, r_wg = w_next(("w_ple_gate", 256 * j))
                cols = slice(j * 256, (j + 1) * 256)
                kk = j % 2
                sc.dma("pool", pgs[kk], pleg_d[l][:, cols].partition_broadcast(128), writes=[r_pgs[kk]])
                for s in range(NSL):
                    k = gi % 2
                    gi += 1
                    ps1, r_p1 = psum()
                    mm_group(ps1[:, 0:256], r_p1, [(xT[:, kc, s * 128:(s + 1) * 128], wg[:, kc, :]) for kc in range(KC)], [r_wg, r_xT])
                    ps2, r_p2 = psum()
                    mm_group(ps2[:, 0:256], r_p2, [(pT[:, k2, s * 128:(s + 1) * 128], wple[:, k2, cols]) for k2 in range(2)], [r_pT, r_wple])
                    sc.op("act", lambda e: e.activation(out=gt[k], in_=ps1[:, 0:256], func=AF.Sigmoid), reads=[r_p1], writes=[r_gt[k]])
                    sc.op("dve", lambda e: e.scalar_tensor_tensor(out=et[k], in0=ps2[:, 0:256], scalar=rse[:, s, 2:3], in1=pgs[kk], op0=ALU.mult, op1=ALU.mult),
                          reads=[r_p2, r_rse, r_pgs[kk]], writes=[r_et[k]])
                    sc.op("dve", lambda e: e.tensor_tensor(out=et[k], in0=et[k], in1=gt[k], op=ALU.mult), reads=[r_et[k], r_gt[k]], writes=[r_et[k]])
                    sc.op("dve", lambda e: e.tensor_tensor(out=xr[:, s, cols], in0=xr[:, s, cols], in1=et[k], op=ALU.add), reads=[r_et[k], r_xr[s]], writes=[r_xr[s]])
            for s in range(NSL):
                rows = slice(t0 + s * 128, t0 + (s + 1) * 128)
                if last:
                    sc.dma("pool", out_d[rows, :], xr[:, s, :], reads=[r_xr[s]])
                else:
                    sc.dma("pool", hb_d[rows, :], xr[:, s, :], reads=[r_xr[s]], writes=[r_hb])
                    for kc in range(KC):
                        transpose_to(xT[:, kc, s * 128:(s + 1) * 128], r_xT, xr[:, s, kc * 128:(kc + 1) * 128], r_xr[s], 128, F32, evac=("dve" if kc % 2 == 0 else "act"))
            sc.barrier(engines=("pe", "act", "dve", "pool"))
        if stop is not None:
            break
    sc.barrier()
    return nc, dbg_out


def make_consts():
    c = np.zeros((128, 128 * 3 + 8 * 32 + 8 * 128), np.float32)
    c[:, 0:128] = np.eye(128, dtype=np.float32)
    j = np.arange(128)[:, None]
    i = np.arange(128)[None, :]
    c[:, 128:256] = (j <= i).astype(np.float32)
    c[:, 256:384] = 1.0
    slopes = 2.0 ** (-(np.arange(8) + 1.0))
    pp = np.arange(128, dtype=np.float64)[:, None, None]
    rr = np.arange(32, dtype=np.float64)[None, None, :]
    tab = slopes[None, :, None] * (pp - 128.0 * rr - 64.0)
    c[:, 384:640] = tab.reshape(128, 256).astype(np.float32)
    ki = np.arange(128, dtype=np.float64)[:, None, None]
    qi = np.arange(128, dtype=np.float64)[None, None, :]
    dh = slopes[None, :, None] * (qi - 64.0 - np.abs(qi - ki))
    dh = np.where((ki >= 64) & (qi < 64), -30000.0, dh)
    c[:, 640:1664] = dh.reshape(128, 1024).astype(np.float32)
    return c


def prep_inputs(inp, b, NL, NTOK):
    f = np.float32
    m = {}
    m["x"] = np.ascontiguousarray(inp["x"][b, :NTOK]).astype(f, copy=False)
    m["p"] = np.ascontiguousarray(inp["p"][:NL, b, :NTOK]).astype(f, copy=False)
    for k in ("w_in", "w_branch", "w_out", "w_ple", "w_ple_gate"):
        m[k] = np.ascontiguousarray(inp[k][:NL])
    m["convw"] = np.ascontiguousarray(inp["conv_w"][:NL].reshape(NL, 4, 32, 128).transpose(0, 3, 2, 1))
    m["convb"] = np.ascontiguousarray(inp["conv_b"][:NL].reshape(NL, 32, 128).transpose(0, 2, 1))
    hp = np.zeros((NL, 32, 4), f)
    hp[:, :, 0] = inp["dt_bias"][:NL]
    hp[:, :, 1] = inp["a_log"][:NL]
    m["hp"] = hp
    m["dskip"] = np.ascontiguousarray(inp["d_skip"][:NL].reshape(NL, 1, 32))
    m["ssmg"] = np.ascontiguousarray(inp["ssm_norm_g"][:NL].reshape(NL, 16, 128).transpose(0, 2, 1))
    m["dlam"] = np.ascontiguousarray(inp["diff_lambda"][:NL].reshape(NL, 1, 256))
    m["dng"] = np.ascontiguousarray(inp["diff_norm_g"][:NL].reshape(NL, 1, 128))
    m["mgb"] = np.ascontiguousarray(inp["mlstm_gate_b"][:NL].transpose(0, 2, 1))
    m["lng"] = np.ascontiguousarray(inp["ln_g"][:NL].reshape(NL, 1, D))
    m["lnb"] = np.ascontiguousarray(inp["ln_b"][:NL].reshape(NL, 1, D))
    m["pleg"] = np.ascontiguousarray(inp["ple_norm_g"][:NL].reshape(NL, 1, D))
    m["cst"] = make_consts()
    return m


def kernel(**inputs):
    NL = 2
    nb = inputs["x"].shape[0]
    nc, _ = build(NL=NL, NTOK=SEQ)
    in_maps = [prep_inputs(inputs, b, NL, SEQ) for b in range(nb)]
    res = run_bass_kernel_spmd(nc, in_maps, core_ids=list(range(nb)))
    out = np.stack([np.asarray(r["out"], dtype=np.float32) for r in res.results], axis=0)
    return out
```

```python
import math
import numpy as np
import concourse.bass as bass
import concourse.mybir as mybir
from concourse.bass_utils import run_bass_kernel_spmd

F32 = mybir.dt.float32
BF16 = mybir.dt.bfloat16
AF = mybir.ActivationFunctionType
ALU = mybir.AluOpType

D = 4096
SEQ = 4096
T = 256
NSL = T // 128
KC = 32
WIN = 27696
C_XBC, C_ZA, C_DT, C_QB, C_KB, C_VB, C_ZB = 0, 4096, 6144, 6176, 7200, 8224, 9248
C_QC, C_KCC, C_VC, C_OC, C_ZC, C_I, C_F, C_G = 10272, 11296, 12320, 13344, 14368, 15392, 15400, 15408
ALPHA = (2.0 * 2) ** 0.25
ARENA_MAX = [0]
PEI = [0]
MARKS = []
EPS = 1e-5


class Res:
    __slots__ = ("w", "r")

    def __init__(self):
        self.w = None
        self.r = {}


class Sched:
    def __init__(self, nc, ndma=20):
        self.nc = nc
        self.eng = dict(pe=nc.tensor, act=nc.scalar, dve=nc.vector, pool=nc.gpsimd, sp=nc.sync)
        self.sems = {}
        self.cnt = {}
        for k in ("pe", "act", "dve", "pool"):
            self.sems[k] = nc.semaphore("s_" + k).__enter__()
            self.cnt[k] = 0
        self.dsem = [nc.semaphore("d%d" % i).__enter__() for i in range(ndma)]
        self.dcnt = [0] * ndma
        self.dnext = 0
        self.seen = {e: {} for e in self.eng}
        self.nops = 0

    def _sem(self, k):
        return self.sems[k] if isinstance(k, str) else self.dsem[k]

    def _wait(self, e, need, keep_one=False):
        sn = self.seen[e]
        todo = [(k, v) for k, v in need.items() if sn.get(k, 0) < v]
        fused = None
        if keep_one and todo:
            k, v = todo.pop()
            fused = (self._sem(k), v)
            sn[k] = v
        for k, v in todo:
            self.eng[e].wait_ge(self._sem(k), v)
            sn[k] = v
        return fused

    def _collect(self, reads, writes):
        need = {}

        def add(t):
            if t is None:
                return
            k, v = t
            if need.get(k, 0) < v:
                need[k] = v

        for r in reads:
            add(r.w)
        for w in writes:
            add(w.w)
            for k, v in w.r.items():
                add((k, v))
        return need

    def _commit(self, tok, reads, writes):
        k, v = tok
        for r in reads:
            if r.r.get(k, 0) < v:
                r.r[k] = v
        for w in writes:
            w.w = tok
            w.r = {}

    def op(self, e, fn, reads=(), writes=()):
        need = self._collect(reads, writes)
        if e == "pe":
            need.pop("pe", None)
        fused = self._wait(e, need, keep_one=True)
        ins = fn(self.eng[e])
        first = ins
        if isinstance(ins, tuple):
            first, ins = ins
        if fused is not None:
            first._wait_ge(fused[0], fused[1])
        self.cnt[e] += 1
        ins.then_inc(self.sems[e], 1)
        tok = (e, self.cnt[e])
        self._commit(tok, reads, writes)
        self.nops += 1
        return tok

    def dma(self, e, out, in_, reads=(), writes=(), **kw):
        i = self.dnext
        self.dnext = (self.dnext + 1) % len(self.dsem)
        need = self._collect(reads, writes)
        if self.dcnt[i]:
            if need.get(i, 0) < self.dcnt[i]:
                need[i] = self.dcnt[i]
        fused = self._wait(e, need, keep_one=True)
        ins = self.eng[e].dma_start(out=out, in_=in_, **kw)
        if fused is not None:
            ins._wait_ge(fused[0], fused[1])
        ins.then_inc(self.dsem[i], 16)
        self.dcnt[i] += 16
        tok = (i, self.dcnt[i])
        self._commit(tok, reads, writes)
        return tok

    def barrier(self, engines=("pe", "act", "dve", "pool", "sp"), skip_sems=()):
        need = {k: v for k, v in self.cnt.items() if v}
        for i, v in enumerate(self.dcnt):
            if v:
                need[i] = v
        for e in engines:
            self._wait(e, dict(need))


def _wblocks():
    b = []
    for i in range(16):
        b.append(("w_in", C_XBC + 256 * i, 256, 32))
    b.append(("w_in", C_DT, 32, 32))
    for i in range(8):
        b.append(("w_in", C_ZA + 256 * i, 256, 32))
    for c0 in (C_QB, C_KB, C_VB, C_ZB):
        for i in range(4):
            b.append(("w_in", c0 + 256 * i, 256, 32))
    for c0 in (C_QC, C_KCC, C_VC, C_OC, C_ZC):
        for i in range(4):
            b.append(("w_in", c0 + 256 * i, 256, 32))
    b.append(("w_in", C_I, 16, 32))
    for j in range(16):
        b.append(("w_branch", 256 * j, 256, 32))
        for br in range(3):
            b.append(("w_in", C_G + br * 4096 + 256 * j, 256, 32))
    for j in range(16):
        b.append(("w_out", 256 * j, 256, 32))
    b.append(("w_ple", 0, 4096, 2))
    for j in range(16):
        b.append(("w_ple_gate", 256 * j, 256, 32))
    return b


WBLOCKS = _wblocks()
WB_ELEMS = [128 * kc * n for (_, _, n, kc) in WBLOCKS]
WB_OFF = np.concatenate([[0], np.cumsum(WB_ELEMS)]).astype(np.int64)
WL_ELEMS = int(WB_OFF[-1])
PAGE_ELEMS = 48 * 1024 * 1024
WB_PAGE = []
_pg, _po = 0, 0
for _n in WB_ELEMS:
    if _po + _n > PAGE_ELEMS:
        _pg += 1
        _po = 0
    WB_PAGE.append((_pg, _po))
    _po += _n
NPAGES = _pg + 1


def build(NL=2, NTOK=SEQ, dbg=None, stop=None):
    NT = NTOK // T
    nc = bass.Bass("TRN2", target_bir_lowering=False)
    sc = Sched(nc)
    E = sc.eng
    dbg_out = {}

    def din(name, shape, dt=F32):
        return nc.dram_tensor(name, list(shape), dt, kind="ExternalInput").ap()

    x_d = din("x", [NTOK, D])
    p_d = din("p", [NL, NTOK, 256])
    wmat = dict(w_in=din("w_in", [NL, D, WIN]), w_branch=din("w_branch", [NL, D, D]), w_out=din("w_out", [NL, D, D]),
                w_ple=din("w_ple", [NL, 256, D]), w_ple_gate=din("w_ple_gate", [NL, D, D]))
    convw_d = din("convw", [NL, 128, 32, 4])
    convb_d = din("convb", [NL, 128, 32])
    hp_d = din("hp", [NL, 32, 4])
    dskip_d = din("dskip", [NL, 1, 32])
    ssmg_d = din("ssmg", [NL, 128, 16])
    dlam_d = din("dlam", [NL, 1, 256])
    dng_d = din("dng", [NL, 1, 128])
    mgb_d = din("mgb", [NL, 8, 2])
    lng_d = din("lng", [NL, 1, D])
    lnb_d = din("lnb", [NL, 1, D])
    pleg_d = din("pleg", [NL, 1, D])
    cst_d = din("cst", [128, 128 * 3 + 8 * 32 + 8 * 128])
    out_d = nc.dram_tensor("out", [NTOK, D], F32, kind="ExternalOutput").ap()
    wsc_pages = [[nc.dram_tensor("wsc%d_%d" % (l, g), [PAGE_ELEMS], BF16, kind="Internal").ap() for g in range(NPAGES)] for l in range(NL)]

    def wsc_view(l, b, ne):
        pg, po = WB_PAGE[b]
        return wsc_pages[l][pg][po: po + 128 * ne].rearrange("(p e) -> p e", p=128)
    kc_d = nc.dram_tensor("kcache", [NL, 8, 128, NTOK], BF16, kind="Internal").ap()
    vc_d = nc.dram_tensor("vcache", [NL, 8, NTOK, 129], BF16, kind="Internal").ap()
    hb_d = nc.dram_tensor("hbuf", [NTOK, D], F32, kind="Internal").ap()
    r_wsc, r_kc, r_vc, r_hb = Res(), Res(), Res(), Res()

    def dbg_dump(name, ap, shape, dt=F32, reads=()):
        if dbg is None or name not in dbg:
            return
        o = nc.dram_tensor("dbg_" + name, list(shape), dt, kind="ExternalOutput").ap()
        dbg_out[name] = sc.dma("pool", o, ap, reads=reads)

    def sb(name, shape, dt=F32):
        return nc.sbuf_tensor(name, list(shape), dt).__enter__()

    st_cm = [nc.sbuf_tensor("pst%d" % i, [128, 8192], F32) for i in range(2)]
    sb_cm = [nc.sbuf_tensor("psb%d" % i, [128, 8192], BF16) for i in range(2)]
    stg = [c.__enter__() for c in st_cm]
    sbg = [c.__enter__() for c in sb_cm]
    r_stg = [Res(), Res()]
    r_sbg = [Res(), Res()]
    bi = 0
    for l in range(NL):
        for bidx, (mat, c0, n, kc) in enumerate(WBLOCKS):
            s = bi % 2
            W = wmat[mat][l]
            src = W[:, c0:c0 + n].rearrange("(k p) c -> p k c", p=128)
            stv = stg[s][:, 0:kc * n].rearrange("p (k c) -> p k c", k=kc)
            step = 8 if kc >= 8 else kc
            if n > 1024:
                step = 1
            for k0 in range(0, kc, step):
                if n > 1024:
                    for c1 in range(0, n, 1024):
                        sc.dma("sp", stv[:, k0:k0 + step, c1:c1 + 1024], src[:, k0:k0 + step, c1:c1 + 1024], writes=[r_stg[s]])
                else:
                    sc.dma("sp", stv[:, k0:k0 + step, :], src[:, k0:k0 + step, :], writes=[r_stg[s]])
            ne = kc * n
            if bi % 2 == 0:
                sc.op("dve", lambda e: e.tensor_copy(out=sbg[s][:, 0:ne], in_=stg[s][:, 0:ne]), reads=[r_stg[s]], writes=[r_sbg[s]])
            else:
                sc.op("act", lambda e: e.copy(out=sbg[s][:, 0:ne], in_=stg[s][:, 0:ne]), reads=[r_stg[s]], writes=[r_sbg[s]])
            dst = wsc_view(l, bidx, ne)
            sc.dma("pool", dst, sbg[s][:, 0:ne], reads=[r_sbg[s]], writes=[r_wsc])
            bi += 1
    sc.barrier()
    for c in reversed(sb_cm):
        c.__exit__(None, None, None)
    for c in reversed(st_cm):
        c.__exit__(None, None, None)

    cst = sb("cst_sb", [128, 128 * 3 + 8 * 32 + 8 * 128])
    ident = cst[:, 0:128]
    trimask = cst[:, 128:256]
    ones = cst[:, 256:384]
    tab = cst[:, 384:384 + 256].rearrange("p (h r) -> p h r", h=8)
    Dh = cst[:, 640:640 + 1024].rearrange("p (h q) -> p h q", h=8)
    identb = sb("identb", [128, 128], BF16)
    epsc = sb("epsc", [128, 1])
    zero32 = sb("zero32", [32, 1])
    xT = sb("xT", [128, KC, T], BF16)
    NWB = 3
    wbuf = [sb("wbuf%d" % i, [128, 8192], BF16) for i in range(NWB)]
    r_wbuf = [Res() for _ in range(NWB)]
    oaT = sb("oaT", [128, 16, T], BF16)
    obT = sb("obT", [128, 8, T], BF16)
    ocT = sb("ocT", [128, 8, T], BF16)
    mT = sb("mT", [128, KC, T], BF16)
    r_xT, r_oaT, r_obT, r_ocT, r_mT = Res(), Res(), Res(), Res(), Res()
    Sst = [sb("Sst%d" % l, [128, 2048]) for l in range(NL)]
    Sbf = [sb("Sbf%d" % l, [128, 2048], BF16) for l in range(NL)]
    tails = [sb("tails%d" % l, [128, 32, 3]) for l in range(NL)]
    Cst = [sb("Cst%d" % l, [128, 8, 129]) for l in range(NL)]
    Cbf = [sb("Cbf%d" % l, [128, 8, 129], BF16) for l in range(NL)]
    FM8 = [sb("fm8_%d" % l, [8, 2]) for l in range(NL)]
    r_S = [Res() for _ in range(NL)]
    r_Sbf = [Res() for _ in range(NL)]
    r_tails = [Res() for _ in range(NL)]
    r_C = [Res() for _ in range(NL)]
    r_Cbf = [Res() for _ in range(NL)]
    r_FM8 = [Res() for _ in range(NL)]
    convw = [sb("convw%d" % l, [128, 32, 4]) for l in range(NL)]
    convb = [sb("convb%d" % l, [128, 32]) for l in range(NL)]
    hp = [sb("hp%d" % l, [32, 4]) for l in range(NL)]
    nega = [sb("nega%d" % l, [32, 1]) for l in range(NL)]
    dskB = [sb("dskB%d" % l, [128, 32]) for l in range(NL)]
    ssmgF = [sb("ssmgF%d" % l, [128, 16]) for l in range(NL)]
    dngB = [sb("dngB%d" % l, [128, 128]) for l in range(NL)]
    lamt = [sb("lam%d" % l, [128, 4]) for l in range(NL)]
    dlt = sb("dlt", [128, 256])
    mgb = [sb("mgb%d" % l, [8, 2]) for l in range(NL)]
    nfb = [sb("nfb%d" % l, [8, 1]) for l in range(NL)]
    ARENA = 60 * 1024
    arena = sb("arena", [128, ARENA // 2], BF16)
    psums = [nc.psum_tensor("ps%d" % i, [128, 512], F32).__enter__() for i in range(8)]
    r_ps = [Res() for _ in range(8)]
    psn = [0]

    psrot = [8]

    def psum():
        i = psn[0] % psrot[0]
        psn[0] += 1
        return psums[i], r_ps[i]

    class Carve:
        def __init__(self):
            self.off = 0

        def get(self, shape, dt=F32):
            n = int(np.prod(shape[1:]))
            nb = n * (4 if dt == F32 else 2)
            nb_al = (nb + 63) // 64 * 64
            assert self.off + nb_al <= ARENA, ("arena overflow", self.off, nb_al)
            ARENA_MAX[0] = max(ARENA_MAX[0], self.off + nb_al)
            v = arena[0:shape[0], self.off // 2: self.off // 2 + nb // 2]
            self.off += nb_al
            if dt == F32:
                v = v.bitcast(F32)
            if len(shape) == 3:
                v = v.rearrange("p (a b) -> p a b", a=shape[1])
            return v

    r_c = Res()
    sc.dma("pool", cst[:], cst_d, writes=[r_c])
    sc.op("dve", lambda e: e.tensor_copy(out=identb[:], in_=ident), reads=[r_c], writes=[r_c])
    sc.op("dve", lambda e: e.memset(epsc[:], EPS), writes=[r_c])
    sc.op("dve", lambda e: e.memset(zero32[:], 0.0), writes=[r_c])
    for l in range(NL):
        sc.dma("pool", convw[l][:], convw_d[l], writes=[r_c])
        sc.dma("pool", convb[l][:], convb_d[l], writes=[r_c])
        sc.dma("pool", hp[l][:], hp_d[l], writes=[r_c])
        sc.dma("pool", dskB[l][:], dskip_d[l].partition_broadcast(128), writes=[r_c])
        sc.dma("pool", ssmgF[l][:], ssmg_d[l], writes=[r_c])
        sc.dma("pool", dngB[l][:], dng_d[l].partition_broadcast(128), writes=[r_c])
        sc.dma("pool", mgb[l][:], mgb_d[l], writes=[r_c])
        sc.dma("pool", dlt[:], dlam_d[l].partition_broadcast(128), writes=[r_c])
        sc.op("act", lambda e: e.activation(out=nega[l][:], in_=hp[l][:, 1:2], func=AF.Exp), reads=[r_c], writes=[r_c])
        sc.op("dve", lambda e: e.tensor_scalar(out=nega[l][:], in0=nega[l][:], scalar1=-1.0, scalar2=None, op0=ALU.mult), reads=[r_c], writes=[r_c])
        sc.op("dve", lambda e: e.tensor_scalar(out=nfb[l][:], in0=mgb[l][:, 1:2], scalar1=-1.0, scalar2=None, op0=ALU.mult), reads=[r_c], writes=[r_c])
        lam_init = 0.8 - 0.6 * math.exp(-0.3 * l)
        sc.op("dve", lambda e: e.tensor_tensor(out=dlt[:, 0:64], in0=dlt[:, 0:64], in1=dlt[:, 64:128], op=ALU.mult), reads=[r_c], writes=[r_c])
        sc.op("dve", lambda e: e.tensor_tensor(out=dlt[:, 128:192], in0=dlt[:, 128:192], in1=dlt[:, 192:256], op=ALU.mult), reads=[r_c], writes=[r_c])
        sc.op("dve", lambda e: e.reduce_sum(out=lamt[l][:, 2:3], in_=dlt[:, 0:64], axis=mybir.AxisListType.X), reads=[r_c], writes=[r_c])
        sc.op("dve", lambda e: e.reduce_sum(out=lamt[l][:, 3:4], in_=dlt[:, 128:192], axis=mybir.AxisListType.X), reads=[r_c], writes=[r_c])
        sc.op("act", lambda e: e.activation(out=lamt[l][:, 2:4], in_=lamt[l][:, 2:4], func=AF.Exp), reads=[r_c], writes=[r_c])
        sc.op("dve", lambda e: e.tensor_tensor(out=lamt[l][:, 0:1], in0=lamt[l][:, 2:3], in1=lamt[l][:, 3:4], op=ALU.subtract), reads=[r_c], writes=[r_c])
        sc.op("dve", lambda e: e.tensor_scalar(out=lamt[l][:, 0:1], in0=lamt[l][:, 0:1], scalar1=lam_init, scalar2=None, op0=ALU.add), reads=[r_c], writes=[r_c])
        sc.op("dve", lambda e: e.tensor_scalar(out=lamt[l][:, 1:2], in0=lamt[l][:, 0:1], scalar1=-1.0, scalar2=None, op0=ALU.mult), reads=[r_c], writes=[r_c])
        sc.op("dve", lambda e: e.memset(Sst[l][:], 0.0), writes=[r_S[l]])
        sc.op("dve", lambda e: e.memset(Sbf[l][:], 0.0), writes=[r_Sbf[l]])
        sc.op("dve", lambda e: e.memset(tails[l][:], 0.0), writes=[r_tails[l]])
        sc.op("dve", lambda e: e.memset(Cst[l][:], 0.0), writes=[r_C[l]])
        sc.op("dve", lambda e: e.memset(Cbf[l][:], 0.0), writes=[r_Cbf[l]])
        sc.op("dve", lambda e: e.memset(FM8[l][:], 0.0), writes=[r_FM8[l]])
    sc.barrier()

    WPLE_B = WBLOCKS.index(("w_ple", 0, 4096, 2))
    wplan = [(l, b) for _t in range(NT) for l in range(NL) for b in range(len(WBLOCKS)) if b != WPLE_B]
    wstate = dict(issued=0, used=0)

    def w_issue():
        i = wstate["issued"]
        if i >= len(wplan):
            return
        l, b = wplan[i]
        _, _, n, kc = WBLOCKS[b]
        ne = kc * n
        s = i % NWB
        src = wsc_view(l, b, ne)
        sc.dma("sp", wbuf[s][:, 0:ne], src, writes=[r_wbuf[s]])
        wstate["issued"] = i + 1

    def w_next(expect):
        i = wstate["used"]
        l, b = wplan[i]
        mat, c0, n, kc = WBLOCKS[b]
        assert (mat, c0) == expect, (WBLOCKS[b], expect)
        while wstate["issued"] <= min(i + NWB - 1, len(wplan) - 1):
            w_issue()
        wstate["used"] = i + 1
        s = i % NWB
        return wbuf[s][:, 0:kc * n].rearrange("p (k c) -> p k c", k=kc), r_wbuf[s]

    def mm_group(ps_ap, r_p, pairs, reads):
        n = len(pairs)
        PEI[0] += n

        def fn(e):
            ins = None
            first = None
            for i, (a, b) in enumerate(pairs):
                ins = e.matmul(ps_ap, lhsT=a, rhs=b, start=(i == 0), stop=(i == n - 1))
                if first is None:
                    first = ins
            return (first, ins)
        return sc.op("pe", fn, reads=reads, writes=[r_p])

    def transpose_to(dst_ap, r_dst, src_ap, r_src, kpart, dt, evac="dve", scale_ap=None):
        ps, r_p = psum()
        PEI[0] += 1
        nfree = src_ap.shape[-1]
        if dt == BF16:
            pv = ps[:].bitcast(BF16)[0:nfree, 0:kpart]
            idn = identb[0:kpart, 0:kpart]
        else:
            pv = ps[0:nfree, 0:kpart]
            idn = ident[0:kpart, 0:kpart]
        sc.op("pe", lambda e: e.transpose(out=pv, in_=src_ap, identity=idn), reads=[r_src], writes=[r_p])
        if scale_ap is not None:
            sc.op("dve", lambda e: e.tensor_scalar(out=dst_ap, in0=pv, scalar1=scale_ap, scalar2=None, op0=ALU.mult), reads=[r_p], writes=[r_dst])
        elif evac == "dve":
            sc.op("dve", lambda e: e.tensor_copy(out=dst_ap, in_=pv), reads=[r_p], writes=[r_dst])
        else:
            sc.op("act", lambda e: e.copy(out=dst_ap, in_=pv), reads=[r_p], writes=[r_dst])

    def dla_chunk(cv, nh, hpg, vw, bT, cT_, r_bc, b_tm, val_tm, r_val, rFM, cFM, r_fm, scT, r_scT, state, state_bf,
                  r_state, r_state_bf, decB, r_dec, out_cb):
        ng = nh // hpg
        gw = hpg * vw
        GTm = cv.get([128, 128])
        r_GTm = Res()
        xin = cv.get([128, hpg, vw], BF16)
        xcs = cv.get([128, hpg, vw], BF16)
        r_xin, r_xcs = Res(), Res()
        rh = [cv.get([nh, 128]) for _ in range(2)]
        argb = [cv.get([128, 128]) for _ in range(2)]
        PTb = [cv.get([128, 128], BF16) for _ in range(2)]
        r_rh = [Res(), Res()]
        r_arg = [Res(), Res()]
        r_PT = [Res(), Res()]
        t4 = cv.get([128, hpg, vw])
        r_t4 = Res()
        hi = 0
        for g in range(ng):
            hs = slice(g * hpg, (g + 1) * hpg)
            ps1, r_p1 = psum()
            mm_group(ps1[:, 0:128], r_p1, [(bT(g), cT_(g))], [r_bc])
            sc.op("dve", lambda e: e.tensor_tensor(out=GTm, in0=ps1[:, 0:128], in1=trimask, op=ALU.mult), reads=[r_p1], writes=[r_GTm])
            sc.op("dve", lambda e: e.tensor_tensor(out=xin, in0=val_tm[:, hs, :], in1=scT[:, 1, hs].unsqueeze(2).to_broadcast([128, hpg, vw]), op=ALU.mult),
                  reads=[r_val, r_scT], writes=[r_xin])
            sc.op("dve", lambda e: e.tensor_tensor(out=xcs, in0=val_tm[:, hs, :], in1=scT[:, 3, hs].unsqueeze(2).to_broadcast([128, hpg, vw]), op=ALU.mult),
                  reads=[r_val, r_scT], writes=[r_xcs])
            psy, r_py = psum()
            psyv = psy[:, 0:2 * gw]
            for hh in range(hpg):
                h = g * hpg + hh
                k = hi % 2
                hi += 1
                sc.op("dve", lambda e: e.tensor_scalar(out=rh[k], in0=rFM, scalar1=ident[0:nh, h:h + 1], scalar2=None, op0=ALU.mult),
                      reads=[r_fm], writes=[r_rh[k]])
                ps2, r_p2 = psum()
                mm_group(ps2[:, 0:128], r_p2, [(ones[0:nh, :], rh[k])], [r_rh[k]])
                sc.op("dve", lambda e: e.tensor_scalar(out=argb[k], in0=ps2[:, 0:128], scalar1=scT[:, 0, h:h + 1], scalar2=0.0, op0=ALU.subtract, op1=ALU.min),
                      reads=[r_p2, r_scT], writes=[r_arg[k]])
                sc.op("act", lambda e: e.activation(out=argb[k], in_=argb[k], func=AF.Exp), reads=[r_arg[k]], writes=[r_arg[k]])
                sc.op("dve", lambda e: e.tensor_tensor(out=PTb[k], in0=argb[k], in1=GTm, op=ALU.mult), reads=[r_arg[k], r_GTm], writes=[r_PT[k]])
                mm_group(psy[:, hh * vw:(hh + 1) * vw], r_py, [(PTb[k], xin[:, hh, :])], [r_PT[k], r_xin])
            mm_group(psy[:, gw:2 * gw], r_py, [(cT_(g), state_bf[:, hs, :])], [r_bc, r_state_bf])
            out_cb(g, psy, r_py)
            ps3, r_p3 = psum()
            mm_group(ps3[:, 0:gw], r_p3, [(b_tm(g), xcs[:])], [r_bc, r_xcs])
            sc.op("dve", lambda e: e.tensor_tensor(out=t4, in0=state[:, hs, :], in1=decB[:, hs].unsqueeze(2).to_broadcast([128, hpg, vw]), op=ALU.mult),
                  reads=[r_state, r_dec], writes=[r_t4])
            sc.op("dve", lambda e: e.tensor_tensor(out=state[:, hs, :], in0=t4, in1=ps3[:, 0:gw].rearrange("p (a b) -> p a b", a=hpg), op=ALU.add),
                  reads=[r_t4, r_p3], writes=[r_state])
            sc.op("act", lambda e: e.copy(out=state_bf[:, hs, :], in_=state[:, hs, :]), reads=[r_state], writes=[r_state_bf])

    def fm_terms(cv, nh, rFM_t, cFM_t, extra_t, r_fm, r_prev, rprev_ap, nchunks):
        outs = []
        nrp = cv.get([nh, 1])
        rs = cv.get([nh, 128])
        cs = cv.get([nh, 128])
        dec = cv.get([nh, 1])
        dg = cv.get([nh, nh])
        r_t = Res()
        for c in range(nchunks):
            ch = slice(c * 128, (c + 1) * 128)
            prev = rprev_ap if c == 0 else rFM_t[:, c * 128 - 1:c * 128]
            rend = rFM_t[:, (c + 1) * 128 - 1:(c + 1) * 128]
            scT = cv.get([128, 4, nh])
            decB = cv.get([128, nh])
            r_scT, r_dec = Res(), Res()
            sc.op("dve", lambda e: e.tensor_scalar(out=nrp, in0=prev, scalar1=-1.0, scalar2=None, op0=ALU.mult), reads=[r_fm, r_prev], writes=[r_t])
            sc.op("act", lambda e: e.activation(out=rs, in_=rFM_t[:, ch], func=AF.Exp, bias=nrp, scale=1.0), reads=[r_fm, r_t], writes=[r_t])
            sc.op("act", lambda e: e.activation(out=cs, in_=cFM_t[:, ch], func=AF.Exp, bias=rend, scale=-1.0), reads=[r_fm, r_t], writes=[r_t])
            if extra_t is not None:
                sc.op("dve", lambda e: e.tensor_tensor(out=cs, in0=cs, in1=extra_t[:, ch], op=ALU.mult), reads=[r_fm, r_t], writes=[r_t])
            sc.op("act", lambda e: e.activation(out=dec, in_=rend, func=AF.Exp, bias=nrp, scale=1.0), reads=[r_fm, r_t], writes=[r_t])
            sc.op("dve", lambda e: e.tensor_scalar(out=dg, in0=ident[0:nh, 0:nh], scalar1=dec, scalar2=None, op0=ALU.mult), reads=[r_t], writes=[r_t])
            ps, r_p = psum()
            mm_group(ps[:, 0:nh], r_p, [(ones[0:nh, :], dg)], [r_t])
            sc.op("dve", lambda e: e.tensor_copy(out=decB, in_=ps[:, 0:nh]), reads=[r_p], writes=[r_dec])
            ps, r_p = psum()
            srcs = [cFM_t[:, ch], (extra_t[:, ch] if extra_t is not None else None), rs, cs]
            PEI[0] += sum(1 for s_ in srcs if s_ is not None)

            def fn(e):
                ins = None
                first = None
                for i, s_ in enumerate(srcs):
                    if s_ is None:
                        continue
                    ins = e.transpose(out=ps[:, i * nh:(i + 1) * nh], in_=s_, identity=ident[0:nh, 0:nh])
                    if first is None:
                        first = ins
                return (first, ins)
            sc.op("pe", fn, reads=[r_fm, r_t], writes=[r_p])
            if extra_t is None:
                sc.op("dve", lambda e: e.memset(scT[:, 1, :], 1.0), writes=[r_scT])
                sc.op("dve", lambda e: e.tensor_copy(out=scT[:, 0, :], in_=ps[:, 0:nh]), reads=[r_p], writes=[r_scT])
                sc.op("dve", lambda e: e.tensor_copy(out=scT[:, 2:4, :], in_=ps[:, 2 * nh:4 * nh].rearrange("p (a b) -> p a b", a=2)), reads=[r_p], writes=[r_scT])
            else:
                sc.op("dve", lambda e: e.tensor_copy(out=scT, in_=ps[:, 0:4 * nh].rearrange("p (a b) -> p a b", a=4)), reads=[r_p], writes=[r_scT])
            outs.append((scT, r_scT, decB, r_dec))
        return outs

    for t in range(NT):
        t0 = t * T
        for l in range(NL):
            last = (l == NL - 1)
            lam_init = 0.8 - 0.6 * math.exp(-0.3 * l)
            if l == 0:
                cv = Carve()
                xs = [cv.get([128, D]) for _ in range(2)]
                r_xs = [Res(), Res()]
                for s in range(NSL):
                    k = s % 2
                    sc.dma("pool", xs[k], x_d[t0 + s * 128: t0 + (s + 1) * 128, :], writes=[r_xs[k]])
                    for kc in range(KC):
                        transpose_to(xT[:, kc, s * 128:(s + 1) * 128], r_xT, xs[k][:, kc * 128:(kc + 1) * 128], r_xs[k], 128, F32,
                                     evac=("dve" if kc % 2 == 0 else "act"))
                sc.barrier(engines=("pe", "act", "dve", "pool"))
            MARKS.append(("A", t, l, PEI[0]))
            cv = Carve()
            xh = cv.get([128, NSL, 2048], BF16)
            BF = cv.get([128, 8, T], BF16)
            CF = cv.get([128, 8, T], BF16)
            BT = cv.get([128, NSL, 1024], BF16)
            za = cv.get([128, NSL, 2048], BF16)
            ub = [cv.get([128, T + 3]) for _ in range(2)]
            acc = [cv.get([128, T]) for _ in range(2)]
            xc = [cv.get([128, T], BF16) for _ in range(2)]
            r_xh, r_BC, r_za = Res(), Res(), Res()
            r_ub = [Res(), Res()]
            r_acc = [Res(), Res()]
            r_xc = [Res(), Res()]
            blk = 0
            pendA = []
            for wi in range(16):
                wv, r_w = w_next(("w_in", C_XBC + 256 * wi))
                for half in range(2):
                    k = blk % 2
                    ps, r_p = psum()
                    mm_group(ps[:, 0:T], r_p, [(wv[:, kc, half * 128:(half + 1) * 128], xT[:, kc, :]) for kc in range(KC)], [r_w, r_xT])
                    for f_ in pendA:
                        f_()
                    pendA = []
                    sc.op("act", lambda e: e.copy(out=ub[k][:, 3:3 + T], in_=ps[:, 0:T]), reads=[r_p], writes=[r_ub[k]])
                    sc.op("dve", lambda e: e.tensor_copy(out=ub[k][:, 0:3], in_=tails[l][:, blk, :]), reads=[r_tails[l]], writes=[r_ub[k]])
                    sc.op("dve", lambda e: e.tensor_scalar(out=acc[k], in0=ub[k][:, 0:T], scalar1=convw[l][:, blk, 0:1], scalar2=None, op0=ALU.mult),
                          reads=[r_ub[k]], writes=[r_acc[k]])
                    for j in range(1, 4):
                        sc.op("dve", lambda e: e.scalar_tensor_tensor(out=acc[k], in0=ub[k][:, j:j + T], scalar=convw[l][:, blk, j:j + 1], in1=acc[k],
                                                                      op0=ALU.mult, op1=ALU.add), reads=[r_ub[k], r_acc[k]], writes=[r_acc[k]])
                    sc.op("dve", lambda e: e.tensor_copy(out=tails[l][:, blk, :], in_=ub[k][:, T:T + 3]), reads=[r_ub[k]], writes=[r_tails[l]])
                    if blk < 16:
                        dst, r_dst = xc[k], r_xc[k]
                    elif blk < 24:
                        dst, r_dst = BF[:, blk - 16, :], r_BC
                    else:
                        dst, r_dst = CF[:, blk - 24, :], r_BC
                    sc.op("act", lambda e: e.activation(out=dst, in_=acc[k], func=AF.Silu, bias=convb[l][:, blk:blk + 1], scale=1.0),
                          reads=[r_acc[k]], writes=[r_dst])
                    if blk < 16:
                        for s in range(NSL):
                            pendA.append(lambda s=s, blk=blk, k=k: transpose_to(xh[:, s, blk * 128:(blk + 1) * 128], r_xh, xc[k][:, s * 128:(s + 1) * 128], r_xc[k], 128, BF16,
                                                                                evac=("dve" if s % 2 == 0 else "act")))
                    elif blk < 24:
                        for s in range(NSL):
                            pendA.append(lambda s=s, blk=blk: transpose_to(BT[:, s, (blk - 16) * 128:(blk - 15) * 128], r_BC, BF[:, blk - 16, s * 128:(s + 1) * 128], r_BC, 128, BF16,
                                                                           evac=("dve" if s % 2 == 0 else "act")))
                    blk += 1
            for f_ in pendA:
                f_()
            pendA = []
            MARKS.append(("A_dt", t, l, PEI[0]))
            dtF = cv.get([32, T])
            laF = cv.get([32, T])
            AF_ = cv.get([32, T])
            ones32T = cv.get([32, T])
            r_dtf = Res()
            wv, r_w = w_next(("w_in", C_DT))
            ps, r_p = psum()
            mm_group(ps[0:32, 0:T], r_p, [(wv[:, kc, 0:32], xT[:, kc, :]) for kc in range(KC)], [r_w, r_xT])
            sc.op("act", lambda e: e.activation(out=dtF, in_=ps[0:32, 0:T], func=AF.Exp, bias=hp[l][:, 0:1], scale=1.0), reads=[r_p], writes=[r_dtf])
            sc.op("act", lambda e: e.activation(out=dtF, in_=dtF, func=AF.Ln, bias=ones[0:32, 0:1], scale=1.0), reads=[r_dtf], writes=[r_dtf])
            sc.op("dve", lambda e: e.tensor_scalar(out=laF, in0=dtF, scalar1=nega[l][:, 0:1], scalar2=None, op0=ALU.mult), reads=[r_dtf], writes=[r_dtf])
            sc.op("dve", lambda e: e.memset(ones32T, 1.0), writes=[r_dtf])
            sc.op("dve", lambda e: e.tensor_tensor_scan(out=AF_, data0=ones32T, data1=laF, initial=0.0, op0=ALU.mult, op1=ALU.add), reads=[r_dtf], writes=[r_dtf])
            r_z32 = Res()
            terms = fm_terms(cv, 32, AF_, AF_, dtF, r_dtf, r_z32, zero32[:], NSL)
            for wi in range(8):
                wv, r_w = w_next(("w_in", C_ZA + 256 * wi))
                for s in range(NSL):
                    ps, r_p = psum()
                    mm_group(ps[:, 0:256], r_p, [(xT[:, kc, s * 128:(s + 1) * 128], wv[:, kc, :]) for kc in range(KC)], [r_w, r_xT])
                    sc.op("act", lambda e: e.activation(out=za[:, s, wi * 256:(wi + 1) * 256], in_=ps[:, 0:256], func=AF.Silu), reads=[r_p], writes=[r_za])
            MARKS.append(("A_ssd", t, l, PEI[0]))
            ysb = cv.get([128, 2048])
            r_y = Res()
            t1 = cv.get([128, 4, 64])
            t3 = cv.get([128, 4, 64])
            r_t1, r_t3 = Res(), Res()
            oab = cv.get([128, 2048], BF16)
            r_oab = Res()
            st2 = cv.get([128, 4])
            r_st2 = Res()
            cvmark = cv.off
            for c in range(NSL):
                cv.off = cvmark
                ch = slice(c * 128, (c + 1) * 128)
                scT, r_scT, decB, r_dec = terms[c]
                xh_c = xh[:, c, :].rearrange("p (h v) -> p h v", h=32)

                def out_cb(g, psy, r_py, c=c, xh_c=xh_c, scT=scT, r_scT=r_scT):
                    hs = slice(4 * g, 4 * g + 4)
                    sc.op("dve", lambda e: e.tensor_tensor(out=t1, in0=psy[:, 256:512].rearrange("p (a b) -> p a b", a=4),
                                                          in1=scT[:, 2, hs].unsqueeze(2).to_broadcast([128, 4, 64]), op=ALU.mult),
                          reads=[r_py, r_scT], writes=[r_t1])
                    sc.op("dve", lambda e: e.tensor_tensor(out=t1, in0=t1, in1=psy[:, 0:256].rearrange("p (a b) -> p a b", a=4), op=ALU.add),
                          reads=[r_py, r_t1], writes=[r_t1])
                    sc.op("dve", lambda e: e.tensor_tensor(out=t3, in0=xh_c[:, hs, :], in1=dskB[l][:, hs].unsqueeze(2).to_broadcast([128, 4, 64]), op=ALU.mult),
                          reads=[r_xh], writes=[r_t3])
                    sc.op("dve", lambda e: e.tensor_tensor(out=ysb[:, g * 256:(g + 1) * 256].rearrange("p (a b) -> p a b", a=4), in0=t1, in1=t3, op=ALU.add),
                          reads=[r_t1, r_t3], writes=[r_y])

                dla_chunk(cv, 32, 4, 64,
                          lambda g: BF[:, g, ch], lambda g: CF[:, g, ch], r_BC,
                          lambda g: BT[:, c, g * 128:(g + 1) * 128], xh_c, r_xh,
                          AF_[:, ch], AF_[:, ch], r_dtf, scT, r_scT,
                          Sst[l][:].rearrange("p (h v) -> p h v", h=32), Sbf[l][:].rearrange("p (h v) -> p h v", h=32),
                          r_S[l], r_Sbf[l], decB, r_dec, out_cb)
                sc.op("dve", lambda e: e.tensor_tensor(out=ysb, in0=ysb, in1=za[:, c, :], op=ALU.mult), reads=[r_y, r_za], writes=[r_y])
                sc.op("dve", lambda e: e.memset(st2[:], 0.0), writes=[r_st2])
                sc.op("act", lambda e: e.activation(out=oab, in_=ysb, func=AF.Square, accum_out=st2[:, 0:1]), reads=[r_y], writes=[r_oab, r_st2])
                sc.op("act", lambda e: e.activation(out=st2[:, 1:2], in_=st2[:, 0:1], func=AF.Sqrt, bias=epsc[:, 0:1], scale=1.0 / 2048), reads=[r_st2], writes=[r_st2])
                sc.op("dve", lambda e: e.reciprocal(out=st2[:, 2:3], in_=st2[:, 1:2]), reads=[r_st2], writes=[r_st2])
                sc.op("dve", lambda e: e.tensor_scalar(out=oab, in0=ysb, scalar1=st2[:, 2:3], scalar2=None, op0=ALU.mult),
                      reads=[r_y, r_st2], writes=[r_oab])
                for kc in range(16):
                    transpose_to(oaT[:, kc, ch], r_oaT, oab[:, kc * 128:(kc + 1) * 128], r_oab, 128, BF16, scale_ap=ssmgF[l][:, kc:kc + 1])
            dbg_dump("oaT", oaT[:], [128, 16, T], BF16, reads=[r_oaT])
            sc.barrier(engines=("pe", "act", "dve", "pool"))
            if stop == "A":
                break

            MARKS.append(("B", t, l, PEI[0]))
            cv = Carve()
            qT = cv.get([128, 8, T], BF16)
            kTn = cv.get([128, 8, T], BF16)
            vn = cv.get([128, NSL, 8 * 129], BF16)
            zb = cv.get([128, NSL, 1024], BF16)
            obb = cv.get([128, NSL, 1024], BF16)
            r_q, r_kn, r_vn, r_zb, r_obb = Res(), Res(), Res(), Res(), Res()
            sc.op("dve", lambda e: e.memset(vn[:], 1.0), writes=[r_vn])
            for (c0, dstT, r_d) in ((C_QB, qT, r_q), (C_KB, kTn, r_kn)):
                for wi in range(4):
                    wv, r_w = w_next(("w_in", c0 + 256 * wi))
                    for half in range(2):
                        h = 2 * wi + half
                        ps, r_p = psum()
                        mm_group(ps[:, 0:T], r_p, [(wv[:, kc, half * 128:(half + 1) * 128], xT[:, kc, :]) for kc in range(KC)], [r_w, r_xT])
                        if half == 0:
                            sc.op("act", lambda e: e.copy(out=dstT[:, h, :], in_=ps[:, 0:T]), reads=[r_p], writes=[r_d])
                        else:
                            sc.op("dve", lambda e: e.tensor_copy(out=dstT[:, h, :], in_=ps[:, 0:T]), reads=[r_p], writes=[r_d])
            sc.dma("pool", kc_d[l][:, :, t0:t0 + T].rearrange("h p t -> p h t"), kTn[:], reads=[r_kn], writes=[r_kc])
            for wi in range(4):
                wv, r_w = w_next(("w_in", C_VB + 256 * wi))
                for s in range(NSL):
                    ps, r_p = psum()
                    mm_group(ps[:, 0:256], r_p, [(xT[:, kc, s * 128:(s + 1) * 128], wv[:, kc, :]) for kc in range(KC)], [r_w, r_xT])
                    sc.op("dve", lambda e: e.tensor_copy(out=vn[:, s, :].rearrange("p (h e) -> p h e", h=8)[:, 2 * wi:2 * wi + 2, 0:128],
                                                        in_=ps[:, 0:256].rearrange("p (h e) -> p h e", h=2)), reads=[r_p], writes=[r_vn])
            for s in range(NSL):
                sc.dma("pool", vc_d[l][:, t0 + s * 128: t0 + (s + 1) * 128, :].rearrange("h p e -> p h e"),
                       vn[:, s, :].rearrange("p (h e) -> p h e", h=8), reads=[r_vn], writes=[r_vc])
            for wi in range(4):
                wv, r_w = w_next(("w_in", C_ZB + 256 * wi))
                for s in range(NSL):
                    ps, r_p = psum()
                    mm_group(ps[:, 0:256], r_p, [(xT[:, kc, s * 128:(s + 1) * 128], wv[:, kc, :]) for kc in range(KC)], [r_w, r_xT])
                    sc.op("act", lambda e: e.activation(out=zb[:, s, wi * 256:(wi + 1) * 256], in_=ps[:, 0:256], func=AF.Silu), reads=[r_p], writes=[r_zb])
            MARKS.append(("B_att", t, l, PEI[0]))
            nkb = (t0 + T) // 128
            kbuf = [cv.get([128, NTOK], BF16) for _ in range(2)]
            vbuf = [cv.get([128, NTOK // 128, 129], BF16) for _ in range(2)]
            r_kb = [Res(), Res()]
            r_vb = [Res(), Res()]
            NPT = 6
            ptb = [cv.get([128, 128], BF16) for _ in range(NPT)]
            r_pt = [Res() for _ in range(NPT)]
            dtmp = [cv.get([128, 128]) for _ in range(2)]
            r_dtmp = [Res(), Res()]
            a0 = cv.get([128, 128])
            att = cv.get([128, 128])
            r_a0, r_att = Res(), Res()
            rec = cv.get([128, 8])
            r_rec = Res()
            psrot[0] = 6
            pti = 0
            oi = 0
            for h in range(8):
                k = h % 2
                sc.dma("pool", kbuf[k][:, 0:t0 + T], kc_d[l, h, :, 0:t0 + T], reads=[r_kc], writes=[r_kb[k]])
                sc.dma("pool", vbuf[k][:, 0:nkb, :], vc_d[l, h, 0:t0 + T, :].rearrange("(b p) e -> p b e", p=128), reads=[r_vc], writes=[r_vb[k]])
                for qb in range(NSL):
                    qabs = t0 // 128 + qb
                    psO, r_pO = psums[6 + oi % 2], r_ps[6 + oi % 2]
                    oi += 1
                    steps = [(c, kb) for c in range(2) for kb in range(qabs + 1)]
                    LOOK = 4

                    def emit_qk_exp(i, h=h, k=k, qb=qb, qabs=qabs):
                        c, kb = steps[i]
                        ps, r_p = psum()
                        mm_group(ps[:, 0:128], r_p, [(kbuf[k][c * 64:(c + 1) * 64, kb * 128:(kb + 1) * 128],
                                                      qT[c * 64:(c + 1) * 64, h, qb * 128:(qb + 1) * 128])], [r_kb[k], r_q])
                        pk = (pti0 + i) % NPT
                        if kb < qabs:
                            r_ = qabs - kb
                            sc.op("act", lambda e: e.activation(out=ptb[pk], in_=ps[:, 0:128], func=AF.Exp, bias=tab[:, h, r_:r_ + 1], scale=0.125),
                                  reads=[r_p], writes=[r_pt[pk]])
                        else:
                            dk = i % 2
                            sc.op("dve", lambda e: e.scalar_tensor_tensor(out=dtmp[dk], in0=ps[:, 0:128], scalar=0.125, in1=Dh[:, h, :], op0=ALU.mult, op1=ALU.add),
                                  reads=[r_p], writes=[r_dtmp[dk]])
                            sc.op("act", lambda e: e.activation(out=ptb[pk], in_=dtmp[dk], func=AF.Exp), reads=[r_dtmp[dk]], writes=[r_pt[pk]])

                    def emit_pv(i, k=k, qabs=qabs, psO=psO, r_pO=r_pO):
                        c, kb = steps[i]
                        Oc = psO[:, c * 256: c * 256 + 129]
                        pk = (pti0 + i) % NPT
                        PEI[0] += 1
                        sc.op("pe", lambda e: e.matmul(Oc, lhsT=ptb[pk], rhs=vbuf[k][:, kb, :], start=(kb == 0), stop=(kb == qabs)),
                              reads=[r_pt[pk], r_vb[k]], writes=[r_pO])

                    pti0 = pti
                    for i in range(len(steps) + LOOK):
                        if i < len(steps):
                            emit_qk_exp(i)
                        if i - LOOK >= 0:
                            emit_pv(i - LOOK)
                    pti += len(steps)
                    sc.op("dve", lambda e: e.reciprocal(out=rec[:, 0:1], in_=psO[:, 128:129]), reads=[r_pO], writes=[r_rec])
                    sc.op("dve", lambda e: e.reciprocal(out=rec[:, 1:2], in_=psO[:, 384:385]), reads=[r_pO], writes=[r_rec])
                    sc.op("dve", lambda e: e.tensor_tensor(out=rec[:, 2:3], in0=rec[:, 1:2], in1=lamt[l][:, 1:2], op=ALU.mult), reads=[r_rec], writes=[r_rec])
                    sc.op("dve", lambda e: e.tensor_scalar(out=a0, in0=psO[:, 0:128], scalar1=rec[:, 0:1], scalar2=None, op0=ALU.mult), reads=[r_pO, r_rec], writes=[r_a0])
                    sc.op("dve", lambda e: e.scalar_tensor_tensor(out=att, in0=psO[:, 256:384], scalar=rec[:, 2:3], in1=a0, op0=ALU.mult, op1=ALU.add),
                          reads=[r_pO, r_rec, r_a0], writes=[r_att])
                    sc.op("dve", lambda e: e.memset(rec[:, 3:4], 0.0), writes=[r_rec])
                    sc.op("act", lambda e: e.activation(out=a0, in_=att, func=AF.Square, accum_out=rec[:, 3:4]), reads=[r_att], writes=[r_a0, r_rec])
                    sc.op("act", lambda e: e.activation(out=rec[:, 4:5], in_=rec[:, 3:4], func=AF.Sqrt, bias=epsc[:, 0:1], scale=1.0 / 128), reads=[r_rec], writes=[r_rec])
                    sc.op("dve", lambda e: e.reciprocal(out=rec[:, 5:6], in_=rec[:, 4:5]), reads=[r_rec], writes=[r_rec])
                    sc.op("dve", lambda e: e.scalar_tensor_tensor(out=att, in0=att, scalar=rec[:, 5:6], in1=dngB[l][:], op0=ALU.mult, op1=ALU.mult),
                          reads=[r_att, r_rec], writes=[r_att])
                    sc.op("dve", lambda e: e.scalar_tensor_tensor(out=obb[:, qb, h * 128:(h + 1) * 128], in0=att, scalar=(1.0 - lam_init), in1=zb[:, qb, h * 128:(h + 1) * 128],
                                                                  op0=ALU.mult, op1=ALU.mult), reads=[r_att, r_zb], writes=[r_obb])
            psrot[0] = 8
            for s in range(NSL):
                for kc in range(8):
                    transpose_to(obT[:, kc, s * 128:(s + 1) * 128], r_obT, obb[:, s, kc * 128:(kc + 1) * 128], r_obb, 128, BF16, evac=("dve" if kc % 2 == 0 else "act"))
            dbg_dump("obT", obT[:], [128, 8, T], BF16, reads=[r_obT])
            sc.barrier(engines=("pe", "act", "dve", "pool"))
            if stop == "B":
                break
            MARKS.append(("C", t, l, PEI[0]))
            cv = Carve()
            qcT = cv.get([128, 8, T], BF16)
            kcT = cv.get([128, 8, T], BF16)
            kcM = cv.get([128, NSL, 1024], BF16)
            vcm = cv.get([128, NSL, 8 * 129], BF16)
            ocm = cv.get([128, NSL, 1024], BF16)
            zcm = cv.get([128, NSL, 1024], BF16)
            ocb = cv.get([128, 1024], BF16)
            r_qk, r_kcM, r_vcm, r_ocm, r_zcm, r_ocb = Res(), Res(), Res(), Res(), Res(), Res()
            sc.op("dve", lambda e: e.memset(vcm[:], 1.0), writes=[r_vcm])
            pendC = []
            for (c0, dstT, scl) in ((C_QC, qcT, 1.0), (C_KCC, kcT, 128.0 ** -0.5)):
                for wi in range(4):
                    wv, r_w = w_next(("w_in", c0 + 256 * wi))
                    for half in range(2):
                        h = 2 * wi + half
                        ps, r_p = psum()
                        mm_group(ps[:, 0:T], r_p, [(wv[:, kc, half * 128:(half + 1) * 128], xT[:, kc, :]) for kc in range(KC)], [r_w, r_xT])
                        for f_ in pendC:
                            f_()
                        pendC = []
                        sc.op("dve", lambda e: e.tensor_scalar(out=dstT[:, h, :], in0=ps[:, 0:T], scalar1=scl, scalar2=None, op0=ALU.mult), reads=[r_p], writes=[r_qk])
                        if c0 == C_KCC:
                            for s in range(NSL):
                                pendC.append(lambda s=s, h=h: transpose_to(kcM[:, s, h * 128:(h + 1) * 128], r_kcM, kcT[:, h, s * 128:(s + 1) * 128], r_qk, 128, BF16, evac="act"))
            for (c0, kind) in ((C_VC, "v"), (C_OC, "o"), (C_ZC, "z")):
                for wi in range(4):
                    wv, r_w = w_next(("w_in", c0 + 256 * wi))
                    for s in range(NSL):
                        ps, r_p = psum()
                        mm_group(ps[:, 0:256], r_p, [(xT[:, kc, s * 128:(s + 1) * 128], wv[:, kc, :]) for kc in range(KC)], [r_w, r_xT])
                        for f_ in pendC:
                            f_()
                        pendC = []
                        if kind == "v":
                            sc.op("dve", lambda e: e.tensor_copy(out=vcm[:, s, :].rearrange("p (h e) -> p h e", h=8)[:, 2 * wi:2 * wi + 2, 0:128],
                                                                in_=ps[:, 0:256].rearrange("p (h e) -> p h e", h=2)), reads=[r_p], writes=[r_vcm])
                        elif kind == "o":
                            sc.op("act", lambda e: e.activation(out=ocm[:, s, wi * 256:(wi + 1) * 256], in_=ps[:, 0:256], func=AF.Sigmoid), reads=[r_p], writes=[r_ocm])
                        else:
                            sc.op("act", lambda e: e.activation(out=zcm[:, s, wi * 256:(wi + 1) * 256], in_=ps[:, 0:256], func=AF.Silu), reads=[r_p], writes=[r_zcm])
            MARKS.append(("C_chunks", t, l, PEI[0]))
            iF = cv.get([8, T])
            lf = cv.get([8, T])
            FF = cv.get([8, T])
            gg = cv.get([8, T])
            MM = cv.get([8, T])
            rF = cv.get([8, T])
            cF = cv.get([8, T])
            flF = cv.get([8, T])
            ones8T = cv.get([8, T])
            nMp = cv.get([8, 1])
            flT = cv.get([128, NSL, 8])
            r_g = Res()
            r_nMp = Res()
            r_flT = Res()
            wv, r_w = w_next(("w_in", C_I))
            ps, r_p = psum()
            mm_group(ps[0:8, 0:T], r_p, [(wv[:, kc, 0:8], xT[:, kc, :]) for kc in range(KC)], [r_w, r_xT])
            sc.op("act", lambda e: e.activation(out=iF, in_=ps[0:8, 0:T], func=AF.Identity, bias=mgb[l][:, 0:1], scale=1.0), reads=[r_p], writes=[r_g])
            ps, r_p = psum()
            mm_group(ps[0:8, 0:T], r_p, [(wv[:, kc, 8:16], xT[:, kc, :]) for kc in range(KC)], [r_w, r_xT])
            sc.op("act", lambda e: e.activation(out=lf, in_=ps[0:8, 0:T], func=AF.Exp, bias=nfb[l][:, 0:1], scale=-1.0), reads=[r_p], writes=[r_g])
            sc.op("act", lambda e: e.activation(out=lf, in_=lf, func=AF.Ln, bias=ones[0:8, 0:1], scale=1.0), reads=[r_g], writes=[r_g])
            sc.op("dve", lambda e: e.tensor_scalar(out=lf, in0=lf, scalar1=-1.0, scalar2=None, op0=ALU.mult), reads=[r_g], writes=[r_g])
            sc.op("dve", lambda e: e.memset(ones8T, 1.0), writes=[r_g])
            sc.op("dve", lambda e: e.tensor_tensor_scan(out=FF, data0=ones8T, data1=lf, initial=FM8[l][:, 0:1], op0=ALU.mult, op1=ALU.add), reads=[r_g, r_FM8[l]], writes=[r_g])
            sc.op("dve", lambda e: e.tensor_tensor(out=gg, in0=iF, in1=FF, op=ALU.subtract), reads=[r_g], writes=[r_g])
            sc.op("dve", lambda e: e.tensor_tensor_scan(out=MM, data0=gg, data1=gg, initial=FM8[l][:, 1:2], op0=ALU.max, op1=ALU.max), reads=[r_g, r_FM8[l]], writes=[r_g])
            sc.op("dve", lambda e: e.tensor_scalar(out=nMp, in0=FM8[l][:, 1:2], scalar1=-1.0, scalar2=None, op0=ALU.mult), reads=[r_FM8[l]], writes=[r_nMp])
            sc.op("dve", lambda e: e.tensor_copy(out=FM8[l][:, 0:1], in_=FF[:, T - 1:T]), reads=[r_g, r_nMp], writes=[r_FM8[l]])
            sc.op("dve", lambda e: e.tensor_copy(out=FM8[l][:, 1:2], in_=MM[:, T - 1:T]), reads=[r_g, r_nMp], writes=[r_FM8[l]])
            sc.op("dve", lambda e: e.tensor_scalar(out=rF, in0=MM, scalar1=-1.0, scalar2=None, op0=ALU.mult), reads=[r_g], writes=[r_g])
            sc.op("dve", lambda e: e.tensor_scalar(out=cF, in0=gg, scalar1=-1.0, scalar2=None, op0=ALU.mult), reads=[r_g], writes=[r_g])
            sc.op("dve", lambda e: e.tensor_tensor(out=flF, in0=FF, in1=MM, op=ALU.add), reads=[r_g], writes=[r_g])
            sc.op("act", lambda e: e.activation(out=flF, in_=flF, func=AF.Exp, scale=-1.0), reads=[r_g], writes=[r_g])
            for s in range(NSL):
                transpose_to(flT[:, s, :], r_flT, flF[:, s * 128:(s + 1) * 128], r_g, 8, F32)
            termsC = fm_terms(cv, 8, rF, cF, None, r_g, r_nMp, nMp, NSL)
            t1c = cv.get([128, 129])
            r_t1c = Res()
            dn = cv.get([128, 4])
            r_dn = Res()
            hc = cv.get([128, 128])
            r_hc = Res()
            cvmark = cv.off
            for c in range(NSL):
                cv.off = cvmark
                ch = slice(c * 128, (c + 1) * 128)
                scT, r_scT, decB, r_dec = termsC[c]

                def out_cbc(g, psy, r_py, c=c, scT=scT, r_scT=r_scT):
                    h = g
                    sc.op("dve", lambda e: e.tensor_scalar(out=t1c, in0=psy[:, 129:258], scalar1=scT[:, 2, h:h + 1], scalar2=None, op0=ALU.mult),
                          reads=[r_py, r_scT], writes=[r_t1c])
                    sc.op("dve", lambda e: e.tensor_tensor(out=t1c, in0=t1c, in1=psy[:, 0:129], op=ALU.add), reads=[r_py, r_t1c], writes=[r_t1c])
                    sc.op("dve", lambda e: e.tensor_scalar(out=dn[:, 3:4], in0=t1c[:, 128:129], scalar1=-1.0, scalar2=None, op0=ALU.mult), reads=[r_t1c], writes=[r_dn])
                    sc.op("dve", lambda e: e.tensor_tensor(out=dn[:, 0:1], in0=dn[:, 3:4], in1=t1c[:, 128:129], op=ALU.max), reads=[r_t1c, r_dn], writes=[r_dn])
                    sc.op("dve", lambda e: e.tensor_tensor(out=dn[:, 1:2], in0=dn[:, 0:1], in1=flT[:, c, h:h + 1], op=ALU.max), reads=[r_dn, r_flT], writes=[r_dn])
                    sc.op("dve", lambda e: e.reciprocal(out=dn[:, 2:3], in_=dn[:, 1:2]), reads=[r_dn], writes=[r_dn])
                    sc.op("dve", lambda e: e.scalar_tensor_tensor(out=hc, in0=t1c[:, 0:128], scalar=dn[:, 2:3], in1=ocm[:, c, h * 128:(h + 1) * 128], op0=ALU.mult, op1=ALU.mult),
                          reads=[r_t1c, r_dn, r_ocm], writes=[r_hc])
                    sc.op("dve", lambda e: e.tensor_tensor(out=ocb[:, h * 128:(h + 1) * 128], in0=hc, in1=zcm[:, c, h * 128:(h + 1) * 128], op=ALU.mult),
                          reads=[r_hc, r_zcm], writes=[r_ocb])

                dla_chunk(cv, 8, 1, 129,
                          lambda g: kcT[:, g, ch], lambda g: qcT[:, g, ch], r_qk,
                          lambda g: kcM[:, c, g * 128:(g + 1) * 128], vcm[:, c, :].rearrange("p (h e) -> p h e", h=8), r_vcm,
                          rF[:, ch], cF[:, ch], r_g, scT, r_scT,
                          Cst[l][:], Cbf[l][:], r_C[l], r_Cbf[l], decB, r_dec, out_cbc)
                for kc in range(8):
                    transpose_to(ocT[:, kc, ch], r_ocT, ocb[:, kc * 128:(kc + 1) * 128], r_ocb, 128, BF16, evac=("dve" if kc % 2 == 0 else "act"))
            dbg_dump("ocT", ocT[:], [128, 8, T], BF16, reads=[r_ocT])
            sc.barrier(engines=("pe", "act", "dve", "pool"))
            if stop == "C":
                break
            MARKS.append(("M", t, l, PEI[0]))
            cv = Carve()
            sg = [cv.get([128, 2 * T]) for _ in range(2)]
            macc = cv.get([128, 2 * T])
            mtmp = cv.get([128, 2 * T])
            r_sg = [Res(), Res()]
            r_macc, r_mtmp = Res(), Res()
            gi = 0
            for j in range(16):
                wb, r_wb = w_next(("w_branch", 256 * j))
                ybanks = []
                for (k0, k1, actT, r_a) in ((0, 16, oaT, r_oaT), (16, 24, obT, r_obT), (24, 32, ocT, r_ocT)):
                    py, r_py = psum()
                    for half in range(2):
                        mm_group(py[:, half * T:(half + 1) * T], r_py, [(wb[:, kc, half * 128:(half + 1) * 128], actT[:, kc - k0, :]) for kc in range(k0, k1)], [r_wb, r_a])
                    ybanks.append((py, r_py))
                for br in range(3):
                    wg, r_wg = w_next(("w_in", C_G + br * 4096 + 256 * j))
                    pg, r_pg = psum()
                    for half in range(2):
                        mm_group(pg[:, half * T:(half + 1) * T], r_pg, [(wg[:, kc, half * 128:(half + 1) * 128], xT[:, kc, :]) for kc in range(KC)], [r_wg, r_xT])
                    k = gi % 2
                    gi += 1
                    sc.op("act", lambda e: e.activation(out=sg[k], in_=pg[:, 0:2 * T], func=AF.Sigmoid), reads=[r_pg], writes=[r_sg[k]])
                    py, r_py = ybanks[br]
                    if br == 0:
                        sc.op("dve", lambda e: e.tensor_tensor(out=macc, in0=sg[k], in1=py[:, 0:2 * T], op=ALU.mult), reads=[r_sg[k], r_py], writes=[r_macc])
                    else:
                        sc.op("dve", lambda e: e.tensor_tensor(out=mtmp, in0=sg[k], in1=py[:, 0:2 * T], op=ALU.mult), reads=[r_sg[k], r_py], writes=[r_mtmp])
                        if br == 1:
                            sc.op("dve", lambda e: e.tensor_tensor(out=macc, in0=macc, in1=mtmp, op=ALU.add), reads=[r_macc, r_mtmp], writes=[r_macc])
                        else:
                            sc.op("dve", lambda e: e.tensor_tensor(out=mT[:, 2 * j:2 * j + 2, :], in0=macc.rearrange("p (a b) -> p a b", a=2),
                                                                  in1=mtmp.rearrange("p (a b) -> p a b", a=2), op=ALU.add), reads=[r_macc, r_mtmp], writes=[r_mT])
            dbg_dump("mT", mT[:], [128, KC, T], BF16, reads=[r_mT])
            sc.barrier(engines=("pe", "act", "dve", "pool"))
            if stop == "M":
                break
            MARKS.append(("O", t, l, PEI[0]))
            cv = Carve()
            xr = cv.get([128, NSL, D])
            r_xr = [Res() for _ in range(NSL)]
            gsl = [cv.get([128, 256]) for _ in range(2)]
            bsl = [cv.get([128, 256]) for _ in range(2)]
            r_gsl = [Res(), Res()]
            r_bsl = [Res(), Res()]
            wple = cv.get([128, 2, D], BF16)
            r_wple = Res()
            pT = cv.get([128, 2, T], BF16)
            pl = cv.get([128, 256])
            r_pT, r_pl = Res(), Res()
            stats = cv.get([128, 8, 6])
            mv = cv.get([128, 8])
            r_stats, r_mv = Res(), Res()
            ssqp = cv.get([128, NSL, 8])
            rse = cv.get([128, NSL, 4])
            r_ssqp, r_rse = Res(), Res()
            gt = [cv.get([128, 256]) for _ in range(2)]
            et = [cv.get([128, 256]) for _ in range(2)]
            pgs_ = cv.get([128, 256])
            pgs = [pgs_, pgs_]
            r_gt = [Res(), Res()]
            r_et = [Res(), Res()]
            r_pgs_ = Res()
            r_pgs = [r_pgs_, r_pgs_]
            xsrc = x_d if l == 0 else hb_d
            for s in range(NSL):
                sc.dma("pool", xr[:, s, :], xsrc[t0 + s * 128: t0 + (s + 1) * 128, :], reads=([r_hb] if l > 0 else []), writes=[r_xr[s]])
            pg_, po_ = WB_PAGE[WPLE_B]
            sc.dma("pool", wple[:].rearrange("p a b -> p (a b)"), wsc_view(l, WPLE_B, 2 * D), writes=[r_wple])
            for j in range(16):
                wo, r_wo = w_next(("w_out", 256 * j))
                for s in range(NSL):
                    ps, r_p = psum()
                    mm_group(ps[:, 0:256], r_p, [(mT[:, kc, s * 128:(s + 1) * 128], wo[:, kc, :]) for kc in range(KC)], [r_wo, r_mT])
                    sc.op("dve", lambda e: e.scalar_tensor_tensor(out=xr[:, s, j * 256:(j + 1) * 256], in0=xr[:, s, j * 256:(j + 1) * 256], scalar=ALPHA, in1=ps[:, 0:256],
                                                                  op0=ALU.mult, op1=ALU.add), reads=[r_p, r_xr[s]], writes=[r_xr[s]])
            MARKS.append(("LN", t, l, PEI[0]))
            gi = 0
            for s in range(NSL):
                for cb in range(8):
                    sc.op("dve", lambda e: e.bn_stats(out=stats[:, cb, :], in_=xr[:, s, cb * 512:(cb + 1) * 512]), reads=[r_xr[s]], writes=[r_stats])
                sc.op("dve", lambda e: e.bn_aggr(out=mv[:, 0:2], in_=stats[:].rearrange("p a b -> p (a b)")), reads=[r_stats], writes=[r_mv])
                sc.op("act", lambda e: e.activation(out=mv[:, 2:3], in_=mv[:, 1:2], func=AF.Sqrt, bias=epsc[:, 0:1], scale=1.0), reads=[r_mv], writes=[r_mv])
                sc.op("dve", lambda e: e.reciprocal(out=mv[:, 3:4], in_=mv[:, 2:3]), reads=[r_mv], writes=[r_mv])
                sc.op("dve", lambda e: e.tensor_scalar(out=xr[:, s, :], in0=xr[:, s, :], scalar1=mv[:, 0:1], scalar2=mv[:, 3:4], op0=ALU.subtract, op1=ALU.mult),
                      reads=[r_mv, r_xr[s]], writes=[r_xr[s]])
                for cb in range(16):
                    k = gi % 2
                    gi += 1
                    cols = slice(cb * 256, (cb + 1) * 256)
                    sc.dma("pool", gsl[k], lng_d[l][:, cols].partition_broadcast(128), writes=[r_gsl[k]])
                    sc.dma("pool", bsl[k], lnb_d[l][:, cols].partition_broadcast(128), writes=[r_bsl[k]])
                    sc.op("dve", lambda e: e.tensor_tensor(out=xr[:, s, cols], in0=xr[:, s, cols], in1=gsl[k], op=ALU.mult), reads=[r_gsl[k], r_xr[s]], writes=[r_xr[s]])
                    sc.op("dve", lambda e: e.tensor_tensor(out=xr[:, s, cols], in0=xr[:, s, cols], in1=bsl[k], op=ALU.add), reads=[r_bsl[k], r_xr[s]], writes=[r_xr[s]])
                for kc in range(KC):
                    transpose_to(xT[:, kc, s * 128:(s + 1) * 128], r_xT, xr[:, s, kc * 128:(kc + 1) * 128], r_xr[s], 128, F32, evac=("dve" if kc % 2 == 0 else "act"))
                sc.dma("pool", pl, p_d[l, t0 + s * 128: t0 + (s + 1) * 128, :], writes=[r_pl])
                for k2 in range(2):
                    transpose_to(pT[:, k2, s * 128:(s + 1) * 128], r_pT, pl[:, k2 * 128:(k2 + 1) * 128], r_pl, 128, F32)
                sc.op("dve", lambda e: e.memset(ssqp[:, s, :], 0.0), writes=[r_ssqp])
                for cb in range(8):
                    ps, r_p = psum()
                    mm_group(ps[:, 0:512], r_p, [(pT[:, k2, s * 128:(s + 1) * 128], wple[:, k2, cb * 512:(cb + 1) * 512]) for k2 in range(2)], [r_pT, r_wple])
                    sc.op("act", lambda e: e.activation(out=ps[:, 0:512], in_=ps[:, 0:512], func=AF.Square, accum_out=ssqp[:, s, cb:cb + 1]), reads=[r_p], writes=[r_p, r_ssqp])
                sc.op("dve", lambda e: e.reduce_sum(out=rse[:, s, 0:1], in_=ssqp[:, s, :], axis=mybir.AxisListType.X), reads=[r_ssqp], writes=[r_rse])
                sc.op("act", lambda e: e.activation(out=rse[:, s, 1:2], in_=rse[:, s, 0:1], func=AF.Sqrt, bias=epsc[:, 0:1], scale=1.0 / D), reads=[r_rse], writes=[r_rse])
                sc.op("dve", lambda e: e.reciprocal(out=rse[:, s, 2:3], in_=rse[:, s, 1:2]), reads=[r_rse], writes=[r_rse])
            MARKS.append(("G", t, l, PEI[0]))
            gi = 0
            for j in range(16):
                wg, r_wg = w_next(("w_ple_gate", 256 * j))
                cols = slice(j * 256, (j + 1) * 256)
                kk = j % 2
                sc.dma("pool", pgs[kk], pleg_d[l][:, cols].partition_broadcast(128), writes=[r_pgs[kk]])
                for s in range(NSL):
                    k = gi % 2
                    gi += 1
                    ps1, r_p1 = psum()
                    mm_group(ps1[:, 0:256], r_p1, [(xT[:, kc, s * 128:(s + 1) * 128], wg[:, kc, :]) for kc in range(KC)], [r_wg, r_xT])
                    ps2, r_p2 = psum()
                    mm_group(ps2[:, 0:256], r_p2, [(pT[:, k2, s * 128:(s + 1) * 128], wple[:, k2, cols]) for k2 in range(2)], [r_pT, r_wple])
                    sc.op("act", lambda e: e.activation(out=gt[k], in_=ps1[:, 0:256], func=AF.Sigmoid), reads=[r_p1], writes=[r_gt[k]])
                    sc.op("dve", lambda e: e.scalar_tensor_tensor(out=et[k], in0=ps2[:, 0:256], scalar=rse[:, s, 2:3], in1=pgs[kk], op0=ALU.mult, op1=ALU.mult),
                          reads=[r_p2, r_rse, r_pgs[kk]], writes=[r_et[k]])
                    sc.op("dve", lambda e: e.tensor_tensor(out=et[k], in0=et[k], in1=gt[k], op=ALU.mult), reads=[r_et[k], r_gt[k]], writes=[r_et[k]])
                    sc.op("dve", lambda e: e.tensor_tensor(out=xr[:, s, cols], in0=xr[:, s, cols], in1=et[k], op=ALU.add), reads=[r_et[k], r_xr[s]], writes=[r_xr[s]])
            for s in range(NSL):
                rows = slice(t0 + s * 128, t0 + (s + 1) * 128)
                if last:
                    sc.dma("pool", out_d[rows, :], xr[:, s, :], reads=[r_xr[s]])
                else:
                    sc.dma("pool", hb_d[rows, :], xr[:, s, :], reads=[r_xr[s]], writes=[r_hb])
                    for kc in range(KC):
                        transpose_to(xT[:, kc, s * 128:(s + 1) * 128], r_xT, xr[:, s, kc * 128:(kc + 1) * 128], r_xr[s], 128, F32, evac=("dve" if kc % 2 == 0 else "act"))
            sc.barrier(engines=("pe", "act", "dve", "pool"))
        if stop is not None:
            break
    sc.barrier()
    return nc, dbg_out


def make_consts():
    c = np.zeros((128, 128 * 3 + 8 * 32 + 8 * 128), np.float32)
    c[:, 0:128] = np.eye(128, dtype=np.float32)
    j = np.arange(128)[:, None]
    i = np.arange(128)[None, :]
    c[:, 128:256] = (j <= i).astype(np.float32)
    c[:, 256:384] = 1.0
    slopes = 2.0 ** (-(np.arange(8) + 1.0))
    pp = np.arange(128, dtype=np.float64)[:, None, None]
    rr = np.arange(32, dtype=np.float64)[None, None, :]
    tab = slopes[None, :, None] * (pp - 128.0 * rr - 64.0)
    c[:, 384:640] = tab.reshape(128, 256).astype(np.float32)
    ki = np.arange(128, dtype=np.float64)[:, None, None]
    qi = np.arange(128, dtype=np.float64)[None, None, :]
    dh = slopes[None, :, None] * (qi - 64.0 - np.abs(qi - ki))
    dh = np.where((ki >= 64) & (qi < 64), -30000.0, dh)
    c[:, 640:1664] = dh.reshape(128, 1024).astype(np.float32)
    return c


def prep_inputs(inp, b, NL, NTOK):
    f = np.float32
    m = {}
    m["x"] = np.ascontiguousarray(inp["x"][b, :NTOK]).astype(f, copy=False)
    m["p"] = np.ascontiguousarray(inp["p"][:NL, b, :NTOK]).astype(f, copy=False)
    for k in ("w_in", "w_branch", "w_out", "w_ple", "w_ple_gate"):
        m[k] = np.ascontiguousarray(inp[k][:NL])
    m["convw"] = np.ascontiguousarray(inp["conv_w"][:NL].reshape(NL, 4, 32, 128).transpose(0, 3, 2, 1))
    m["convb"] = np.ascontiguousarray(inp["conv_b"][:NL].reshape(NL, 32, 128).transpose(0, 2, 1))
    hp = np.zeros((NL, 32, 4), f)
    hp[:, :, 0] = inp["dt_bias"][:NL]
    hp[:, :, 1] = inp["a_log"][:NL]
    m["hp"] = hp
    m["dskip"] = np.ascontiguousarray(inp["d_skip"][:NL].reshape(NL, 1, 32))
    m["ssmg"] = np.ascontiguousarray(inp["ssm_norm_g"][:NL].reshape(NL, 16, 128).transpose(0, 2, 1))
    m["dlam"] = np.ascontiguousarray(inp["diff_lambda"][:NL].reshape(NL, 1, 256))
    m["dng"] = np.ascontiguousarray(inp["diff_norm_g"][:NL].reshape(NL, 1, 128))
    m["mgb"] = np.ascontiguousarray(inp["mlstm_gate_b"][:NL].transpose(0, 2, 1))
    m["lng"] = np.ascontiguousarray(inp["ln_g"][:NL].reshape(NL, 1, D))
    m["lnb"] = np.ascontiguousarray(inp["ln_b"][:NL].reshape(NL, 1, D))
    m["pleg"] = np.ascontiguousarray(inp["ple_norm_g"][:NL].reshape(NL, 1, D))
    m["cst"] = make_consts()
    return m


def kernel(**inputs):
    NL = 2
    nb = inputs["x"].shape[0]
    nc, _ = build(NL=NL, NTOK=SEQ)
    in_maps = [prep_inputs(inputs, b, NL, SEQ) for b in range(nb)]
    res = run_bass_kernel_spmd(nc, in_maps, core_ids=list(range(nb)))
    out = np.stack([np.asarray(r["out"], dtype=np.float32) for r in res.results], axis=0)
    return out
```

```python
import math
import numpy as np
import concourse.bass as bass
import concourse.mybir as mybir
from concourse.bass_utils import run_bass_kernel_spmd

F32 = mybir.dt.float32
BF16 = mybir.dt.bfloat16
AF = mybir.ActivationFunctionType
ALU = mybir.AluOpType

D = 4096
SEQ = 4096
T = 256
NSL = T // 128
KC = 32
WIN = 27696
C_XBC, C_ZA, C_DT, C_QB, C_KB, C_VB, C_ZB = 0, 4096, 6144, 6176, 7200, 8224, 9248
C_QC, C_KCC, C_VC, C_OC, C_ZC, C_I, C_F, C_G = 10272, 11296, 12320, 13344, 14368, 15392, 15400, 15408
ALPHA = (2.0 * 2) ** 0.25
ARENA_MAX = [0]
PEI = [0]
MARKS = []
EPS = 1e-5


class Res:
    __slots__ = ("w", "r")

    def __init__(self):
        self.w = None
        self.r = {}


class Sched:
    def __init__(self, nc, ndma=24):
        self.nc = nc
        self.eng = dict(pe=nc.tensor, act=nc.scalar, dve=nc.vector, pool=nc.gpsimd, sp=nc.sync)
        self.sems = {}
        self.cnt = {}
        for k in ("pe", "act", "dve", "pool"):
            self.sems[k] = nc.semaphore("s_" + k).__enter__()
            self.cnt[k] = 0
        self.dsem = [nc.semaphore("d%d" % i).__enter__() for i in range(ndma)]
        self.dcnt = [0] * ndma
        self.dnext = 0
        self.dnext_sp = 0
        self.nsp = 6
        self.seen = {e: {} for e in self.eng}
        self.nops = 0

    def _sem(self, k):
        return self.sems[k] if isinstance(k, str) else self.dsem[k]

    def _wait(self, e, need, keep_one=False):
        sn = self.seen[e]
        todo = [(k, v) for k, v in need.items() if sn.get(k, 0) < v]
        fused = None
        if keep_one and todo:
            k, v = todo.pop()
            fused = (self._sem(k), v)
            sn[k] = v
        for k, v in todo:
            self.eng[e].wait_ge(self._sem(k), v)
            sn[k] = v
        return fused

    def _collect(self, reads, writes):
        need = {}

        def add(t):
            if t is None:
                return
            k, v = t
            if need.get(k, 0) < v:
                need[k] = v

        for r in reads:
            add(r.w)
        for w in writes:
            add(w.w)
            for k, v in w.r.items():
                add((k, v))
        return need

    def _commit(self, tok, reads, writes):
        k, v = tok
        for r in reads:
            if r.r.get(k, 0) < v:
                r.r[k] = v
        for w in writes:
            w.w = tok
            w.r = {}

    def op(self, e, fn, reads=(), writes=()):
        need = self._collect(reads, writes)
        if e == "pe":
            need.pop("pe", None)
        fused = self._wait(e, need, keep_one=True)
        ins = fn(self.eng[e])
        first = ins
        if isinstance(ins, tuple):
            first, ins = ins
        if fused is not None:
            first._wait_ge(fused[0], fused[1])
        self.cnt[e] += 1
        ins.then_inc(self.sems[e], 1)
        tok = (e, self.cnt[e])
        self._commit(tok, reads, writes)
        self.nops += 1
        return tok

    def dma(self, e, out, in_, reads=(), writes=(), **kw):
        if e == "sp":
            i = self.dnext_sp
            self.dnext_sp = (self.dnext_sp + 1) % self.nsp
        else:
            i = self.nsp + self.dnext
            self.dnext = (self.dnext + 1) % (len(self.dsem) - self.nsp)
        need = self._collect(reads, writes)
        if self.dcnt[i]:
            if need.get(i, 0) < self.dcnt[i]:
                need[i] = self.dcnt[i]
        fused = self._wait(e, need, keep_one=True)
        ins = self.eng[e].dma_start(out=out, in_=in_, **kw)
        if fused is not None:
            ins._wait_ge(fused[0], fused[1])
        ins.then_inc(self.dsem[i], 16)
        self.dcnt[i] += 16
        tok = (i, self.dcnt[i])
        self._commit(tok, reads, writes)
        return tok

    def barrier(self, engines=("pe", "act", "dve", "pool", "sp"), skip_sems=()):
        need = {k: v for k, v in self.cnt.items() if v}
        for i, v in enumerate(self.dcnt):
            if v:
                need[i] = v
        for e in engines:
            self._wait(e, dict(need))


def _wblocks():
    b = []
    for i in range(16):
        b.append(("w_in", C_XBC + 256 * i, 256, 32))
    b.append(("w_in", C_DT, 32, 32))
    for i in range(8):
        b.append(("w_in", C_ZA + 256 * i, 256, 32))
    for c0 in (C_QB, C_KB, C_VB, C_ZB):
        for i in range(4):
            b.append(("w_in", c0 + 256 * i, 256, 32))
    for c0 in (C_QC, C_KCC, C_VC, C_OC, C_ZC):
        for i in range(4):
            b.append(("w_in", c0 + 256 * i, 256, 32))
    b.append(("w_in", C_I, 16, 32))
    for j in range(16):
        b.append(("w_branch", 256 * j, 256, 32))
        for br in range(3):
            b.append(("w_in", C_G + br * 4096 + 256 * j, 256, 32))
    for j in range(16):
        b.append(("w_out", 256 * j, 256, 32))
    b.append(("w_ple", 0, 4096, 2))
    for j in range(16):
        b.append(("w_ple_gate", 256 * j, 256, 32))
    return b


WBLOCKS = _wblocks()
WB_ELEMS = [128 * kc * n for (_, _, n, kc) in WBLOCKS]
WB_OFF = np.concatenate([[0], np.cumsum(WB_ELEMS)]).astype(np.int64)
WL_ELEMS = int(WB_OFF[-1])
PAGE_ELEMS = 48 * 1024 * 1024
WB_PAGE = []
_pg, _po = 0, 0
for _n in WB_ELEMS:
    if _po + _n > PAGE_ELEMS:
        _pg += 1
        _po = 0
    WB_PAGE.append((_pg, _po))
    _po += _n
NPAGES = _pg + 1


def build(NL=2, NTOK=SEQ, dbg=None, stop=None):
    NT = NTOK // T
    nc = bass.Bass("TRN2", target_bir_lowering=False)
    sc = Sched(nc)
    E = sc.eng
    dbg_out = {}

    def din(name, shape, dt=F32):
        return nc.dram_tensor(name, list(shape), dt, kind="ExternalInput").ap()

    x_d = din("x", [NTOK, D])
    p_d = din("p", [NL, NTOK, 256])
    wmat = dict(w_in=din("w_in", [NL, D, WIN]), w_branch=din("w_branch", [NL, D, D]), w_out=din("w_out", [NL, D, D]),
                w_ple=din("w_ple", [NL, 256, D]), w_ple_gate=din("w_ple_gate", [NL, D, D]))
    convw_d = din("convw", [NL, 128, 32, 4])
    convb_d = din("convb", [NL, 128, 32])
    hp_d = din("hp", [NL, 32, 4])
    dskip_d = din("dskip", [NL, 1, 32])
    ssmg_d = din("ssmg", [NL, 128, 16])
    dlam_d = din("dlam", [NL, 1, 256])
    dng_d = din("dng", [NL, 1, 128])
    mgb_d = din("mgb", [NL, 8, 2])
    lng_d = din("lng", [NL, 1, D])
    lnb_d = din("lnb", [NL, 1, D])
    pleg_d = din("pleg", [NL, 1, D])
    cst_d = din("cst", [128, 128 * 3 + 8 * 32 + 8 * 128])
    out_d = nc.dram_tensor("out", [NTOK, D], F32, kind="ExternalOutput").ap()
    wsc_pages = [[nc.dram_tensor("wsc%d_%d" % (l, g), [PAGE_ELEMS], BF16, kind="Internal").ap() for g in range(NPAGES)] for l in range(NL)]

    def wsc_view(l, b, ne):
        pg, po = WB_PAGE[b]
        return wsc_pages[l][pg][po: po + 128 * ne].rearrange("(p e) -> p e", p=128)
    kc_d = nc.dram_tensor("kcache", [NL, 8, 128, NTOK], BF16, kind="Internal").ap()
    vc_d = nc.dram_tensor("vcache", [NL, 8, NTOK, 129], BF16, kind="Internal").ap()
    hb_d = nc.dram_tensor("hbuf", [NTOK, D], F32, kind="Internal").ap()
    r_wsc, r_kc, r_vc, r_hb = Res(), Res(), Res(), Res()

    def dbg_dump(name, ap, shape, dt=F32, reads=()):
        if dbg is None or name not in dbg:
            return
        o = nc.dram_tensor("dbg_" + name, list(shape), dt, kind="ExternalOutput").ap()
        dbg_out[name] = sc.dma("pool", o, ap, reads=reads)

    def sb(name, shape, dt=F32):
        return nc.sbuf_tensor(name, list(shape), dt).__enter__()

    st_cm = [nc.sbuf_tensor("pst%d" % i, [128, 16384], F32) for i in range(2)]
    sb_cm = [nc.sbuf_tensor("psb%d" % i, [128, 16384], BF16) for i in range(2)]
    stg = [c.__enter__() for c in st_cm]
    sbg = [c.__enter__() for c in sb_cm]
    r_stg = [Res(), Res()]
    r_sbg = [Res(), Res()]
    bymat = {}
    singles = []
    for bidx, (mat, c0, n, kc) in enumerate(WBLOCKS):
        if n == 256 and kc == 32:
            bymat.setdefault(mat, []).append((c0, bidx))
        else:
            singles.append(bidx)
    groups = []
    for mat, lst in bymat.items():
        lst.sort()
        i = 0
        while i < len(lst):
            if i + 3 < len(lst) and all(lst[i + j][0] == lst[i][0] + 256 * j for j in range(4)):
                groups.append((mat, lst[i][0], [lst[i + j][1] for j in range(4)]))
                i += 4
            else:
                singles.append(lst[i][1])
                i += 1
    bi = 0

    def cast(s_, out_ap, in_ap):
        if bi % 2 == 0:
            sc.op("dve", lambda e: e.tensor_copy(out=out_ap, in_=in_ap), reads=[r_stg[s_]], writes=[r_sbg[s_]])
        else:
            sc.op("act", lambda e: e.copy(out=out_ap, in_=in_ap), reads=[r_stg[s_]], writes=[r_sbg[s_]])

    for l in range(NL):
        for (mat, c0, bl) in groups:
            src_all = wmat[mat][l][:, c0:c0 + 1024].rearrange("(k p) c -> p k c", p=128)
            for half in range(2):
                s_ = bi % 2
                k0 = half * 16
                stv = stg[s_][:].rearrange("p (k c) -> p k c", k=16)
                for kk in range(0, 16, 4):
                    sc.dma("sp", stv[:, kk:kk + 4, :], src_all[:, k0 + kk:k0 + kk + 4, :], writes=[r_stg[s_]])
                cast(s_, sbg[s_][:].rearrange("p (g k c) -> p g k c", g=4, k=16), stg[s_][:].rearrange("p (k g c) -> p g k c", k=16, g=4))
                for g_, bidx in enumerate(bl):
                    dst = wsc_view(l, bidx, 8192)[:, k0 * 256:(k0 + 16) * 256]
                    sc.dma("pool", dst, sbg[s_][:, g_ * 4096:(g_ + 1) * 4096], reads=[r_sbg[s_]], writes=[r_wsc])
                bi += 1
        for bidx in singles:
            mat, c0, n, kc = WBLOCKS[bidx]
            s_ = bi % 2
            src = wmat[mat][l][:, c0:c0 + n].rearrange("(k p) c -> p k c", p=128)
            stv = stg[s_][:, 0:kc * n].rearrange("p (k c) -> p k c", k=kc)
            step = 8 if kc >= 8 else kc
            if n > 1024:
                step = 1
            for k0 in range(0, kc, step):
                if n > 1024:
                    for c1 in range(0, n, 1024):
                        sc.dma("sp", stv[:, k0:k0 + step, c1:c1 + 1024], src[:, k0:k0 + step, c1:c1 + 1024], writes=[r_stg[s_]])
                else:
                    sc.dma("sp", stv[:, k0:k0 + step, :], src[:, k0:k0 + step, :], writes=[r_stg[s_]])
            ne = kc * n
            cast(s_, sbg[s_][:, 0:ne], stg[s_][:, 0:ne])
            sc.dma("pool", wsc_view(l, bidx, ne), sbg[s_][:, 0:ne], reads=[r_sbg[s_]], writes=[r_wsc])
            bi += 1
    sc.barrier()
    for c in reversed(sb_cm):
        c.__exit__(None, None, None)
    for c in reversed(st_cm):
        c.__exit__(None, None, None)

    cst = sb("cst_sb", [128, 128 * 3 + 8 * 32 + 8 * 128])
    ident = cst[:, 0:128]
    trimask = cst[:, 128:256]
    ones = cst[:, 256:384]
    tab = cst[:, 384:384 + 256].rearrange("p (h r) -> p h r", h=8)
    Dh = cst[:, 640:640 + 1024].rearrange("p (h q) -> p h q", h=8)
    identb = sb("identb", [128, 128], BF16)
    epsc = sb("epsc", [128, 1])
    zero32 = sb("zero32", [32, 1])
    xT = sb("xT", [128, KC, T], BF16)
    NWB = 3
    wbuf = [sb("wbuf%d" % i, [128, 8192], BF16) for i in range(NWB)]
    r_wbuf = [Res() for _ in range(NWB)]
    oaT = sb("oaT", [128, 16, T], BF16)
    obT = sb("obT", [128, 8, T], BF16)
    ocT = sb("ocT", [128, 8, T], BF16)
    mT = sb("mT", [128, KC, T], BF16)
    r_xT, r_oaT, r_obT, r_ocT, r_mT = Res(), Res(), Res(), Res(), Res()
    Sst = [sb("Sst%d" % l, [128, 2048]) for l in range(NL)]
    Sbf = [sb("Sbf%d" % l, [128, 2048], BF16) for l in range(NL)]
    tails = [sb("tails%d" % l, [128, 32, 3]) for l in range(NL)]
    Cst = [sb("Cst%d" % l, [128, 8, 129]) for l in range(NL)]
    Cbf = [sb("Cbf%d" % l, [128, 8, 129], BF16) for l in range(NL)]
    FM8 = [sb("fm8_%d" % l, [8, 2]) for l in range(NL)]
    r_S = [Res() for _ in range(NL)]
    r_Sbf = [Res() for _ in range(NL)]
    r_tails = [Res() for _ in range(NL)]
    r_C = [Res() for _ in range(NL)]
    r_Cbf = [Res() for _ in range(NL)]
    r_FM8 = [Res() for _ in range(NL)]
    convw = [sb("convw%d" % l, [128, 32, 4]) for l in range(NL)]
    convb = [sb("convb%d" % l, [128, 32]) for l in range(NL)]
    hp = [sb("hp%d" % l, [32, 4]) for l in range(NL)]
    nega = [sb("nega%d" % l, [32, 1]) for l in range(NL)]
    dskB = [sb("dskB%d" % l, [128, 32]) for l in range(NL)]
    ssmgF = [sb("ssmgF%d" % l, [128, 16]) for l in range(NL)]
    dngB = [sb("dngB%d" % l, [128, 128]) for l in range(NL)]
    lamt = [sb("lam%d" % l, [128, 4]) for l in range(NL)]
    dlt = sb("dlt", [128, 256])
    mgb = [sb("mgb%d" % l, [8, 2]) for l in range(NL)]
    nfb = [sb("nfb%d" % l, [8, 1]) for l in range(NL)]
    ARENA = 60 * 1024
    arena = sb("arena", [128, ARENA // 2], BF16)
    psums = [nc.psum_tensor("ps%d" % i, [128, 512], F32).__enter__() for i in range(8)]
    r_ps = [Res() for _ in range(8)]
    psn = [0]

    psrot = [8]

    def psum():
        i = psn[0] % psrot[0]
        psn[0] += 1
        return psums[i], r_ps[i]

    class Carve:
        def __init__(self):
            self.off = 0

        def get(self, shape, dt=F32):
            n = int(np.prod(shape[1:]))
            nb = n * (4 if dt == F32 else 2)
            nb_al = (nb + 63) // 64 * 64
            assert self.off + nb_al <= ARENA, ("arena overflow", self.off, nb_al)
            ARENA_MAX[0] = max(ARENA_MAX[0], self.off + nb_al)
            v = arena[0:shape[0], self.off // 2: self.off // 2 + nb // 2]
            self.off += nb_al
            if dt == F32:
                v = v.bitcast(F32)
            if len(shape) == 3:
                v = v.rearrange("p (a b) -> p a b", a=shape[1])
            return v

    r_c = Res()
    sc.dma("pool", cst[:], cst_d, writes=[r_c])
    sc.op("dve", lambda e: e.tensor_copy(out=identb[:], in_=ident), reads=[r_c], writes=[r_c])
    sc.op("dve", lambda e: e.memset(epsc[:], EPS), writes=[r_c])
    sc.op("dve", lambda e: e.memset(zero32[:], 0.0), writes=[r_c])
    for l in range(NL):
        sc.dma("pool", convw[l][:], convw_d[l], writes=[r_c])
        sc.dma("pool", convb[l][:], convb_d[l], writes=[r_c])
        sc.dma("pool", hp[l][:], hp_d[l], writes=[r_c])
        sc.dma("pool", dskB[l][:], dskip_d[l].partition_broadcast(128), writes=[r_c])
        sc.dma("pool", ssmgF[l][:], ssmg_d[l], writes=[r_c])
        sc.dma("pool", dngB[l][:], dng_d[l].partition_broadcast(128), writes=[r_c])
        sc.dma("pool", mgb[l][:], mgb_d[l], writes=[r_c])
        sc.dma("pool", dlt[:], dlam_d[l].partition_broadcast(128), writes=[r_c])
        sc.op("act", lambda e: e.activation(out=nega[l][:], in_=hp[l][:, 1:2], func=AF.Exp), reads=[r_c], writes=[r_c])
        sc.op("dve", lambda e: e.tensor_scalar(out=nega[l][:], in0=nega[l][:], scalar1=-1.0, scalar2=None, op0=ALU.mult), reads=[r_c], writes=[r_c])
        sc.op("dve", lambda e: e.tensor_scalar(out=nfb[l][:], in0=mgb[l][:, 1:2], scalar1=-1.0, scalar2=None, op0=ALU.mult), reads=[r_c], writes=[r_c])
        lam_init = 0.8 - 0.6 * math.exp(-0.3 * l)
        sc.op("dve", lambda e: e.tensor_tensor(out=dlt[:, 0:64], in0=dlt[:, 0:64], in1=dlt[:, 64:128], op=ALU.mult), reads=[r_c], writes=[r_c])
        sc.op("dve", lambda e: e.tensor_tensor(out=dlt[:, 128:192], in0=dlt[:, 128:192], in1=dlt[:, 192:256], op=ALU.mult), reads=[r_c], writes=[r_c])
        sc.op("dve", lambda e: e.reduce_sum(out=lamt[l][:, 2:3], in_=dlt[:, 0:64], axis=mybir.AxisListType.X), reads=[r_c], writes=[r_c])
        sc.op("dve", lambda e: e.reduce_sum(out=lamt[l][:, 3:4], in_=dlt[:, 128:192], axis=mybir.AxisListType.X), reads=[r_c], writes=[r_c])
        sc.op("act", lambda e: e.activation(out=lamt[l][:, 2:4], in_=lamt[l][:, 2:4], func=AF.Exp), reads=[r_c], writes=[r_c])
        sc.op("dve", lambda e: e.tensor_tensor(out=lamt[l][:, 0:1], in0=lamt[l][:, 2:3], in1=lamt[l][:, 3:4], op=ALU.subtract), reads=[r_c], writes=[r_c])
        sc.op("dve", lambda e: e.tensor_scalar(out=lamt[l][:, 0:1], in0=lamt[l][:, 0:1], scalar1=lam_init, scalar2=None, op0=ALU.add), reads=[r_c], writes=[r_c])
        sc.op("dve", lambda e: e.tensor_scalar(out=lamt[l][:, 1:2], in0=lamt[l][:, 0:1], scalar1=-1.0, scalar2=None, op0=ALU.mult), reads=[r_c], writes=[r_c])
        sc.op("dve", lambda e: e.memset(Sst[l][:], 0.0), writes=[r_S[l]])
        sc.op("dve", lambda e: e.memset(Sbf[l][:], 0.0), writes=[r_Sbf[l]])
        sc.op("dve", lambda e: e.memset(tails[l][:], 0.0), writes=[r_tails[l]])
        sc.op("dve", lambda e: e.memset(Cst[l][:], 0.0), writes=[r_C[l]])
        sc.op("dve", lambda e: e.memset(Cbf[l][:], 0.0), writes=[r_Cbf[l]])
        sc.op("dve", lambda e: e.memset(FM8[l][:], 0.0), writes=[r_FM8[l]])
    sc.barrier()

    WPLE_B = WBLOCKS.index(("w_ple", 0, 4096, 2))
    wplan = [(l, b) for _t in range(NT) for l in range(NL) for b in range(len(WBLOCKS)) if b != WPLE_B]
    wstate = dict(issued=0, used=0)

    def w_issue():
        i = wstate["issued"]
        if i >= len(wplan):
            return
        l, b = wplan[i]
        _, _, n, kc = WBLOCKS[b]
        ne = kc * n
        s = i % NWB
        src = wsc_view(l, b, ne)
        sc.dma("sp", wbuf[s][:, 0:ne], src, writes=[r_wbuf[s]])
        wstate["issued"] = i + 1

    def w_next(expect):
        i = wstate["used"]
        l, b = wplan[i]
        mat, c0, n, kc = WBLOCKS[b]
        assert (mat, c0) == expect, (WBLOCKS[b], expect)
        while wstate["issued"] <= min(i + NWB - 1, len(wplan) - 1):
            w_issue()
        wstate["used"] = i + 1
        s = i % NWB
        return wbuf[s][:, 0:kc * n].rearrange("p (k c) -> p k c", k=kc), r_wbuf[s]

    def mm_group(ps_ap, r_p, pairs, reads):
        n = len(pairs)
        PEI[0] += n

        def fn(e):
            ins = None
            first = None
            for i, (a, b) in enumerate(pairs):
                ins = e.matmul(ps_ap, lhsT=a, rhs=b, start=(i == 0), stop=(i == n - 1))
                if first is None:
                    first = ins
            return (first, ins)
        return sc.op("pe", fn, reads=reads, writes=[r_p])

    def transpose_to(dst_ap, r_dst, src_ap, r_src, kpart, dt, evac="dve", scale_ap=None):
        ps, r_p = psum()
        PEI[0] += 1
        nfree = src_ap.shape[-1]
        if dt == BF16:
            pv = ps[:].bitcast(BF16)[0:nfree, 0:kpart]
            idn = identb[0:kpart, 0:kpart]
        else:
            pv = ps[0:nfree, 0:kpart]
            idn = ident[0:kpart, 0:kpart]
        sc.op("pe", lambda e: e.transpose(out=pv, in_=src_ap, identity=idn), reads=[r_src], writes=[r_p])
        if scale_ap is not None:
            sc.op("dve", lambda e: e.tensor_scalar(out=dst_ap, in0=pv, scalar1=scale_ap, scalar2=None, op0=ALU.mult), reads=[r_p], writes=[r_dst])
        elif evac == "dve":
            sc.op("dve", lambda e: e.tensor_copy(out=dst_ap, in_=pv), reads=[r_p], writes=[r_dst])
        else:
            sc.op("act", lambda e: e.copy(out=dst_ap, in_=pv), reads=[r_p], writes=[r_dst])

    def dla_chunk(cv, nh, hpg, vw, bT, cT_, r_bc, b_tm, val_tm, r_val, rFM, cFM, r_fm, scT, r_scT, state, state_bf,
                  r_state, r_state_bf, decB, r_dec, out_cb):
        ng = nh // hpg
        gw = hpg * vw
        GTm = cv.get([128, 128])
        r_GTm = Res()
        xin = cv.get([128, hpg, vw], BF16)
        xcs = cv.get([128, hpg, vw], BF16)
        r_xin, r_xcs = Res(), Res()
        rh = [cv.get([nh, 128]) for _ in range(2)]
        argb = [cv.get([128, 128]) for _ in range(2)]
        PTb = [cv.get([128, 128], BF16) for _ in range(2)]
        r_rh = [Res(), Res()]
        r_arg = [Res(), Res()]
        r_PT = [Res(), Res()]
        t4 = cv.get([128, hpg, vw])
        r_t4 = Res()
        hi = 0
        for g in range(ng):
            hs = slice(g * hpg, (g + 1) * hpg)
            ps1, r_p1 = psum()
            mm_group(ps1[:, 0:128], r_p1, [(bT(g), cT_(g))], [r_bc])
            sc.op("dve", lambda e: e.tensor_tensor(out=GTm, in0=ps1[:, 0:128], in1=trimask, op=ALU.mult), reads=[r_p1], writes=[r_GTm])
            sc.op("dve", lambda e: e.tensor_tensor(out=xin, in0=val_tm[:, hs, :], in1=scT[:, 1, hs].unsqueeze(2).to_broadcast([128, hpg, vw]), op=ALU.mult),
                  reads=[r_val, r_scT], writes=[r_xin])
            sc.op("dve", lambda e: e.tensor_tensor(out=xcs, in0=val_tm[:, hs, :], in1=scT[:, 3, hs].unsqueeze(2).to_broadcast([128, hpg, vw]), op=ALU.mult),
                  reads=[r_val, r_scT], writes=[r_xcs])
            psy, r_py = psum()
            psyv = psy[:, 0:2 * gw]
            for hh in range(hpg):
                h = g * hpg + hh
                k = hi % 2
                hi += 1
                sc.op("dve", lambda e: e.tensor_scalar(out=rh[k], in0=rFM, scalar1=ident[0:nh, h:h + 1], scalar2=None, op0=ALU.mult),
                      reads=[r_fm], writes=[r_rh[k]])
                ps2, r_p2 = psum()
                mm_group(ps2[:, 0:128], r_p2, [(ones[0:nh, :], rh[k])], [r_rh[k]])
                sc.op("dve", lambda e: e.tensor_scalar(out=argb[k], in0=ps2[:, 0:128], scalar1=scT[:, 0, h:h + 1], scalar2=0.0, op0=ALU.subtract, op1=ALU.min),
                      reads=[r_p2, r_scT], writes=[r_arg[k]])
                sc.op("act", lambda e: e.activation(out=argb[k], in_=argb[k], func=AF.Exp), reads=[r_arg[k]], writes=[r_arg[k]])
                sc.op("dve", lambda e: e.tensor_tensor(out=PTb[k], in0=argb[k], in1=GTm, op=ALU.mult), reads=[r_arg[k], r_GTm], writes=[r_PT[k]])
                mm_group(psy[:, hh * vw:(hh + 1) * vw], r_py, [(PTb[k], xin[:, hh, :])], [r_PT[k], r_xin])
            mm_group(psy[:, gw:2 * gw], r_py, [(cT_(g), state_bf[:, hs, :])], [r_bc, r_state_bf])
            out_cb(g, psy, r_py)
            ps3, r_p3 = psum()
            mm_group(ps3[:, 0:gw], r_p3, [(b_tm(g), xcs[:])], [r_bc, r_xcs])
            sc.op("dve", lambda e: e.tensor_tensor(out=t4, in0=state[:, hs, :], in1=decB[:, hs].unsqueeze(2).to_broadcast([128, hpg, vw]), op=ALU.mult),
                  reads=[r_state, r_dec], writes=[r_t4])
            sc.op("dve", lambda e: e.tensor_tensor(out=state[:, hs, :], in0=t4, in1=ps3[:, 0:gw].rearrange("p (a b) -> p a b", a=hpg), op=ALU.add),
                  reads=[r_t4, r_p3], writes=[r_state])
            sc.op("act", lambda e: e.copy(out=state_bf[:, hs, :], in_=state[:, hs, :]), reads=[r_state], writes=[r_state_bf])

    def fm_terms(cv, nh, rFM_t, cFM_t, extra_t, r_fm, r_prev, rprev_ap, nchunks):
        outs = []
        nrp = cv.get([nh, 1])
        rs = cv.get([nh, 128])
        cs = cv.get([nh, 128])
        dec = cv.get([nh, 1])
        dg = cv.get([nh, nh])
        r_t = Res()
        for c in range(nchunks):
            ch = slice(c * 128, (c + 1) * 128)
            prev = rprev_ap if c == 0 else rFM_t[:, c * 128 - 1:c * 128]
            rend = rFM_t[:, (c + 1) * 128 - 1:(c + 1) * 128]
            scT = cv.get([128, 4, nh])
            decB = cv.get([128, nh])
            r_scT, r_dec = Res(), Res()
            sc.op("dve", lambda e: e.tensor_scalar(out=nrp, in0=prev, scalar1=-1.0, scalar2=None, op0=ALU.mult), reads=[r_fm, r_prev], writes=[r_t])
            sc.op("act", lambda e: e.activation(out=rs, in_=rFM_t[:, ch], func=AF.Exp, bias=nrp, scale=1.0), reads=[r_fm, r_t], writes=[r_t])
            sc.op("act", lambda e: e.activation(out=cs, in_=cFM_t[:, ch], func=AF.Exp, bias=rend, scale=-1.0), reads=[r_fm, r_t], writes=[r_t])
            if extra_t is not None:
                sc.op("dve", lambda e: e.tensor_tensor(out=cs, in0=cs, in1=extra_t[:, ch], op=ALU.mult), reads=[r_fm, r_t], writes=[r_t])
            sc.op("act", lambda e: e.activation(out=dec, in_=rend, func=AF.Exp, bias=nrp, scale=1.0), reads=[r_fm, r_t], writes=[r_t])
            sc.op("dve", lambda e: e.tensor_scalar(out=dg, in0=ident[0:nh, 0:nh], scalar1=dec, scalar2=None, op0=ALU.mult), reads=[r_t], writes=[r_t])
            ps, r_p = psum()
            mm_group(ps[:, 0:nh], r_p, [(ones[0:nh, :], dg)], [r_t])
            sc.op("dve", lambda e: e.tensor_copy(out=decB, in_=ps[:, 0:nh]), reads=[r_p], writes=[r_dec])
            ps, r_p = psum()
            srcs = [cFM_t[:, ch], (extra_t[:, ch] if extra_t is not None else None), rs, cs]
            PEI[0] += sum(1 for s_ in srcs if s_ is not None)

            def fn(e):
                ins = None
                first = None
                for i, s_ in enumerate(srcs):
                    if s_ is None:
                        continue
                    ins = e.transpose(out=ps[:, i * nh:(i + 1) * nh], in_=s_, identity=ident[0:nh, 0:nh])
                    if first is None:
                        first = ins
                return (first, ins)
            sc.op("pe", fn, reads=[r_fm, r_t], writes=[r_p])
            if extra_t is None:
                sc.op("dve", lambda e: e.memset(scT[:, 1, :], 1.0), writes=[r_scT])
                sc.op("dve", lambda e: e.tensor_copy(out=scT[:, 0, :], in_=ps[:, 0:nh]), reads=[r_p], writes=[r_scT])
                sc.op("dve", lambda e: e.tensor_copy(out=scT[:, 2:4, :], in_=ps[:, 2 * nh:4 * nh].rearrange("p (a b) -> p a b", a=2)), reads=[r_p], writes=[r_scT])
            else:
                sc.op("dve", lambda e: e.tensor_copy(out=scT, in_=ps[:, 0:4 * nh].rearrange("p (a b) -> p a b", a=4)), reads=[r_p], writes=[r_scT])
            outs.append((scT, r_scT, decB, r_dec))
        return outs

    for t in range(NT):
        t0 = t * T
        for l in range(NL):
            last = (l == NL - 1)
            lam_init = 0.8 - 0.6 * math.exp(-0.3 * l)
            if l == 0:
                cv = Carve()
                xs = [cv.get([128, D]) for _ in range(2)]
                r_xs = [Res(), Res()]
                for s in range(NSL):
                    k = s % 2
                    sc.dma("pool", xs[k], x_d[t0 + s * 128: t0 + (s + 1) * 128, :], writes=[r_xs[k]])
                    for kc in range(KC):
                        transpose_to(xT[:, kc, s * 128:(s + 1) * 128], r_xT, xs[k][:, kc * 128:(kc + 1) * 128], r_xs[k], 128, F32,
                                     evac=("dve" if kc % 2 == 0 else "act"))
                sc.barrier(engines=("pe", "act", "dve", "pool"))
            MARKS.append(("A", t, l, PEI[0]))
            cv = Carve()
            xh = cv.get([128, NSL, 2048], BF16)
            BF = cv.get([128, 8, T], BF16)
            CF = cv.get([128, 8, T], BF16)
            BT = cv.get([128, NSL, 1024], BF16)
            za = cv.get([128, NSL, 2048], BF16)
            ub = [cv.get([128, T + 3]) for _ in range(2)]
            acc = [cv.get([128, T]) for _ in range(2)]
            xc = [cv.get([128, T], BF16) for _ in range(2)]
            r_xh, r_BC, r_za = Res(), Res(), Res()
            r_ub = [Res(), Res()]
            r_acc = [Res(), Res()]
            r_xc = [Res(), Res()]
            blk = 0
            pendA = []
            for wi in range(16):
                wv, r_w = w_next(("w_in", C_XBC + 256 * wi))
                for half in range(2):
                    k = blk % 2
                    ps, r_p = psum()
                    mm_group(ps[:, 0:T], r_p, [(wv[:, kc, half * 128:(half + 1) * 128], xT[:, kc, :]) for kc in range(KC)], [r_w, r_xT])
                    for f_ in pendA:
                        f_()
                    pendA = []
                    sc.op("act", lambda e: e.copy(out=ub[k][:, 3:3 + T], in_=ps[:, 0:T]), reads=[r_p], writes=[r_ub[k]])
                    sc.op("dve", lambda e: e.tensor_copy(out=ub[k][:, 0:3], in_=tails[l][:, blk, :]), reads=[r_tails[l]], writes=[r_ub[k]])
                    sc.op("dve", lambda e: e.tensor_scalar(out=acc[k], in0=ub[k][:, 0:T], scalar1=convw[l][:, blk, 0:1], scalar2=None, op0=ALU.mult),
                          reads=[r_ub[k]], writes=[r_acc[k]])
                    for j in range(1, 4):
                        sc.op("dve", lambda e: e.scalar_tensor_tensor(out=acc[k], in0=ub[k][:, j:j + T], scalar=convw[l][:, blk, j:j + 1], in1=acc[k],
                                                                      op0=ALU.mult, op1=ALU.add), reads=[r_ub[k], r_acc[k]], writes=[r_acc[k]])
                    sc.op("dve", lambda e: e.tensor_copy(out=tails[l][:, blk, :], in_=ub[k][:, T:T + 3]), reads=[r_ub[k]], writes=[r_tails[l]])
                    if blk < 16:
                        dst, r_dst = xc[k], r_xc[k]
                    elif blk < 24:
                        dst, r_dst = BF[:, blk - 16, :], r_BC
                    else:
                        dst, r_dst = CF[:, blk - 24, :], r_BC
                    sc.op("act", lambda e: e.activation(out=dst, in_=acc[k], func=AF.Silu, bias=convb[l][:, blk:blk + 1], scale=1.0),
                          reads=[r_acc[k]], writes=[r_dst])
                    if blk < 16:
                        for s in range(NSL):
                            pendA.append(lambda s=s, blk=blk, k=k: transpose_to(xh[:, s, blk * 128:(blk + 1) * 128], r_xh, xc[k][:, s * 128:(s + 1) * 128], r_xc[k], 128, BF16,
                                                                                evac=("dve" if s % 2 == 0 else "act")))
                    elif blk < 24:
                        for s in range(NSL):
                            pendA.append(lambda s=s, blk=blk: transpose_to(BT[:, s, (blk - 16) * 128:(blk - 15) * 128], r_BC, BF[:, blk - 16, s * 128:(s + 1) * 128], r_BC, 128, BF16,
                                                                           evac=("dve" if s % 2 == 0 else "act")))
                    blk += 1
            for f_ in pendA:
                f_()
            pendA = []
            MARKS.append(("A_dt", t, l, PEI[0]))
            dtF = cv.get([32, T])
            laF = cv.get([32, T])
            AF_ = cv.get([32, T])
            ones32T = cv.get([32, T])
            r_dtf = Res()
            wv, r_w = w_next(("w_in", C_DT))
            ps, r_p = psum()
            mm_group(ps[0:32, 0:T], r_p, [(wv[:, kc, 0:32], xT[:, kc, :]) for kc in range(KC)], [r_w, r_xT])
            sc.op("act", lambda e: e.activation(out=dtF, in_=ps[0:32, 0:T], func=AF.Exp, bias=hp[l][:, 0:1], scale=1.0), reads=[r_p], writes=[r_dtf])
            sc.op("act", lambda e: e.activation(out=dtF, in_=dtF, func=AF.Ln, bias=ones[0:32, 0:1], scale=1.0), reads=[r_dtf], writes=[r_dtf])
            sc.op("dve", lambda e: e.tensor_scalar(out=laF, in0=dtF, scalar1=nega[l][:, 0:1], scalar2=None, op0=ALU.mult), reads=[r_dtf], writes=[r_dtf])
            sc.op("dve", lambda e: e.memset(ones32T, 1.0), writes=[r_dtf])
            sc.op("dve", lambda e: e.tensor_tensor_scan(out=AF_, data0=ones32T, data1=laF, initial=0.0, op0=ALU.mult, op1=ALU.add), reads=[r_dtf], writes=[r_dtf])
            r_z32 = Res()
            terms = fm_terms(cv, 32, AF_, AF_, dtF, r_dtf, r_z32, zero32[:], NSL)
            for wi in range(8):
                wv, r_w = w_next(("w_in", C_ZA + 256 * wi))
                for s in range(NSL):
                    ps, r_p = psum()
                    mm_group(ps[:, 0:256], r_p, [(xT[:, kc, s * 128:(s + 1) * 128], wv[:, kc, :]) for kc in range(KC)], [r_w, r_xT])
                    sc.op("act", lambda e: e.activation(out=za[:, s, wi * 256:(wi + 1) * 256], in_=ps[:, 0:256], func=AF.Silu), reads=[r_p], writes=[r_za])
            MARKS.append(("A_ssd", t, l, PEI[0]))
            ysb = cv.get([128, 2048])
            r_y = Res()
            t1 = cv.get([128, 4, 64])
            t3 = cv.get([128, 4, 64])
            r_t1, r_t3 = Res(), Res()
            oab = cv.get([128, 2048], BF16)
            r_oab = Res()
            st2 = cv.get([128, 4])
            r_st2 = Res()
            cvmark = cv.off
            for c in range(NSL):
                cv.off = cvmark
                ch = slice(c * 128, (c + 1) * 128)
                scT, r_scT, decB, r_dec = terms[c]
                xh_c = xh[:, c, :].rearrange("p (h v) -> p h v", h=32)

                def out_cb(g, psy, r_py, c=c, xh_c=xh_c, scT=scT, r_scT=r_scT):
                    hs = slice(4 * g, 4 * g + 4)
                    sc.op("dve", lambda e: e.tensor_tensor(out=t1, in0=psy[:, 256:512].rearrange("p (a b) -> p a b", a=4),
                                                          in1=scT[:, 2, hs].unsqueeze(2).to_broadcast([128, 4, 64]), op=ALU.mult),
                          reads=[r_py, r_scT], writes=[r_t1])
                    sc.op("dve", lambda e: e.tensor_tensor(out=t1, in0=t1, in1=psy[:, 0:256].rearrange("p (a b) -> p a b", a=4), op=ALU.add),
                          reads=[r_py, r_t1], writes=[r_t1])
                    sc.op("dve", lambda e: e.tensor_tensor(out=t3, in0=xh_c[:, hs, :], in1=dskB[l][:, hs].unsqueeze(2).to_broadcast([128, 4, 64]), op=ALU.mult),
                          reads=[r_xh], writes=[r_t3])
                    sc.op("dve", lambda e: e.tensor_tensor(out=ysb[:, g * 256:(g + 1) * 256].rearrange("p (a b) -> p a b", a=4), in0=t1, in1=t3, op=ALU.add),
                          reads=[r_t1, r_t3], writes=[r_y])

                dla_chunk(cv, 32, 4, 64,
                          lambda g: BF[:, g, ch], lambda g: CF[:, g, ch], r_BC,
                          lambda g: BT[:, c, g * 128:(g + 1) * 128], xh_c, r_xh,
                          AF_[:, ch], AF_[:, ch], r_dtf, scT, r_scT,
                          Sst[l][:].rearrange("p (h v) -> p h v", h=32), Sbf[l][:].rearrange("p (h v) -> p h v", h=32),
                          r_S[l], r_Sbf[l], decB, r_dec, out_cb)
                sc.op("dve", lambda e: e.tensor_tensor(out=ysb, in0=ysb, in1=za[:, c, :], op=ALU.mult), reads=[r_y, r_za], writes=[r_y])
                sc.op("dve", lambda e: e.memset(st2[:], 0.0), writes=[r_st2])
                sc.op("act", lambda e: e.activation(out=oab, in_=ysb, func=AF.Square, accum_out=st2[:, 0:1]), reads=[r_y], writes=[r_oab, r_st2])
                sc.op("act", lambda e: e.activation(out=st2[:, 1:2], in_=st2[:, 0:1], func=AF.Sqrt, bias=epsc[:, 0:1], scale=1.0 / 2048), reads=[r_st2], writes=[r_st2])
                sc.op("dve", lambda e: e.reciprocal(out=st2[:, 2:3], in_=st2[:, 1:2]), reads=[r_st2], writes=[r_st2])
                sc.op("dve", lambda e: e.tensor_scalar(out=oab, in0=ysb, scalar1=st2[:, 2:3], scalar2=None, op0=ALU.mult),
                      reads=[r_y, r_st2], writes=[r_oab])
                for kc in range(16):
                    transpose_to(oaT[:, kc, ch], r_oaT, oab[:, kc * 128:(kc + 1) * 128], r_oab, 128, BF16, scale_ap=ssmgF[l][:, kc:kc + 1])
            dbg_dump("oaT", oaT[:], [128, 16, T], BF16, reads=[r_oaT])
            sc.barrier(engines=("pe", "act", "dve", "pool"))
            if stop == "A":
                break

            MARKS.append(("B", t, l, PEI[0]))
            cv = Carve()
            qT = cv.get([128, 8, T], BF16)
            kTn = cv.get([128, 8, T], BF16)
            vn = cv.get([128, NSL, 8 * 129], BF16)
            zb = cv.get([128, NSL, 1024], BF16)
            obb = cv.get([128, NSL, 1024], BF16)
            r_q, r_kn, r_vn, r_zb, r_obb = Res(), Res(), Res(), Res(), Res()
            sc.op("dve", lambda e: e.memset(vn[:], 1.0), writes=[r_vn])
            for (c0, dstT, r_d) in ((C_QB, qT, r_q), (C_KB, kTn, r_kn)):
                for wi in range(4):
                    wv, r_w = w_next(("w_in", c0 + 256 * wi))
                    for half in range(2):
                        h = 2 * wi + half
                        ps, r_p = psum()
                        mm_group(ps[:, 0:T], r_p, [(wv[:, kc, half * 128:(half + 1) * 128], xT[:, kc, :]) for kc in range(KC)], [r_w, r_xT])
                        if half == 0:
                            sc.op("act", lambda e: e.copy(out=dstT[:, h, :], in_=ps[:, 0:T]), reads=[r_p], writes=[r_d])
                        else:
                            sc.op("dve", lambda e: e.tensor_copy(out=dstT[:, h, :], in_=ps[:, 0:T]), reads=[r_p], writes=[r_d])
            sc.dma("pool", kc_d[l][:, :, t0:t0 + T].rearrange("h p t -> p h t"), kTn[:], reads=[r_kn], writes=[r_kc])
            for wi in range(4):
                wv, r_w = w_next(("w_in", C_VB + 256 * wi))
                for s in range(NSL):
                    ps, r_p = psum()
                    mm_group(ps[:, 0:256], r_p, [(xT[:, kc, s * 128:(s + 1) * 128], wv[:, kc, :]) for kc in range(KC)], [r_w, r_xT])
                    sc.op("dve", lambda e: e.tensor_copy(out=vn[:, s, :].rearrange("p (h e) -> p h e", h=8)[:, 2 * wi:2 * wi + 2, 0:128],
                                                        in_=ps[:, 0:256].rearrange("p (h e) -> p h e", h=2)), reads=[r_p], writes=[r_vn])
            for s in range(NSL):
                sc.dma("pool", vc_d[l][:, t0 + s * 128: t0 + (s + 1) * 128, :].rearrange("h p e -> p h e"),
                       vn[:, s, :].rearrange("p (h e) -> p h e", h=8), reads=[r_vn], writes=[r_vc])
            for wi in range(4):
                wv, r_w = w_next(("w_in", C_ZB + 256 * wi))
                for s in range(NSL):
                    ps, r_p = psum()
                    mm_group(ps[:, 0:256], r_p, [(xT[:, kc, s * 128:(s + 1) * 128], wv[:, kc, :]) for kc in range(KC)], [r_w, r_xT])
                    sc.op("act", lambda e: e.activation(out=zb[:, s, wi * 256:(wi + 1) * 256], in_=ps[:, 0:256], func=AF.Silu), reads=[r_p], writes=[r_zb])
            MARKS.append(("B_att", t, l, PEI[0]))
            nkb = (t0 + T) // 128
            kbuf = [cv.get([128, NTOK], BF16) for _ in range(2)]
            vbuf = [cv.get([128, NTOK // 128, 129], BF16) for _ in range(2)]
            r_kb = [Res(), Res()]
            r_vb = [Res(), Res()]
            NPT = 6
            ptb = [cv.get([128, 128], BF16) for _ in range(NPT)]
            r_pt = [Res() for _ in range(NPT)]
            dtmp = [cv.get([128, 128]) for _ in range(2)]
            r_dtmp = [Res(), Res()]
            a0 = cv.get([128, 128])
            att = cv.get([128, 128])
            r_a0, r_att = Res(), Res()
            rec = cv.get([128, 8])
            r_rec = Res()
            psrot[0] = 6
            pti = 0
            oi = 0
            for h in range(8):
                k = h % 2
                sc.dma("pool", kbuf[k][:, 0:t0 + T], kc_d[l, h, :, 0:t0 + T], reads=[r_kc], writes=[r_kb[k]])
                sc.dma("pool", vbuf[k][:, 0:nkb, :], vc_d[l, h, 0:t0 + T, :].rearrange("(b p) e -> p b e", p=128), reads=[r_vc], writes=[r_vb[k]])
                for qb in range(NSL):
                    qabs = t0 // 128 + qb
                    psO, r_pO = psums[6 + oi % 2], r_ps[6 + oi % 2]
                    oi += 1
                    steps = [(c, kb) for c in range(2) for kb in range(qabs + 1)]
                    LOOK = 4

                    def emit_qk_exp(i, h=h, k=k, qb=qb, qabs=qabs):
                        c, kb = steps[i]
                        ps, r_p = psum()
                        mm_group(ps[:, 0:128], r_p, [(kbuf[k][c * 64:(c + 1) * 64, kb * 128:(kb + 1) * 128],
                                                      qT[c * 64:(c + 1) * 64, h, qb * 128:(qb + 1) * 128])], [r_kb[k], r_q])
                        pk = (pti0 + i) % NPT
                        if kb < qabs:
                            r_ = qabs - kb
                            sc.op("act", lambda e: e.activation(out=ptb[pk], in_=ps[:, 0:128], func=AF.Exp, bias=tab[:, h, r_:r_ + 1], scale=0.125),
                                  reads=[r_p], writes=[r_pt[pk]])
                        else:
                            dk = i % 2
                            sc.op("dve", lambda e: e.scalar_tensor_tensor(out=dtmp[dk], in0=ps[:, 0:128], scalar=0.125, in1=Dh[:, h, :], op0=ALU.mult, op1=ALU.add),
                                  reads=[r_p], writes=[r_dtmp[dk]])
                            sc.op("act", lambda e: e.activation(out=ptb[pk], in_=dtmp[dk], func=AF.Exp), reads=[r_dtmp[dk]], writes=[r_pt[pk]])

                    def emit_pv(i, k=k, qabs=qabs, psO=psO, r_pO=r_pO):
                        c, kb = steps[i]
                        Oc = psO[:, c * 256: c * 256 + 129]
                        pk = (pti0 + i) % NPT
                        PEI[0] += 1
                        sc.op("pe", lambda e: e.matmul(Oc, lhsT=ptb[pk], rhs=vbuf[k][:, kb, :], start=(kb == 0), stop=(kb == qabs)),
                              reads=[r_pt[pk], r_vb[k]], writes=[r_pO])

                    pti0 = pti
                    for i in range(len(steps) + LOOK):
                        if i < len(steps):
                            emit_qk_exp(i)
                        if i - LOOK >= 0:
                            emit_pv(i - LOOK)
                    pti += len(steps)
                    sc.op("dve", lambda e: e.reciprocal(out=rec[:, 0:1], in_=psO[:, 128:129]), reads=[r_pO], writes=[r_rec])
                    sc.op("dve", lambda e: e.reciprocal(out=rec[:, 1:2], in_=psO[:, 384:385]), reads=[r_pO], writes=[r_rec])
                    sc.op("dve", lambda e: e.tensor_tensor(out=rec[:, 2:3], in0=rec[:, 1:2], in1=lamt[l][:, 1:2], op=ALU.mult), reads=[r_rec], writes=[r_rec])
                    sc.op("dve", lambda e: e.tensor_scalar(out=a0, in0=psO[:, 0:128], scalar1=rec[:, 0:1], scalar2=None, op0=ALU.mult), reads=[r_pO, r_rec], writes=[r_a0])
                    sc.op("dve", lambda e: e.scalar_tensor_tensor(out=att, in0=psO[:, 256:384], scalar=rec[:, 2:3], in1=a0, op0=ALU.mult, op1=ALU.add),
                          reads=[r_pO, r_rec, r_a0], writes=[r_att])
                    sc.op("dve", lambda e: e.memset(rec[:, 3:4], 0.0), writes=[r_rec])
                    sc.op("act", lambda e: e.activation(out=a0, in_=att, func=AF.Square, accum_out=rec[:, 3:4]), reads=[r_att], writes=[r_a0, r_rec])
                    sc.op("act", lambda e: e.activation(out=rec[:, 4:5], in_=rec[:, 3:4], func=AF.Sqrt, bias=epsc[:, 0:1], scale=1.0 / 128), reads=[r_rec], writes=[r_rec])
                    sc.op("dve", lambda e: e.reciprocal(out=rec[:, 5:6], in_=rec[:, 4:5]), reads=[r_rec], writes=[r_rec])
                    sc.op("dve", lambda e: e.scalar_tensor_tensor(out=att, in0=att, scalar=rec[:, 5:6], in1=dngB[l][:], op0=ALU.mult, op1=ALU.mult),
                          reads=[r_att, r_rec], writes=[r_att])
                    sc.op("dve", lambda e: e.scalar_tensor_tensor(out=obb[:, qb, h * 128:(h + 1) * 128], in0=att, scalar=(1.0 - lam_init), in1=zb[:, qb, h * 128:(h + 1) * 128],
                                                                  op0=ALU.mult, op1=ALU.mult), reads=[r_att, r_zb], writes=[r_obb])
            psrot[0] = 8
            for s in range(NSL):
                for kc in range(8):
                    transpose_to(obT[:, kc, s * 128:(s + 1) * 128], r_obT, obb[:, s, kc * 128:(kc + 1) * 128], r_obb, 128, BF16, evac=("dve" if kc % 2 == 0 else "act"))
            dbg_dump("obT", obT[:], [128, 8, T], BF16, reads=[r_obT])
            sc.barrier(engines=("pe", "act", "dve", "pool"))
            if stop == "B":
                break
            MARKS.append(("C", t, l, PEI[0]))
            cv = Carve()
            qcT = cv.get([128, 8, T], BF16)
            kcT = cv.get([128, 8, T], BF16)
            kcM = cv.get([128, NSL, 1024], BF16)
            vcm = cv.get([128, NSL, 8 * 129], BF16)
            ocm = cv.get([128, NSL, 1024], BF16)
            zcm = cv.get([128, NSL, 1024], BF16)
            ocb = cv.get([128, 1024], BF16)
            r_qk, r_kcM, r_vcm, r_ocm, r_zcm, r_ocb = Res(), Res(), Res(), Res(), Res(), Res()
            sc.op("dve", lambda e: e.memset(vcm[:], 1.0), writes=[r_vcm])
            pendC = []
            for (c0, dstT, scl) in ((C_QC, qcT, 1.0), (C_KCC, kcT, 128.0 ** -0.5)):
                for wi in range(4):
                    wv, r_w = w_next(("w_in", c0 + 256 * wi))
                    for half in range(2):
                        h = 2 * wi + half
                        ps, r_p = psum()
                        mm_group(ps[:, 0:T], r_p, [(wv[:, kc, half * 128:(half + 1) * 128], xT[:, kc, :]) for kc in range(KC)], [r_w, r_xT])
                        for f_ in pendC:
                            f_()
                        pendC = []
                        sc.op("dve", lambda e: e.tensor_scalar(out=dstT[:, h, :], in0=ps[:, 0:T], scalar1=scl, scalar2=None, op0=ALU.mult), reads=[r_p], writes=[r_qk])
                        if c0 == C_KCC:
                            for s in range(NSL):
                                pendC.append(lambda s=s, h=h: transpose_to(kcM[:, s, h * 128:(h + 1) * 128], r_kcM, kcT[:, h, s * 128:(s + 1) * 128], r_qk, 128, BF16, evac="act"))
            for (c0, kind) in ((C_VC, "v"), (C_OC, "o"), (C_ZC, "z")):
                for wi in range(4):
                    wv, r_w = w_next(("w_in", c0 + 256 * wi))
                    for s in range(NSL):
                        ps, r_p = psum()
                        mm_group(ps[:, 0:256], r_p, [(xT[:, kc, s * 128:(s + 1) * 128], wv[:, kc, :]) for kc in range(KC)], [r_w, r_xT])
                        for f_ in pendC:
                            f_()
                        pendC = []
                        if kind == "v":
                            sc.op("dve", lambda e: e.tensor_copy(out=vcm[:, s, :].rearrange("p (h e) -> p h e", h=8)[:, 2 * wi:2 * wi + 2, 0:128],
                                                                in_=ps[:, 0:256].rearrange("p (h e) -> p h e", h=2)), reads=[r_p], writes=[r_vcm])
                        elif kind == "o":
                            sc.op("act", lambda e: e.activation(out=ocm[:, s, wi * 256:(wi + 1) * 256], in_=ps[:, 0:256], func=AF.Sigmoid), reads=[r_p], writes=[r_ocm])
                        else:
                            sc.op("act", lambda e: e.activation(out=zcm[:, s, wi * 256:(wi + 1) * 256], in_=ps[:, 0:256], func=AF.Silu), reads=[r_p], writes=[r_zcm])
            MARKS.append(("C_chunks", t, l, PEI[0]))
            iF = cv.get([8, T])
            lf = cv.get([8, T])
            FF = cv.get([8, T])
            gg = cv.get([8, T])
            MM = cv.get([8, T])
            rF = cv.get([8, T])
            cF = cv.get([8, T])
            flF = cv.get([8, T])
            ones8T = cv.get([8, T])
            nMp = cv.get([8, 1])
            flT = cv.get([128, NSL, 8])
            r_g = Res()
            r_nMp = Res()
            r_flT = Res()
            wv, r_w = w_next(("w_in", C_I))
            ps, r_p = psum()
            mm_group(ps[0:8, 0:T], r_p, [(wv[:, kc, 0:8], xT[:, kc, :]) for kc in range(KC)], [r_w, r_xT])
            sc.op("act", lambda e: e.activation(out=iF, in_=ps[0:8, 0:T], func=AF.Identity, bias=mgb[l][:, 0:1], scale=1.0), reads=[r_p], writes=[r_g])
            ps, r_p = psum()
            mm_group(ps[0:8, 0:T], r_p, [(wv[:, kc, 8:16], xT[:, kc, :]) for kc in range(KC)], [r_w, r_xT])
            sc.op("act", lambda e: e.activation(out=lf, in_=ps[0:8, 0:T], func=AF.Exp, bias=nfb[l][:, 0:1], scale=-1.0), reads=[r_p], writes=[r_g])
            sc.op("act", lambda e: e.activation(out=lf, in_=lf, func=AF.Ln, bias=ones[0:8, 0:1], scale=1.0), reads=[r_g], writes=[r_g])
            sc.op("dve", lambda e: e.tensor_scalar(out=lf, in0=lf, scalar1=-1.0, scalar2=None, op0=ALU.mult), reads=[r_g], writes=[r_g])
            sc.op("dve", lambda e: e.memset(ones8T, 1.0), writes=[r_g])
            sc.op("dve", lambda e: e.tensor_tensor_scan(out=FF, data0=ones8T, data1=lf, initial=FM8[l][:, 0:1], op0=ALU.mult, op1=ALU.add), reads=[r_g, r_FM8[l]], writes=[r_g])
            sc.op("dve", lambda e: e.tensor_tensor(out=gg, in0=iF, in1=FF, op=ALU.subtract), reads=[r_g], writes=[r_g])
            sc.op("dve", lambda e: e.tensor_tensor_scan(out=MM, data0=gg, data1=gg, initial=FM8[l][:, 1:2], op0=ALU.max, op1=ALU.max), reads=[r_g, r_FM8[l]], writes=[r_g])
            sc.op("dve", lambda e: e.tensor_scalar(out=nMp, in0=FM8[l][:, 1:2], scalar1=-1.0, scalar2=None, op0=ALU.mult), reads=[r_FM8[l]], writes=[r_nMp])
            sc.op("dve", lambda e: e.tensor_copy(out=FM8[l][:, 0:1], in_=FF[:, T - 1:T]), reads=[r_g, r_nMp], writes=[r_FM8[l]])
            sc.op("dve", lambda e: e.tensor_copy(out=FM8[l][:, 1:2], in_=MM[:, T - 1:T]), reads=[r_g, r_nMp], writes=[r_FM8[l]])
            sc.op("dve", lambda e: e.tensor_scalar(out=rF, in0=MM, scalar1=-1.0, scalar2=None, op0=ALU.mult), reads=[r_g], writes=[r_g])
            sc.op("dve", lambda e: e.tensor_scalar(out=cF, in0=gg, scalar1=-1.0, scalar2=None, op0=ALU.mult), reads=[r_g], writes=[r_g])
            sc.op("dve", lambda e: e.tensor_tensor(out=flF, in0=FF, in1=MM, op=ALU.add), reads=[r_g], writes=[r_g])
            sc.op("act", lambda e: e.activation(out=flF, in_=flF, func=AF.Exp, scale=-1.0), reads=[r_g], writes=[r_g])
            for s in range(NSL):
                transpose_to(flT[:, s, :], r_flT, flF[:, s * 128:(s + 1) * 128], r_g, 8, F32)
            termsC = fm_terms(cv, 8, rF, cF, None, r_g, r_nMp, nMp, NSL)
            t1c = cv.get([128, 129])
            r_t1c = Res()
            dn = cv.get([128, 4])
            r_dn = Res()
            hc = cv.get([128, 128])
            r_hc = Res()
            cvmark = cv.off
            for c in range(NSL):
                cv.off = cvmark
                ch = slice(c * 128, (c + 1) * 128)
                scT, r_scT, decB, r_dec = termsC[c]

                def out_cbc(g, psy, r_py, c=c, scT=scT, r_scT=r_scT):
                    h = g
                    sc.op("dve", lambda e: e.tensor_scalar(out=t1c, in0=psy[:, 129:258], scalar1=scT[:, 2, h:h + 1], scalar2=None, op0=ALU.mult),
                          reads=[r_py, r_scT], writes=[r_t1c])
                    sc.op("dve", lambda e: e.tensor_tensor(out=t1c, in0=t1c, in1=psy[:, 0:129], op=ALU.add), reads=[r_py, r_t1c], writes=[r_t1c])
                    sc.op("dve", lambda e: e.tensor_scalar(out=dn[:, 3:4], in0=t1c[:, 128:129], scalar1=-1.0, scalar2=None, op0=ALU.mult), reads=[r_t1c], writes=[r_dn])
                    sc.op("dve", lambda e: e.tensor_tensor(out=dn[:, 0:1], in0=dn[:, 3:4], in1=t1c[:, 128:129], op=ALU.max), reads=[r_t1c, r_dn], writes=[r_dn])
                    sc.op("dve", lambda e: e.tensor_tensor(out=dn[:, 1:2], in0=dn[:, 0:1], in1=flT[:, c, h:h + 1], op=ALU.max), reads=[r_dn, r_flT], writes=[r_dn])
                    sc.op("dve", lambda e: e.reciprocal(out=dn[:, 2:3], in_=dn[:, 1:2]), reads=[r_dn], writes=[r_dn])
                    sc.op("dve", lambda e: e.scalar_tensor_tensor(out=hc, in0=t1c[:, 0:128], scalar=dn[:, 2:3], in1=ocm[:, c, h * 128:(h + 1) * 128], op0=ALU.mult, op1=ALU.mult),
                          reads=[r_t1c, r_dn, r_ocm], writes=[r_hc])
                    sc.op("dve", lambda e: e.tensor_tensor(out=ocb[:, h * 128:(h + 1) * 128], in0=hc, in1=zcm[:, c, h * 128:(h + 1) * 128], op=ALU.mult),
                          reads=[r_hc, r_zcm], writes=[r_ocb])

                dla_chunk(cv, 8, 1, 129,
                          lambda g: kcT[:, g, ch], lambda g: qcT[:, g, ch], r_qk,
                          lambda g: kcM[:, c, g * 128:(g + 1) * 128], vcm[:, c, :].rearrange("p (h e) -> p h e", h=8), r_vcm,
                          rF[:, ch], cF[:, ch], r_g, scT, r_scT,
                          Cst[l][:], Cbf[l][:], r_C[l], r_Cbf[l], decB, r_dec, out_cbc)
                for kc in range(8):
                    transpose_to(ocT[:, kc, ch], r_ocT, ocb[:, kc * 128:(kc + 1) * 128], r_ocb, 128, BF16, evac=("dve" if kc % 2 == 0 else "act"))
            dbg_dump("ocT", ocT[:], [128, 8, T], BF16, reads=[r_ocT])
            sc.barrier(engines=("pe", "act", "dve", "pool"))
            if stop == "C":
                break
            MARKS.append(("M", t, l, PEI[0]))
            cv = Carve()
            sg = [cv.get([128, 2 * T]) for _ in range(2)]
            macc = cv.get([128, 2 * T])
            mtmp = cv.get([128, 2 * T])
            r_sg = [Res(), Res()]
            r_macc, r_mtmp = Res(), Res()
            gi = 0
            for j in range(16):
                wb, r_wb = w_next(("w_branch", 256 * j))
                ybanks = []
                for (k0, k1, actT, r_a) in ((0, 16, oaT, r_oaT), (16, 24, obT, r_obT), (24, 32, ocT, r_ocT)):
                    py, r_py = psum()
                    for half in range(2):
                        mm_group(py[:, half * T:(half + 1) * T], r_py, [(wb[:, kc, half * 128:(half + 1) * 128], actT[:, kc - k0, :]) for kc in range(k0, k1)], [r_wb, r_a])
                    ybanks.append((py, r_py))
                for br in range(3):
                    wg, r_wg = w_next(("w_in", C_G + br * 4096 + 256 * j))
                    pg, r_pg = psum()
                    for half in range(2):
                        mm_group(pg[:, half * T:(half + 1) * T], r_pg, [(wg[:, kc, half * 128:(half + 1) * 128], xT[:, kc, :]) for kc in range(KC)], [r_wg, r_xT])
                    k = gi % 2
                    gi += 1
                    sc.op("act", lambda e: e.activation(out=sg[k], in_=pg[:, 0:2 * T], func=AF.Sigmoid), reads=[r_pg], writes=[r_sg[k]])
                    py, r_py = ybanks[br]
                    if br == 0:
                        sc.op("dve", lambda e: e.tensor_tensor(out=macc, in0=sg[k], in1=py[:, 0:2 * T], op=ALU.mult), reads=[r_sg[k], r_py], writes=[r_macc])
                    else:
                        sc.op("dve", lambda e: e.tensor_tensor(out=mtmp, in0=sg[k], in1=py[:, 0:2 * T], op=ALU.mult), reads=[r_sg[k], r_py], writes=[r_mtmp])
                        if br == 1:
                            sc.op("dve", lambda e: e.tensor_tensor(out=macc, in0=macc, in1=mtmp, op=ALU.add), reads=[r_macc, r_mtmp], writes=[r_macc])
                        else:
                            sc.op("dve", lambda e: e.tensor_tensor(out=mT[:, 2 * j:2 * j + 2, :], in0=macc.rearrange("p (a b) -> p a b", a=2),
                                                                  in1=mtmp.rearrange("p (a b) -> p a b", a=2), op=ALU.add), reads=[r_macc, r_mtmp], writes=[r_mT])
            dbg_dump("mT", mT[:], [128, KC, T], BF16, reads=[r_mT])
            sc.barrier(engines=("pe", "act", "dve", "pool"))
            if stop == "M":
                break
            MARKS.append(("O", t, l, PEI[0]))
            cv = Carve()
            xr = cv.get([128, NSL, D])
            r_xr = [Res() for _ in range(NSL)]
            gsl = [cv.get([128, 256]) for _ in range(2)]
            bsl = [cv.get([128, 256]) for _ in range(2)]
            r_gsl = [Res(), Res()]
            r_bsl = [Res(), Res()]
            wple = cv.get([128, 2, D], BF16)
            r_wple = Res()
            pT = cv.get([128, 2, T], BF16)
            pl = cv.get([128, 256])
            r_pT, r_pl = Res(), Res()
            stats = cv.get([128, 8, 6])
            mv = cv.get([128, 8])
            r_stats, r_mv = Res(), Res()
            ssqp = cv.get([128, NSL, 8])
            rse = cv.get([128, NSL, 4])
            r_ssqp, r_rse = Res(), Res()
            gt = [cv.get([128, 256]) for _ in range(2)]
            et = [cv.get([128, 256]) for _ in range(2)]
            pgs_ = cv.get([128, 256])
            pgs = [pgs_, pgs_]
            r_gt = [Res(), Res()]
            r_et = [Res(), Res()]
            r_pgs_ = Res()
            r_pgs = [r_pgs_, r_pgs_]
            xsrc = x_d if l == 0 else hb_d
            for s in range(NSL):
                sc.dma("pool", xr[:, s, :], xsrc[t0 + s * 128: t0 + (s + 1) * 128, :], reads=([r_hb] if l > 0 else []), writes=[r_xr[s]])
            pg_, po_ = WB_PAGE[WPLE_B]
            sc.dma("pool", wple[:].rearrange("p a b -> p (a b)"), wsc_view(l, WPLE_B, 2 * D), writes=[r_wple])
            for j in range(16):
                wo, r_wo = w_next(("w_out", 256 * j))
                for s in range(NSL):
                    ps, r_p = psum()
                    mm_group(ps[:, 0:256], r_p, [(mT[:, kc, s * 128:(s + 1) * 128], wo[:, kc, :]) for kc in range(KC)], [r_wo, r_mT])
                    sc.op("dve", lambda e: e.scalar_tensor_tensor(out=xr[:, s, j * 256:(j + 1) * 256], in0=xr[:, s, j * 256:(j + 1) * 256], scalar=ALPHA, in1=ps[:, 0:256],
                                                                  op0=ALU.mult, op1=ALU.add), reads=[r_p, r_xr[s]], writes=[r_xr[s]])
            MARKS.append(("LN", t, l, PEI[0]))
            gi = 0
            for s in range(NSL):
                for cb in range(8):
                    sc.op("dve", lambda e: e.bn_stats(out=stats[:, cb, :], in_=xr[:, s, cb * 512:(cb + 1) * 512]), reads=[r_xr[s]], writes=[r_stats])
                sc.op("dve", lambda e: e.bn_aggr(out=mv[:, 0:2], in_=stats[:].rearrange("p a b -> p (a b)")), reads=[r_stats], writes=[r_mv])
                sc.op("act", lambda e: e.activation(out=mv[:, 2:3], in_=mv[:, 1:2], func=AF.Sqrt, bias=epsc[:, 0:1], scale=1.0), reads=[r_mv], writes=[r_mv])
                sc.op("dve", lambda e: e.reciprocal(out=mv[:, 3:4], in_=mv[:, 2:3]), reads=[r_mv], writes=[r_mv])
                sc.op("dve", lambda e: e.tensor_scalar(out=xr[:, s, :], in0=xr[:, s, :], scalar1=mv[:, 0:1], scalar2=mv[:, 3:4], op0=ALU.subtract, op1=ALU.mult),
                      reads=[r_mv, r_xr[s]], writes=[r_xr[s]])
                for cb in range(16):
                    k = gi % 2
                    gi += 1
                    cols = slice(cb * 256, (cb + 1) * 256)
                    sc.dma("pool", gsl[k], lng_d[l][:, cols].partition_broadcast(128), writes=[r_gsl[k]])
                    sc.dma("pool", bsl[k], lnb_d[l][:, cols].partition_broadcast(128), writes=[r_bsl[k]])
                    sc.op("dve", lambda e: e.tensor_tensor(out=xr[:, s, cols], in0=xr[:, s, cols], in1=gsl[k], op=ALU.mult), reads=[r_gsl[k], r_xr[s]], writes=[r_xr[s]])
                    sc.op("dve", lambda e: e.tensor_tensor(out=xr[:, s, cols], in0=xr[:, s, cols], in1=bsl[k], op=ALU.add), reads=[r_bsl[k], r_xr[s]], writes=[r_xr[s]])
                for kc in range(KC):
                    transpose_to(xT[:, kc, s * 128:(s + 1) * 128], r_xT, xr[:, s, kc * 128:(kc + 1) * 128], r_xr[s], 128, F32, evac=("dve" if kc % 2 == 0 else "act"))
                sc.dma("pool", pl, p_d[l, t0 + s * 128: t0 + (s + 1) * 128, :], writes=[r_pl])
                for k2 in range(2):
                    transpose_to(pT[:, k2, s * 128:(s + 1) * 128], r_pT, pl[:, k2 * 128:(k2 + 1) * 128], r_pl, 128, F32)
                sc.op("dve", lambda e: e.memset(ssqp[:, s, :], 0.0), writes=[r_ssqp])
                for cb in range(8):
                    ps, r_p = psum()
                    mm_group(ps[:, 0:512], r_p, [(pT[:, k2, s * 128:(s + 1) * 128], wple[:, k2, cb * 512:(cb + 1) * 512]) for k2 in range(2)], [r_pT, r_wple])
                    sc.op("act", lambda e: e.activation(out=ps[:, 0:512], in_=ps[:, 0:512], func=AF.Square, accum_out=ssqp[:, s, cb:cb + 1]), reads=[r_p], writes=[r_p, r_ssqp])
                sc.op("dve", lambda e: e.reduce_sum(out=rse[:, s, 0:1], in_=ssqp[:, s, :], axis=mybir.AxisListType.X), reads=[r_ssqp], writes=[r_rse])
                sc.op("act", lambda e: e.activation(out=rse[:, s, 1:2], in_=rse[:, s, 0:1], func=AF.Sqrt, bias=epsc[:, 0:1], scale=1.0 / D), reads=[r_rse], writes=[r_rse])
                sc.op("dve", lambda e: e.reciprocal(out=rse[:, s, 2:3], in_=rse[:, s, 1:2]), reads=[r_rse], writes=[r_rse])
            MARKS.append(("G", t, l, PEI[0]))
            gi = 0
            for j in range(16):
                wg, r_wg = w_next(("w_ple_gate", 256 * j))
                cols = slice(j * 256, (j + 1) * 256)
                kk = j % 2
                sc.dma("pool", pgs[kk], pleg_d[l][:, cols].partition_broadcast(128), writes=[r_pgs[kk]])
                for s in range(NSL):
                    k = gi % 2
                    gi += 1
                    ps1, r_p1 = psum()
                    mm_group(ps1[:, 0:256], r_p1, [(xT[:, kc, s * 128:(s + 1) * 128], wg[:, kc, :]) for kc in range(KC)], [r_wg, r_xT])
                    ps2, r_p2 = psum()
                    mm_group(ps2[:, 0:256], r_p2, [(pT[:, k2, s * 128:(s + 1) * 128], wple[:, k2, cols]) for k2 in range(2)], [r_pT, r_wple])
                    sc.op("act", lambda e: e.activation(out=gt[k], in_=ps1[:, 0:256], func=AF.Sigmoid), reads=[r_p1], writes=[r_gt[k]])
                    sc.op("dve", lambda e: e.scalar_tensor_tensor(out=et[k], in0=ps2[:, 0:256], scalar=rse[:, s, 2:3], in1=pgs[kk], op0=ALU.mult, op1=ALU.mult),
                          reads=[r_p2, r_rse, r_pgs[kk]], writes=[r_et[k]])
                    sc.op("dve", lambda e: e.tensor_tensor(out=et[k], in0=et[k], in1=gt[k], op=ALU.mult), reads=[r_et[k], r_gt[k]], writes=[r_et[k]])
                    sc.op("dve", lambda e: e.tensor_tensor(out=xr[:, s, cols], in0=xr[:, s, cols], in1=et[k], op=ALU.add), reads=[r_et[k], r_xr[s]], writes=[r_xr[s]])
            for s in range(NSL):
                rows = slice(t0 + s * 128, t0 + (s + 1) * 128)
                if last:
                    sc.dma("pool", out_d[rows, :], xr[:, s, :], reads=[r_xr[s]])
                else:
                    sc.dma("pool", hb_d[rows, :], xr[:, s, :], reads=[r_xr[s]], writes=[r_hb])
                    for kc in range(KC):
                        transpose_to(xT[:, kc, s * 128:(s + 1) * 128], r_xT, xr[:, s, kc * 128:(kc + 1) * 128], r_xr[s], 128, F32, evac=("dve" if kc % 2 == 0 else "act"))
            sc.barrier(engines=("pe", "act", "dve", "pool"))
        if stop is not None:
            break
    sc.barrier()
    return nc, dbg_out


def make_consts():
    c = np.zeros((128, 128 * 3 + 8 * 32 + 8 * 128), np.float32)
    c[:, 0:128] = np.eye(128, dtype=np.float32)
    j = np.arange(128)[:, None]
    i = np.arange(128)[None, :]
    c[:, 128:256] = (j <= i).astype(np.float32)
    c[:, 256:384] = 1.0
    slopes = 2.0 ** (-(np.arange(8) + 1.0))
    pp = np.arange(128, dtype=np.float64)[:, None, None]
    rr = np.arange(32, dtype=np.float64)[None, None, :]
    tab = slopes[None, :, None] * (pp - 128.0 * rr - 64.0)
    c[:, 384:640] = tab.reshape(128, 256).astype(np.float32)
    ki = np.arange(128, dtype=np.float64)[:, None, None]
    qi = np.arange(128, dtype=np.float64)[None, None, :]
    dh = slopes[None, :, None] * (qi - 64.0 - np.abs(qi - ki))
    dh = np.where((ki >= 64) & (qi < 64), -30000.0, dh)
    c[:, 640:1664] = dh.reshape(128, 1024).astype(np.float32)
    return c


def prep_inputs(inp, b, NL, NTOK):
    f = np.float32
    m = {}
    m["x"] = np.ascontiguousarray(inp["x"][b, :NTOK]).astype(f, copy=False)
    m["p"] = np.ascontiguousarray(inp["p"][:NL, b, :NTOK]).astype(f, copy=False)
    for k in ("w_in", "w_branch", "w_out", "w_ple", "w_ple_gate"):
        m[k] = np.ascontiguousarray(inp[k][:NL])
    m["convw"] = np.ascontiguousarray(inp["conv_w"][:NL].reshape(NL, 4, 32, 128).transpose(0, 3, 2, 1))
    m["convb"] = np.ascontiguousarray(inp["conv_b"][:NL].reshape(NL, 32, 128).transpose(0, 2, 1))
    hp = np.zeros((NL, 32, 4), f)
    hp[:, :, 0] = inp["dt_bias"][:NL]
    hp[:, :, 1] = inp["a_log"][:NL]
    m["hp"] = hp
    m["dskip"] = np.ascontiguousarray(inp["d_skip"][:NL].reshape(NL, 1, 32))
    m["ssmg"] = np.ascontiguousarray(inp["ssm_norm_g"][:NL].reshape(NL, 16, 128).transpose(0, 2, 1))
    m["dlam"] = np.ascontiguousarray(inp["diff_lambda"][:NL].reshape(NL, 1, 256))
    m["dng"] = np.ascontiguousarray(inp["diff_norm_g"][:NL].reshape(NL, 1, 128))
    m["mgb"] = np.ascontiguousarray(inp["mlstm_gate_b"][:NL].transpose(0, 2, 1))
    m["lng"] = np.ascontiguousarray(inp["ln_g"][:NL].reshape(NL, 1, D))
    m["lnb"] = np.ascontiguousarray(inp["ln_b"][:NL].reshape(NL, 1, D))
    m["pleg"] = np.ascontiguousarray(inp["ple_norm_g"][:NL].reshape(NL, 1, D))
    m["cst"] = make_consts()
    return m


def kernel(**inputs):
    NL = 2
    nb = inputs["x"].shape[0]
    nc, _ = build(NL=NL, NTOK=SEQ)
    in_maps = [prep_inputs(inputs, b, NL, SEQ) for b in range(nb)]
    res = run_bass_kernel_spmd(nc, in_maps, core_ids=list(range(nb)))
    out = np.stack([np.asarray(r["out"], dtype=np.float32) for r in res.results], axis=0)
    return out
```
